# Optimizing a Trainium2 kernel written in Bass

```python
import jax, jax.numpy as jnp
from jax import lax
import numpy as np

D_MODEL = 1024
BATCH = 8
SEQ = 2048
DEPTH = 2
DEC_BATCH = 128
DEC_SEQ = 4
PAST_LEN = 16384
PAGE_SIZE = 128

HEAD_A = 64
D_A = D_MODEL
N_HEADS_A = D_A // HEAD_A
D_DECAY_LORA = max(32, round(1.8 * D_MODEL ** 0.5 / 32) * 32)
D_AAA_LORA = max(32, round(1.8 * D_MODEL ** 0.5 / 32) * 32)
D_GATE_LORA = max(32, round(0.6 * D_MODEL ** 0.8 / 32) * 32)
GN_EPS = 64e-5
D_B = D_MODEL
CHUNK = 128
N_GROUPS_B = 8
GROUP_B = D_B // N_GROUPS_B
D_FF = 4 * D_MODEL
D_PLE = 256
N_IN = 3 * D_A + 2 * D_B + 2 * D_MODEL
EPS = 1e-6

kernel_name = "rwkv7_chunk_gmlp_hybrid_step"


def _rmsnorm(x, g):
    xf = x.astype(jnp.float32)
    y = xf * lax.rsqrt(jnp.mean(xf * xf, -1, keepdims=True) + EPS)
    return (y * g.astype(jnp.float32)).astype(x.dtype)


def _layernorm(x, g, b):
    xf = x.astype(jnp.float32)
    mu = jnp.mean(xf, -1, keepdims=True)
    var = jnp.mean(jnp.square(xf - mu), -1, keepdims=True)
    y = (xf - mu) * lax.rsqrt(var + EPS)
    return (y * g.astype(jnp.float32) + b.astype(jnp.float32)).astype(x.dtype)


def _head_norm(y, g, b):
    mu = jnp.mean(y, -1, keepdims=True)
    var = jnp.mean(jnp.square(y - mu), -1, keepdims=True)
    yn = (y - mu) * lax.rsqrt(var + GN_EPS)
    return yn * g.astype(jnp.float32).reshape(N_HEADS_A, HEAD_A) + b.astype(jnp.float32).reshape(N_HEADS_A, HEAD_A)


def _wkv_scan(r, w, k, v, kk, a, S0):
    def step(S, inp):
        r_t, w_t, k_t, v_t, kk_t, a_t = inp
        sa = jnp.einsum('bhij,bhj->bhi', S, -kk_t)
        S = (S * w_t[:, :, None, :] + sa[..., None] * (kk_t * a_t)[:, :, None, :]
             + v_t[..., None] * k_t[:, :, None, :])
        return S, jnp.einsum('bhij,bhj->bhi', S, r_t)
    xs = tuple(jnp.moveaxis(t, 1, 0) for t in (r, w, k, v, kk, a))
    S, y = lax.scan(step, S0.astype(jnp.float32), xs)
    return S, jnp.moveaxis(y, 0, 1)


def _chunk_mix(vn, w_s, b_s):
    B, L, _ = vn.shape
    n_chunks = -(-L // CHUNK)
    pad = n_chunks * CHUNK - L
    vp = jnp.pad(vn, ((0, 0), (0, pad), (0, 0))).reshape(B, n_chunks, CHUNK, N_GROUPS_B, GROUP_B)
    ws = w_s * jnp.tril(jnp.ones((CHUNK, CHUNK), w_s.dtype))
    s = jnp.einsum('gts,bnsgc->bntgc', ws, vp) + b_s.T[:, :, None]
    return s.reshape(B, n_chunks * CHUNK, D_B)[:, :L]


def _layer(h, p_l, wkv0, shift0, mix_pre_g, mix_post_g, ffn_pre_g, ffn_post_g,
           w_in, mu_rkv, mu_wag, w0, w1, w2, a0, a1, a2, g1, g2, k_k, k_a, r_k,
           lnx_g, lnx_b, vn_g, vn_b, w_s, b_s, w_out, w_up, w_down, w_pe, w_pg):
    f32 = jnp.float32
    B, L, _ = h.shape
    xn = _rmsnorm(h, mix_pre_g)
    sh0 = shift0.astype(xn.dtype)
    xs = jnp.concatenate([sh0[:, None, :], xn[:, :-1]], axis=1)
    proj = xn @ w_in
    rkv = proj[..., :3 * D_A]
    rkv_first = sh0 @ w_in[:, :3 * D_A]
    rkv_prev = jnp.concatenate([rkv_first[:, None], rkv[:, :-1]], axis=1)
    rkv = rkv + (rkv_prev - rkv) * mu_rkv
    r, k, v = jnp.split(rkv, 3, axis=-1)
    gu, gv, ga, gb = jnp.split(proj[..., 3 * D_A:], [D_B, 2 * D_B, 2 * D_B + D_MODEL], axis=-1)
    dx = xs - xn
    xw = xn + dx * mu_wag[0]
    xa = xn + dx * mu_wag[1]
    xg = xn + dx * mu_wag[2]
    w_log = -jax.nn.softplus(-(w0 + jnp.tanh(xw @ w1) @ w2).astype(f32)) - 0.5
    decay = jnp.exp(-jnp.exp(w_log))
    a = jax.nn.sigmoid((a0 + (xa @ a1) @ a2).astype(f32))
    g = jax.nn.sigmoid(xg @ g1) @ g2
    hd = lambda t: t.astype(f32).reshape(B, L, N_HEADS_A, HEAD_A)
    ph = lambda t: t.astype(f32).reshape(N_HEADS_A, HEAD_A)
    rf, kf, vf, a_h = hd(r), hd(k), hd(v), hd(a)
    kk = kf * ph(k_k)
    kk = kk / jnp.maximum(jnp.sqrt(jnp.sum(kk * kk, -1, keepdims=True)), 1e-12)
    kf = kf * (1.0 + (a_h - 1.0) * ph(k_a))
    S, y = _wkv_scan(rf, hd(decay), kf, vf, kk, a_h, wkv0)
    y = _head_norm(y, lnx_g, lnx_b)
    y = y + jnp.sum(rf * kf * r_k.astype(f32), -1, keepdims=True) * vf
    o_a = y.reshape(B, L, D_A).astype(h.dtype) * g
    u = jax.nn.gelu(gu, approximate=False)
    vn = _layernorm(jax.nn.gelu(gv, approximate=False), vn_g, vn_b)
    o_b = u * _chunk_mix(vn, w_s, b_s)
    mixed = jax.nn.sigmoid(ga) * o_a + jax.nn.sigmoid(gb) * o_b
    h = h + _rmsnorm(mixed @ w_out, mix_post_g)
    z = _rmsnorm(h, ffn_pre_g) @ w_up
    h = h + _rmsnorm(jnp.square(jax.nn.relu(z)) @ w_down, ffn_post_g)
    h = h + jax.nn.sigmoid(h @ w_pg) * (p_l @ w_pe)
    return h, S, xn[:, -1], vn


def setup_inputs(seed: int = 0) -> dict:
    key = jax.random.key(seed)
    ks = iter(jax.random.split(key, 40))
    nrm = lambda shape, s: s * jax.random.normal(next(ks), shape, jnp.float32)
    uni = lambda shape, lo, hi: jax.random.uniform(next(ks), shape, jnp.float32, lo, hi)
    L, D = DEPTH, D_MODEL
    return {
        "x_prompt": nrm((BATCH, SEQ, D), 1.0),
        "x_sample": nrm((DEC_BATCH, DEC_SEQ, D), 1.0),
        "state_wkv": nrm((L, DEC_BATCH, N_HEADS_A, HEAD_A, HEAD_A), 0.5),
        "state_shift": nrm((L, DEC_BATCH, D), 1.0),
        "p_prompt": nrm((L, BATCH, SEQ, D_PLE), 1.0),
        "p_sample": nrm((L, DEC_BATCH, DEC_SEQ, D_PLE), 1.0),
        "mix_pre_g": 1.0 + nrm((L, D), 0.05),
        "mix_post_g": 1.0 + nrm((L, D), 0.05),
        "ffn_pre_g": 1.0 + nrm((L, D), 0.05),
        "ffn_post_g": 1.0 + nrm((L, D), 0.05),
        "w_in": nrm((L, D, N_IN), D ** -0.5),
        "mu_rkv": uni((L, 3 * D_A), 0.0, 1.0),
        "mu_wag": uni((L, 3, D), 0.0, 1.0),
        "w0": uni((L, D_A), -3.0, 1.0),
        "w1": nrm((L, D, D_DECAY_LORA), D ** -0.5),
        "w2": nrm((L, D_DECAY_LORA, D_A), 0.1 * D_DECAY_LORA ** -0.5),
        "a0": nrm((L, D_A), 0.5),
        "a1": nrm((L, D, D_AAA_LORA), D ** -0.5),
        "a2": nrm((L, D_AAA_LORA, D_A), 0.3 * D_AAA_LORA ** -0.5),
        "g1": nrm((L, D, D_GATE_LORA), D ** -0.5),
        "g2": nrm((L, D_GATE_LORA, D_A), D_GATE_LORA ** -0.5),
        "k_k": 0.85 + nrm((L, D_A), 0.05),
        "k_a": 1.0 + nrm((L, D_A), 0.05),
        "r_k": nrm((L, N_HEADS_A, HEAD_A), 0.1),
        "lnx_g": 1.0 + nrm((L, D_A), 0.05),
        "lnx_b": nrm((L, D_A), 0.02),
        "vn_g": 1.0 + nrm((L, D_B), 0.05),
        "vn_b": nrm((L, D_B), 0.02),
        "w_s": nrm((L, N_GROUPS_B, CHUNK, CHUNK), 0.5 * CHUNK ** -0.5),
        "b_s": 1.0 + nrm((L, N_GROUPS_B, CHUNK), 0.1),
        "w_out": nrm((L, D, D), D ** -0.5),
        "w_up": nrm((L, D, D_FF), D ** -0.5),
        "w_down": nrm((L, D_FF, D), D_FF ** -0.5),
        "w_pe": nrm((L, D_PLE, D), D_PLE ** -0.5),
        "w_pg": nrm((L, D, D), D ** -0.5),
    }


def reference(x_prompt, x_sample, state_wkv, state_shift, p_prompt, p_sample,
              mix_pre_g, mix_post_g, ffn_pre_g, ffn_post_g, w_in, mu_rkv, mu_wag,
              w0, w1, w2, a0, a1, a2, g1, g2, k_k, k_a, r_k, lnx_g, lnx_b,
              vn_g, vn_b, w_s, b_s, w_out, w_up, w_down, w_pe, w_pg):
    bp = x_prompt.shape[0]
    yp, ys = x_prompt, x_sample
    wkv_p, sh_p, wkv_s, sh_s, cv_s = [], [], [], [], []
    for i in range(DEPTH):
        lw = (mix_pre_g[i], mix_post_g[i], ffn_pre_g[i], ffn_post_g[i], w_in[i], mu_rkv[i],
              mu_wag[i], w0[i], w1[i], w2[i], a0[i], a1[i], a2[i], g1[i], g2[i], k_k[i],
              k_a[i], r_k[i], lnx_g[i], lnx_b[i], vn_g[i], vn_b[i], w_s[i], b_s[i],
              w_out[i], w_up[i], w_down[i], w_pe[i], w_pg[i])
        yp, S_p, last_p, _ = _layer(
            yp, p_prompt[i], jnp.zeros((bp, N_HEADS_A, HEAD_A, HEAD_A), jnp.float32),
            jnp.zeros((bp, D_MODEL), x_prompt.dtype), *lw)
        ys, S_s, last_s, vn_s = _layer(ys, p_sample[i], state_wkv[i], state_shift[i], *lw)
        wkv_p.append(S_p.astype(state_wkv.dtype))
        sh_p.append(last_p.astype(state_shift.dtype))
        wkv_s.append(S_s.astype(state_wkv.dtype))
        sh_s.append(last_s.astype(state_shift.dtype))
        cv_s.append(vn_s)
    new_wkv_prompt = jnp.stack(wkv_p)
    new_shift_prompt = jnp.stack(sh_p)
    new_wkv_sample = jnp.stack(wkv_s)
    new_shift_sample = jnp.stack(sh_s)
    new_chunk_v_sample = jnp.stack(cv_s)
    return (yp, ys, new_wkv_prompt, new_shift_prompt, new_wkv_sample, new_shift_sample, new_chunk_v_sample)
```

```python
import numpy as np
from contextlib import ExitStack
import concourse.bass as bass
import concourse.mybir as mybir
from concourse.bass_utils import run_bass_kernel_spmd

F32, BF16 = mybir.dt.float32, mybir.dt.bfloat16
AF = mybir.ActivationFunctionType
ALU = mybir.AluOpType
MUL, ADD, SUB, MAX = ALU.mult, ALU.add, ALU.subtract, ALU.max

D = 1024; NCORE = 8; SEQ = 2048; NSQ = 16; DEC = 4; NTOK = SEQ + NSQ * DEC
W = 516
LAM = 0.6065306597126334
EPS = 1e-6; GN_EPS = 64e-5
NBLK = 38; SLOTW = 4608
G_PRE, G_POST, F_PRE, F_POST, MU_R, MU_K, MU_V, MU_W, MU_A, MU_G = 0, 8, 16, 24, 32, 40, 48, 56, 64, 72
W0, A0, KK, KA, RK, LNG, LNB, VNG, VNB, OMM_R, OMM_K, OMM_V, OMKA, NPC = 80, 88, 96, 104, 112, 120, 128, 136, 144, 152, 160, 168, 176, 184
C_ID, C_ONES, C_BONES, C_BMEAN, C_M4, C_ML4, C_ID4, C_RSP, C_RSS, C_SH, CW = 0, 128, 256, 384, 512, 1024, 1536, 2048, 2560, 2624, 2688


class V:
    __slots__ = ("ap", "keys")

    def __init__(s, ap, keys):
        s.ap = ap; s.keys = tuple(keys)

    def __getitem__(s, i):
        return V(s.ap[i], s.keys)

    def re(s, pat, **kw):
        return V(s.ap.rearrange(pat, **kw), s.keys)

    def bc(s, shape):
        return V(s.ap.to_broadcast(list(shape)), s.keys)

    def cast(s, dt):
        return V(s.ap.bitcast(dt), s.keys)


class Op:
    __slots__ = ("eng", "fn", "idx", "sig", "waits", "dma", "dsem", "dval", "cnt")


ENGS = ("pe", "act", "dve", "pool", "sp")


class Sched:
    def __init__(s, ndma=24):
        s.ops = {e: [] for e in ENGS}
        s.last_w = {}; s.readers = {}
        s.known = {e: {} for e in ENGS}
        s.ndma = ndma; s.dma_i = 0; s.dma_last = [None] * ndma
        import os
        s.limit = int(float(os.environ.get("KLIMIT", "1e12"))); s.marks = []

    def add(s, eng, fn, r, w, dma=False, cost=256):
        s.total = getattr(s, "total", 0) + 1
        if s.total > s.limit: return None
        op = Op(); op.eng = eng; op.fn = fn; op.idx = len(s.ops[eng]); op.sig = False
        op.waits = []; op.dma = dma; op.cnt = 0; op.dsem = -1; op.dval = 0
        deps = []
        for k in r:
            d = s.last_w.get(k)
            if d is not None: deps.append(d)
        for k in w:
            d = s.last_w.get(k)
            if d is not None: deps.append(d)
            deps.extend(s.readers.get(k, ()))
        if dma:
            ph = s.__dict__.setdefault("hist_" + eng, [])
            acc = cost
            lim = 800 if eng == "pool" else 400
            for (pop, pc) in reversed(ph[-16:]):
                acc += pc
                if acc > lim:
                    deps.append(pop); break
        if dma:
            cnts = s.__dict__.setdefault("dma_cnt", {"pool": 0, "sp": 0})
            base, n = (0, 8) if eng == "pool" else (8, s.ndma - 8)
            i = cnts[eng]; cnts[eng] += 1
            slot = base + i % n; op.dsem = slot; op.dval = 16 * (i // n + 1)
            if s.dma_last[slot] is not None: deps.append(s.dma_last[slot])
            s.dma_last[slot] = op
        kn = s.known[eng]
        best = {}
        for d in deps:
            key, val = (("d", d.dsem), d.dval) if d.dma else (d.eng, d.idx)
            if key not in best or val > best[key][0]: best[key] = (val, d)
        deps = [v[1] for v in best.values()]
        for d in deps:
            if d.dma:
                key = ("d", d.dsem)
                if kn.get(key, 0) >= d.dval: continue
                kn[key] = d.dval; op.waits.append(d)
            else:
                if d.eng == "pe" and eng == "pe": continue
                if kn.get(d.eng, -1) >= d.idx: continue
                kn[d.eng] = d.idx; d.sig = True; op.waits.append(d)
        for k in r: s.readers.setdefault(k, []).append(op)
        for k in w:
            s.last_w[k] = op; s.readers[k] = []
        s.ops[eng].append(op)
        import sys as _sys
        s.__dict__.setdefault("log", []).append((s.total, eng, _sys._getframe(1).f_code.co_name, len(op.waits), tuple(w)[:2]))
        if dma: s.__dict__["hist_" + eng].append((op, cost))
        return op


class Geo:
    def __init__(s, sample):
        s.sample = sample
        if sample: s.NT, s.T, s.NCH, s.NE, s.M = 64, 4, 16, 80, 2
        else: s.NT, s.T, s.NCH, s.NE, s.M = 512, 128, 4, 513, 7
        s.CE = s.NCH * (s.T + 1)

    def cur(s, v):
        if s.sample: return v[:, 0:80].re("p (g t) -> p g t", t=5)[:, :, 1:5]
        return v[:, 1:513]

    def prev(s, v):
        if s.sample: return v[:, 0:80].re("p (g t) -> p g t", t=5)[:, :, 0:4]
        return v[:, 0:512]

    def cmp(s, v):
        if s.sample: return v[:, 0:64].re("p (g t) -> p g t", t=4)
        return v[:, 0:512]

    def c3(s, v):
        return v[:, 0:s.NT].re("p (n t) -> p n t", t=s.T)

    def ccur(s, v):
        return v[:, 0:s.CE].re("p (n t) -> p n t", t=s.T + 1)[:, :, 1:s.T + 1]

    def cprev(s, v):
        return v[:, 0:s.CE].re("p (n t) -> p n t", t=s.T + 1)[:, :, 0:s.T]


class Builder:
    def __init__(b, nc):
        b.nc = nc; b.s = Sched(); b.bank_i = 0; b.reserved = set()

    def _k(b, vs):
        ks = []
        for v in vs:
            if v is None or isinstance(v, (int, float)): continue
            ks.extend(v.keys)
        return ks

    def _a(b, v):
        return v.ap if isinstance(v, V) else v

    def mm(b, out, lhsT, rhs, start=True, stop=True):
        o, l, r = out.ap, lhsT.ap, rhs.ap
        b.s.add("pe", lambda e: e.matmul(o, lhsT=l, rhs=r, start=start, stop=stop), b._k([lhsT, rhs]), b._k([out]))

    def tr(b, out, in_, ident):
        o, i, d = out.ap, in_.ap, ident.ap
        b.s.add("pe", lambda e: e.transpose(o, i, d), b._k([in_, ident]), b._k([out]))

    def act(b, out, in_, func, bias=0.0, scale=1.0):
        o, i, bi, sc = out.ap, in_.ap, b._a(bias), b._a(scale)
        b.s.add("act", lambda e: e.activation(out=o, in_=i, func=func, bias=bi, scale=sc), b._k([in_, bias, scale]), b._k([out]))

    def tt(b, eng, out, in0, in1, op):
        o, x, y = out.ap, in0.ap, in1.ap
        b.s.add(eng, lambda e: e.tensor_tensor(out=o, in0=x, in1=y, op=op), b._k([in0, in1]), b._k([out]))

    def ts(b, eng, out, in0, s1, s2=None, op0=MUL, op1=None):
        o, x, a1, a2 = out.ap, in0.ap, b._a(s1), b._a(s2)
        if op1 is None:
            fn = lambda e: e.tensor_scalar(out=o, in0=x, scalar1=a1, scalar2=None, op0=op0)
        else:
            fn = lambda e: e.tensor_scalar(out=o, in0=x, scalar1=a1, scalar2=a2, op0=op0, op1=op1)
        b.s.add(eng, fn, b._k([in0, s1, s2]), b._k([out]))

    def stt(b, out, in0, sc, in1, op0, op1):
        o, x, a, y = out.ap, in0.ap, b._a(sc), in1.ap
        b.s.add("dve", lambda e: e.scalar_tensor_tensor(out=o, in0=x, scalar=a, in1=y, op0=op0, op1=op1), b._k([in0, sc, in1]), b._k([out]))

    def cp(b, eng, out, in_):
        o, i = out.ap, in_.ap
        if eng == "act":
            b.s.add("act", lambda e: e.activation(out=o, in_=i, func=AF.Copy), b._k([in_]), b._k([out]))
        else:
            b.s.add(eng, lambda e: e.tensor_copy(out=o, in_=i), b._k([in_]), b._k([out]))

    def recip(b, out, in_):
        o, i = out.ap, in_.ap
        b.s.add("dve", lambda e: e.reciprocal(out=o, in_=i), b._k([in_]), b._k([out]))

    def memset(b, eng, out, val):
        o = out.ap
        b.s.add(eng, lambda e: e.memset(o, val), [], b._k([out]))

    def scan(b, out, d0, d1):
        o, x, y = out.ap, d0.ap, d1.ap
        b.s.add("dve", lambda e: e.tensor_tensor_scan(out=o, data0=x, data1=y, initial=0.0, op0=MUL, op1=ADD), b._k([d0, d1]), b._k([out]))

    def dma(b, eng, out, in_, cost=None):
        if cost is None: cost = 256 if eng == "pool" else 16
        o, i = out.ap, in_.ap
        b.s.add(eng, lambda e: e.dma_start(out=o, in_=i), b._k([in_]), b._k([out]), dma=True, cost=cost)

    def bank(b, reserve=False):
        while True:
            i = b.bank_i % 8; b.bank_i += 1
            if i not in b.reserved: break
        if reserve: b.reserved.add(i)
        return V(b.ps[i][:], [("ps", i)])

    def release(b, *vs):
        for v in vs: b.reserved.discard(v.keys[0][1])


def build_nc():
    nc = bass.Bass("TRN2", target_bir_lowering=False)
    b = Builder(nc)

    def din(name, shape, dt=F32):
        return nc.dram_tensor(name, list(shape), dt, kind="ExternalInput").ap()

    def dout(name, shape, dt=F32):
        return nc.dram_tensor(name, list(shape), dt, kind="ExternalOutput").ap()

    xT = din("xT", [128, 8, NTOK]); pT = din("pT", [2, 128, 2, NTOK])
    shT = din("shT", [2, 128, 8, NSQ]); wkvT = din("wkvT", [2, 128, NSQ, 8, 64])
    prm_d = din("prm", [128, 2, NPC]); cst_d = din("cst", [128, CW])
    wsT_d = din("wsT", [2, 128, 8, 128]); bs_d = din("bs", [2, 1, 8, 128])
    w_in = din("w_in", [2, D, 7 * D]); w_out = din("w_out", [2, D, D]); w_up = din("w_up", [2, D, 4 * D])
    w_down = din("w_down", [2, 4 * D, D]); w_pe = din("w_pe", [2, 256, D]); w_pg = din("w_pg", [2, D, D])
    w1 = din("w1", [2, D, 64]); w2 = din("w2", [2, 64, D]); a1 = din("a1", [2, D, 64]); a2 = din("a2", [2, 64, D])
    g1 = din("g1", [2, D, 160]); g2 = din("g2", [2, 160, D])
    yT = dout("yT", [128, 8, NTOK]); wkvp_o = dout("wkvp", [2, 128, 8, 64]); shp_o = dout("shp", [2, 128, 8])
    wkvs_o = dout("wkvs", [2, 128, NSQ, 8, 64]); shs_o = dout("shs", [2, 128, 8, NSQ]); cvs_o = dout("cvs", [2, 128, 8, NSQ * DEC])
    scr = nc.dram_tensor("scr", [2, 2, 128, SLOTW], BF16, kind="Internal").ap()
    nW = {nm: nc.dram_tensor("n_" + nm, list(shp), BF16, kind="Internal").ap() for nm, shp in
          (("w_in", (2, D, 7 * D)), ("w_out", (2, D, D)), ("w_up", (2, D, 4 * D)), ("w_down", (2, 4 * D, D)), ("w_pg", (2, D, D)), ("w_pe", (2, 256, D)))}
    scrD = nc.dram_tensor("scrD", [2, 8, 128, 4096], BF16, kind="Internal").ap()
    srcW = {"w_in": w_in, "w_out": w_out, "w_up": w_up, "w_down": w_down, "w_pg": w_pg, "w_pe": w_pe}

    es = ExitStack()
    with es:
        def sb(name, shape, dt):
            return es.enter_context(nc.sbuf_tensor(name, list(shape), dt))

        Ht = sb("H", [128, 8, W], F32); F1t = sb("F1", [128, 8, W], F32)
        XNt = sb("XN", [128, 8, W], BF16); P1t = sb("P1", [128, 8, W], BF16)
        P4t = sb("P4", [128, 8, W], BF16); P5t = sb("P5", [128, 8, W], BF16)
        P6t = sb("P6", [128, 8, W], BF16); P7t = sb("P7", [128, 8, W], BF16)
        ARt = sb("AR", [128, 8, 2, W], BF16)
        TFt = sb("TF", [128, 5, W], F32); TBt = sb("TB", [128, 14, W], BF16)
        M4t = sb("M4", [128, 16, 4, 128], BF16); TTt = sb("TTm", [128, 16, 128], BF16)
        XBt = sb("XB", [128, 2, 4, 4, 128], BF16)
        NSLOT = 3
        WSt = [sb(f"WS{i}", [128, 4096], BF16) for i in range(NSLOT)]
        CBt = sb("CB", [128, CW], BF16); PRt = sb("PRM", [128, 2, NPC], F32)
        RKt = sb("RKO", [128, 2, 8, 128], BF16); WSMt = sb("WSM", [128, 2, 8, 128], BF16)
        BSt = sb("BSR", [1, 2, 8, 128], BF16)
        St = sb("S", [128, 2, 8, 64], F32); Sbt = sb("Sb", [128, 2, 8, 128], BF16)
        BMt = sb("BM", [128, 8, 128], BF16); KMt = sb("KM", [128, 8, 128], BF16); AMt = sb("AM", [128, 8, 128], BF16)
        IDFt = sb("IDF", [128, 128], F32)
        JNKt = sb("JNK", [128, 4], F32)
        P1flat = P1t[:].rearrange("p a w -> p (a w)")
        M4flat = M4t[:].rearrange("p h m t -> p (h m t)")
        def TBa(i): return V(M4flat[:, i * W:(i + 1) * W], [("TD", i)])
        def TFa(i): return V(M4flat[:, i * W:(i + 2) * W].bitcast(F32), [("TD", i), ("TD", i + 1)])
        XBflat = XBt[:].rearrange("p a k j t -> p (a k j t)")
        def XBa(i): return V(XBflat[:, i * W:(i + 1) * W], [("XD", i)])
        def fence_m4(col):
            b.memset("pool", V(JNKt[:, col:col + 1], [("TD", i) for i in range(15)] + [("M4", h) for h in range(16)]
                               + [("XD", i) for i in range(7)] + [("XB", s_, k_) for s_ in (0, 1) for k_ in range(4)]), 0.0)
        PTt = sb("PT", [128, 8, 16], F32)
        XCt = sb("XC", [128, 2, 8], BF16); RCt = sb("RC", [128, 2, 24], BF16)
        SHt = TFt[:, 3, 0:128].rearrange("p (c n) -> p c n", c=8); SH2t = TFt[:, 4, 0:128].rearrange("p (c n) -> p c n", c=8)
        SHPt = sb("SHP", [128, 8], F32)
        SHIt = TFt[:, 2, 0:128].rearrange("p (c n) -> p c n", c=8)
        b.ps = [es.enter_context(nc.psum_tensor(f"ps{i}", [128, 512], F32)) for i in range(8)]

        class Arr:
            def __init__(s, name, t): s.name = name; s.t = t
            def c(s, i): return V(s.t[:, i, :], [(s.name, i)])
            def all(s): return V(s.t[:], [(s.name, i) for i in range(8)])
            def rng(s, a, e): return V(s.t[:, a:e], [(s.name, i) for i in range(a, e)])

        H, F1, XN, P1, P4, P5, P6, P7 = (Arr(n, t) for n, t in (("H", Ht), ("F1", F1t), ("XN", XNt), ("P1", P1t), ("P4", P4t), ("P5", P5t), ("P6", P6t), ("P7", P7t)))

        def AR(c, j=None):
            if j is None: return V(ARt[:, c], [("AR", c)])
            return V(ARt[:, c, j, :], [("AR", c)])

        def TF(i): return V(TFt[:, i, :], [("TF", i)])
        def TB(i): return V(TBt[:, i, :], [("TB", i)])
        def TB2(i): return V(TBt[:, i:i + 2, :].rearrange("p a w -> p (a w)"), [("TB", i), ("TB", i + 1)])
        def CB(o, n): return V(CBt[:, o:o + n], [("CB",)])
        def PRM(l, col, n=1): return V(PRt[:, l, col:col + n], [("PRM", l)])
        def WS(i): return V(WSt[i][:], [("WS", i)])

        ident = CB(C_ID, 128); ones = CB(C_ONES, 128); bones = CB(C_BONES, 128); bmean = CB(C_BMEAN, 128)
        mask4 = CB(C_M4, 512).re("p (m t) -> p m t", m=4); maskL4 = CB(C_ML4, 512).re("p (m t) -> p m t", m=4)
        ident4 = CB(C_ID4, 512).re("p (m t) -> p m t", m=4)
        shiftI = CB(C_SH, 64)

        b.dma("sp", V(Ht[:, :, 0:512], [("H", i) for i in range(8)]), V(xT[:, :, 0:512], []))
        b.dma("pool", CB(0, CW), V(cst_d, []))
        b.dma("sp", V(PRt[:], [("PRM", 0), ("PRM", 1)]), V(prm_d, []))
        b.dma("sp", V(IDFt[:], [("IDF",)]), V(cst_d[:, C_ID:C_ID + 128], []))
        for l in range(2):
            b.ts("pool", PRM(l, OMM_R, 24), PRM(l, MU_R, 24), -1.0, 1.0, MUL, ADD)
            b.ts("pool", PRM(l, OMKA, 8), PRM(l, KA, 8), -1.0, 1.0, MUL, ADD)
        b.memset("pool", V(St[:], [("S", 0), ("S", 1)]), 0.0)
        b.memset("pool", V(Sbt[:], [("Sb", 0), ("Sb", 1)]), 0.0)
        for (t_, k_) in ((BMt, "BM"), (KMt, "KM"), (AMt, "AM")):
            b.memset("pool", V(t_[:], [(k_,)]), 0.0)
        b.memset("pool", V(XCt[:], [("XC", 0), ("XC", 1)]), 0.0)
        b.memset("pool", V(RCt[:], [("RC", 0), ("RC", 1)]), 0.0)

        def SCR(l, k): return V(scr[l, k], [("scr", l, k)])

        conv = [[], []]
        for l in range(2):
            stg = V(F1t[:].rearrange("p a w -> p (a w)"), [("F1", i) for i in range(8)])
            w1s = stg[:, 0:512].re("p (k n) -> p k n", k=8)
            a1s = stg[:, 512:1024].re("p (k n) -> p k n", k=8)
            g1s = stg[:, 1024:2304].re("p (k n) -> p k n", k=8)
            wss = stg[:, 2304:3328].re("p (g t) -> p g t", g=8)
            bss = V(TFt[:].rearrange("p a w -> p (a w)"), [("TF", i) for i in range(5)])[0:1, 0:1024].re("p (g t) -> p g t", g=8)
            b.dma("sp", w1s, V(w1[l].rearrange("(k p) n -> p k n", p=128), []))
            b.dma("sp", a1s, V(a1[l].rearrange("(k p) n -> p k n", p=128), []))
            b.dma("sp", g1s, V(g1[l].rearrange("(k p) n -> p k n", p=128), []))
            b.dma("sp", wss, V(wsT_d[l], []))
            b.dma("sp", bss, V(bs_d[l], []))
            LBflat = V(M4flat[:, 0:4608], [("TD", i) for i in range(9)])
            LB = LBflat.re("p (k n) -> p k n", k=8)
            for (src, mu, o, n) in ((w1s, MU_W, 0, 64), (a1s, MU_A, 128, 64), (g1s, MU_G, 256, 160)):
                b.cp("dve", LB[:, :, o:o + n], src)
                muv = V(PRt[:, l, mu:mu + 8].unsqueeze(2).to_broadcast([128, 8, n]), [("PRM", l)])
                b.tt("dve", LB[:, :, o + n:o + 2 * n], src, muv, MUL)
            b.dma("sp", V(scr[l, 0, :, 0:4608], [("scr", l, 0)]), LBflat)
            b.tt("dve", V(WSMt[:, l], [("WSM", l)]), wss, V(CBt[:, C_M4 + 128:C_M4 + 256].unsqueeze(1).to_broadcast([128, 8, 128]), [("CB",)]), MUL)
            b.cp("dve", V(BSt[0:1, l], [("BSR", l)]), bss)
            for c in range(8):
                b.ts("dve", V(RKt[:, l, c, :], [("RKO", l)]), bones, PRM(l, RK + c), None, MUL)
            cl = conv[l]
            def cv(dst, key, src, cost, cl=cl):
                cl.append((lambda: b.dma("pool", V(dst, [key]), V(src, []), cost=cost)))
            cv(scr[l, 1, 0:64, 0:1024], ("scr", l, 1), w2[l], 8)
            cv(scr[l, 1, 0:64, 1024:2048], ("scr", l, 1), a2[l], 8)
            cv(scr[l, 1, 0:128, 2048:3072], ("scr", l, 1), g2[l, 0:128], 8)
            cv(scr[l, 1, 0:32, 3072:4096], ("scr", l, 1), g2[l, 128:160], 8)
            def natcv(nm):
                rows = srcW[nm].shape[1]
                npart = 4 if rows >= 1024 else 1
                rp = rows // npart
                for i in range(npart):
                    cv(nW[nm][l, i * rp:(i + 1) * rp, :], ("nw", l, nm), srcW[nm][l, i * rp:(i + 1) * rp, :], 40)
            natcv("w_in"); natcv("w_out"); natcv("w_up")
            wdv = w_down[l].rearrange("(k p) n -> p k n", p=128)
            for j in range(8):
                cv(scrD[l, j].rearrange("p (k n) -> p k n", k=32), ("scrD", l, j), wdv[:, :, 128 * j:128 * (j + 1)], 258)
            natcv("w_pg"); natcv("w_pe")

        convq = conv[0] + conv[1]
        def drain(n):
            for _ in range(min(n, len(convq))):
                convq.pop(0)()
        drain(8)

        class WStream:
            def __init__(s): s.seq = []; s.issued = 0; s.used = 0
            def plan(s, items): s.seq.extend(items)
            def _issue(s):
                l, k = s.seq[s.issued]
                sl = s.issued % NSLOT
                wk = [("WS", sl)]
                def nat(nm): return nW[nm][l].rearrange("(k p) n -> p k n", p=128), [("nw", l, nm)]
                if k == 1:
                    for (np_, c0, c1) in [(64, 0, 2048), (128, 2048, 3072), (32, 3072, 4096)]:
                        b.dma("sp", V(WSt[sl][0:np_, c0:c1], wk), V(scr[l, 1, 0:np_, c0:c1], [("scr", l, 1)]))
                elif k < 8:
                    src, sk = nat("w_in")
                    dst = WSt[sl][:, 0:4096].rearrange("p (k n) -> p k n", k=8)
                    for q in range(4):
                        fb = 4 * (k - 2) + q; c, j = fb // 3, fb % 3
                        b.dma("sp", V(dst[:, :, q * 128:(q + 1) * 128], wk), V(src[:, :, j * 1024 + c * 128:j * 1024 + (c + 1) * 128], sk), cost=64)
                elif k < 26 or k in (34, 35):
                    nm, j, off = ("w_in", k - 8, 3072) if k < 16 else (("w_out", k - 16, 0) if k < 18 else (("w_up", k - 18, 0) if k < 26 else ("w_pg", k - 34, 0)))
                    src, sk = nat(nm)
                    dst = WSt[sl][:, 0:4096].rearrange("p (k n) -> p k n", k=8)
                    b.dma("sp", V(dst, wk), V(src[:, :, off + 512 * j:off + 512 * (j + 1)], sk), cost=64)
                elif k < 34:
                    j = k - 26
                    b.dma("sp", V(WSt[sl][:, 0:4096], wk), V(scrD[l, j], [("scrD", l, j)]))
                else:
                    src, sk = nat("w_pe")
                    j = k - 36
                    dst = WSt[sl][:, 0:1024].rearrange("p (k n) -> p k n", k=2)
                    b.dma("sp", V(dst, wk), V(src[:, :, 512 * j:512 * (j + 1)], sk))
                s.issued += 1
            def get(s, ahead=NSLOT - 1):
                while s.issued < min(len(s.seq), s.used + 1 + ahead): s._issue()
                v = WS(s.used % NSLOT); s.used += 1
                return v
        ws = WStream()
        ORDER = [1, 2, 3, 4, 5, 6, 7, 1, 10, 11, 8, 9, 12, 13, 14, 15, 16, 17] + list(range(18, 34)) + [34, 36, 35, 37]
        tiles = [(False, i) for i in range(4)] + [(True, 0)]
        for _t in tiles:
            for l in range(2):
                ws.plan([(l, k) for k in ORDER])

        def rms_stats(g, arr):
            NT = g.NT
            ps = b.bank()
            for c in range(8):
                sq = TB(c % 2)
                b.act(sq[:, 0:NT], arr.c(c)[:, 0:NT], AF.Square)
                b.mm(ps[:, 0:NT], ones, sq[:, 0:NT], c == 0, c == 7)
            b.act(TF(0)[:, 0:NT], ps[:, 0:NT], AF.Ln, bias=V(EPSt[:, 0:1], [("EPSC",)]), scale=1.0 / D)
            b.act(TF(1)[:, 0:NT], TF(0)[:, 0:NT], AF.Exp, scale=-0.5)
            return TF(1)

        EPSt = sb("EPSC", [128, 2], F32)
        b.memset("pool", V(EPSt[:, 0:1], [("EPSC",)]), EPS)
        b.memset("pool", V(EPSt[:, 1:2], [("EPSC",)]), GN_EPS)
        epsv = V(EPSt[:, 0:1], [("EPSC",)]); gnepsv = V(EPSt[:, 1:2], [("EPSC",)])

        def load_l1(ln):
            fence_m4(2)
            b.dma("sp", V(M4flat[:, 0:4608], [("TD", i) for i in range(9)]), V(scr[ln, 0, :, 0:4608], [("scr", ln, 0)]))
        load_l1(0)
        b.s.marks.append(("init_end", b.s.total, len(b.s.ops["pe"])))
        for (sample, ti) in tiles:
            g = Geo(sample); NT, T, NCH = g.NT, g.T, g.NCH
            col0 = SEQ if sample else ti * 512
            last_prompt = (not sample) and ti == 3
            if sample or ti > 0:
                b.dma("sp", V(Ht[:, :, 0:NT], H.all().keys), V(xT[:, :, col0:col0 + NT], []), cost=64)
            for l in range(2):
                b.s.marks.append(("Phase A t%d l%d" % (ti + 4 * sample, l), b.s.total, len(b.s.ops["pe"])))
                if ti == 0 and l == 0 and not sample: drain(0)
                rstd = rms_stats(g, H)
                for c in range(8):
                    b.stt(g.cur(XN.c(c)), g.cmp(H.c(c)), PRM(l, G_PRE + c), g.cmp(rstd), MUL, MUL)
                if sample:
                    b.dma("sp", V(SHIt[:], [("TF", 2)]), V(shT[l], []))
                    b.cp("pool", V(XNt[:, :, 0:80].rearrange("p c (g t) -> p c g t", t=5)[:, :, :, 0], XN.all().keys), V(SHIt[:], [("TF", 2)]))
                    hv = V(Ht[:, :, 0:64].rearrange("p c (g t) -> p c g t", t=4)[:, :, :, 3], H.all().keys)
                    b.tt("pool", V(SHt[:], [("TF", 3)]), hv, V(PRt[:, l, G_PRE:G_PRE + 8].unsqueeze(2).to_broadcast([128, 8, NSQ]), [("PRM", l)]), MUL)
                    rv = V(TFt[:, 1, 0:64].rearrange("p (g t) -> p g t", t=4)[:, :, 3].unsqueeze(1).to_broadcast([128, 8, NSQ]), [("TF", 1)])
                    b.tt("pool", V(SH2t[:], [("TF", 4)]), V(SHt[:], [("TF", 3)]), rv, MUL)
                    b.dma("sp", V(shs_o[l], []), V(SH2t[:], [("TF", 4)]))
                else:
                    b.cp("pool", V(XNt[:, :, 0:1], XN.all().keys), V(XCt[:, l, :].unsqueeze(2), [("XC", l)]))
                    b.cp("pool", V(XCt[:, l, :].unsqueeze(2), [("XC", l)]), V(XNt[:, :, 512:513], XN.all().keys))
                    if last_prompt:
                        b.tt("pool", V(SHt[:, :, 0:1], [("TF", 3)]), V(Ht[:, :, 511:512], H.all().keys), V(PRt[:, l, G_PRE:G_PRE + 8].unsqueeze(2), [("PRM", l)]), MUL)
                        b.ts("pool", V(SHPt[:], [("SHP",)]), V(SHt[:, :, 0], [("TF", 3)]), rstd[:, 511:512], None, MUL)
                        b.dma("sp", V(shp_o[l], []), V(SHPt[:], [("SHP",)]))
                for c in range(8):
                    b.tt("pool", g.cmp(P4.c(c)), g.prev(XN.c(c)), g.cur(XN.c(c)), SUB)
                b.memset("pool", V(F1t[:, :, 0:g.CE].rearrange("p c (n t) -> p c n t", t=T + 1)[:, :, :, 0], F1.all().keys), 0.0)

                b.s.marks.append(("Phase B t%d l%d" % (ti + 4 * sample, l), b.s.total, len(b.s.ops["pe"])))
                if ti == 0 and l == 0 and not sample: drain(4)
                L1f = V(M4flat[:, 0:4608], [("TD", i) for i in range(9)])
                L1 = L1f.re("p (k n) -> p k n", k=8)
                def lora1(o, n, m0, m1, ps):
                    for kc in range(8):
                        b.mm(g.cmp(ps[0:m1 - m0, :]), L1[:, kc, o + m0:o + m1], g.cur(XN.c(kc)), kc == 0, False)
                        b.mm(g.cmp(ps[0:m1 - m0, :]), L1[:, kc, o + n + m0:o + n + m1], g.cmp(P4.c(kc)), False, kc == 7)
                ps = b.bank(); lora1(0, 64, 0, 64, ps)
                b.act(TB(2)[0:64, 0:NT], ps[0:64, 0:NT], AF.Tanh)
                ps = b.bank(); lora1(128, 64, 0, 64, ps)
                b.cp("dve", TB(3)[0:64, 0:NT], ps[0:64, 0:NT])
                ps = b.bank(); lora1(256, 160, 0, 128, ps)
                b.act(TB(4)[:, 0:NT], ps[:, 0:NT], AF.Sigmoid)
                ps = b.bank(); lora1(256, 160, 128, 160, ps)
                b.act(TB(5)[0:32, 0:NT], ps[0:32, 0:NT], AF.Sigmoid)
                L2 = ws.get()
                rsm = CB(C_RSS, 64) if sample else CB(C_RSP, 512)
                for c in range(8):
                    ps = b.bank()
                    b.mm(ps[:, 0:NT], L2[0:64, c * 128:(c + 1) * 128], TB(2)[0:64, 0:NT])
                    b.act(TF(2)[:, 0:NT], ps[:, 0:NT], AF.Sigmoid, bias=PRM(l, W0 + c))
                    for n in range(NCH):
                        b.scan(g.ccur(F1.c(c))[:, n, :], ones[:, 0:T], TF(2)[:, n * T:(n + 1) * T])
                    ps = b.bank()
                    b.mm(ps[:, 0:NT], L2[0:64, 1024 + c * 128:1024 + (c + 1) * 128], TB(3)[0:64, 0:NT])
                    b.act(P1.c(c)[:, 0:NT], ps[:, 0:NT], AF.Sigmoid, bias=PRM(l, A0 + c))

                b.s.marks.append(("Phase C/D t%d l%d" % (ti + 4 * sample, l), b.s.total, len(b.s.ops["pe"])))
                if ti == 0 and l == 0 and not sample: drain(0)
                fence_m4(2)
                wcur = None
                def rkv_block(fb):
                    nonlocal wcur
                    if fb % 4 == 0: wcur = ws.get()
                    return wcur[:, 0:4096].re("p (k n) -> p k n", k=8)[:, :, (fb % 4) * 128:(fb % 4 + 1) * 128]
                def proj_rkv(c, j, outv, eb):
                    wv = rkv_block(3 * c + j)
                    ps = b.bank()
                    NE = g.NE if sample else 512
                    rhs = (lambda kc: XN.c(kc)[:, 0:80]) if sample else (lambda kc: XN.c(kc)[:, 1:513])
                    for kc in range(8):
                        b.mm(ps[:, 0:NE], wv[:, kc, :], rhs(kc), kc == 0, kc == 7)
                    mu = PRM(l, MU_R + 8 * j + c); om = PRM(l, OMM_R + 8 * j + c)
                    if sample:
                        b.act(eb[:, 0:80], ps[:, 0:80], AF.Copy, scale=mu)
                        raw = ps[:, 0:80].re("p (g t) -> p g t", t=5)[:, :, 1:5]
                    else:
                        b.act(eb[:, 1:513], ps[:, 0:512], AF.Copy, scale=mu)
                        rc = V(RCt[:, l, 8 * j + c:8 * j + c + 1], [("RC", l)])
                        b.cp("pool", eb[:, 0:1], rc)
                        b.cp("pool", rc, eb[:, 512:513])
                        raw = ps[:, 0:512]
                    b.stt(g.cmp(outv), raw, om, g.prev(eb), MUL, ADD)
                def d_temps(c):
                    s3 = c % 3
                    if s3 == 0: Pe, Pinv, rr, kraw, ebR, ebK = TB(6), TB(7), TB(8), TB(9), TB(0), TB(1)
                    elif s3 == 1: Pe, Pinv, rr, kraw, ebR, ebK = TBa(0), TBa(1), TBa(2), TBa(3), TBa(14), V(TFt[:, 2, :].bitcast(BF16)[:, 0:W], [("TF", 2)])
                    else: Pe, Pinv, rr, kraw, ebR, ebK = XBa(0), XBa(1), XBa(2), XBa(3), XBa(4), XBa(5)
                    if c % 2 == 0: sqk, kkn, f, kp, rk, bb, sdT, rsT = TB(10), TB(11), TB(12), TB(13), TB(2), TB(3), TF(0), TF(1)
                    else: sqk, kkn, f, kp, rk, bb, sdT, rsT = TBa(4), TBa(5), TBa(6), TBa(7), TBa(8), TBa(9), TFa(10), TFa(12)
                    return Pe, Pinv, rr, kraw, sqk, kkn, f, kp, rk, bb, sdT, rsT, ebR, ebK
                def d_stage1(c):
                    Pe, Pinv, rr, kraw, sqk, kkn, f, kp, rk, bb, sdT, rsT, ebR, ebK = d_temps(c)
                    b.act(Pe[:, 0:g.CE], F1.c(c)[:, 0:g.CE], AF.Exp, scale=-LAM)
                    b.act(g.c3(Pinv), g.ccur(F1.c(c)), AF.Exp, scale=LAM)
                    endv = V(F1t[:, c, 0:g.CE].rearrange("p (n t) -> p n t", t=T + 1)[:, :, T], [("F1", c)])
                    b.act(V(PTt[:, c, 0:NCH], [("PT",)]), endv, AF.Exp, scale=-LAM)
                    proj_rkv(c, 0, rr, ebR)
                    proj_rkv(c, 1, kraw, ebK)
                    proj_rkv(c, 2, P6.c(c), ebR)
                def d_stage2a(c):
                    Pe, Pinv, rr, kraw, sqk, kkn, f, kp, rk, bb, sdT, rsT, ebR, ebK = d_temps(c)
                    b.tt("dve", g.c3(AR(c, 1)), g.c3(rr), g.ccur(Pe), MUL)
                    b.act(sqk[:, 0:NT], kraw[:, 0:NT], AF.Square, scale=PRM(l, KK + c))
                    ps = b.bank()
                    b.mm(ps[:, 0:NT], bones, sqk[:, 0:NT])
                    b.ts("dve", sdT[:, 0:NT], ps[:, 0:NT], 1e-24, None, MAX)
                    b.act(sdT[:, 0:NT], sdT[:, 0:NT], AF.Ln)
                    b.act(rsT[:, 0:NT], sdT[:, 0:NT], AF.Exp, scale=-0.5)
                    b.act(f[:, 0:NT], P1.c(c)[:, 0:NT], AF.Identity, bias=PRM(l, OMKA + c), scale=PRM(l, KA + c))
                    b.stt(kkn[:, 0:NT], kraw[:, 0:NT], PRM(l, KK + c), rsT[:, 0:NT], MUL, MUL)
                def d_stage2b(c):
                    Pe, Pinv, rr, kraw, sqk, kkn, f, kp, rk, bb, sdT, rsT, ebR, ebK = d_temps(c)
                    b.tt("pool", kp[:, 0:NT], kraw[:, 0:NT], f[:, 0:NT], MUL)
                    b.tt("pool", P4.c(c)[:, 0:NT], kp[:, 0:NT], Pinv[:, 0:NT], MUL)
                    b.tt("dve", bb[:, 0:NT], kkn[:, 0:NT], P1.c(c)[:, 0:NT], MUL)
                    b.tt("pool", P5.c(c)[:, 0:NT], bb[:, 0:NT], Pinv[:, 0:NT], MUL)
                    b.stt(g.c3(AR(c, 0)), g.c3(kkn), -1.0, g.cprev(Pe), MUL, MUL)
                    b.tt("pool", rk[:, 0:NT], rr[:, 0:NT], kp[:, 0:NT], MUL)
                def d_stage2c(c):
                    Pe, Pinv, rr, kraw, sqk, kkn, f, kp, rk, bb, sdT, rsT, ebR, ebK = d_temps(c)
                    ps2 = b.bank()
                    b.mm(ps2[:, 0:NT], V(RKt[:, l, c, :], [("RKO", l)]), rk[:, 0:NT])
                    b.tt("dve", P7.c(c)[:, 0:NT], ps2[:, 0:NT], P6.c(c)[:, 0:NT], MUL)

                d_stage1(0); d_stage1(1); d_stage2a(0)
                for c in range(8):
                    if ti == 0 and l == 0 and not sample: drain(1)
                    if c + 2 < 8: d_stage1(c + 2)
                    if c + 1 < 8: d_stage2a(c + 1)
                    d_stage2b(c)
                    if c >= 1: d_stage2c(c - 1)
                d_stage2c(7)
                b.s.marks.append(("Phase E t%d l%d" % (ti + 4 * sample, l), b.s.total, len(b.s.ops["pe"])))
                if ti == 0 and l == 0 and not sample: drain(0)
                fence_m4(3)
                vt, kt, bt, Wb, Ub = TB2(0), TB2(2), TB2(6), TB2(8), TB2(10)
                Ystg = V(TFt[:, 3:5, :].rearrange("p a w -> p (a w)"), [("TF", 3), ("TF", 4)])
                def chunk_gen(n):
                    cs = slice(n * T, (n + 1) * T)
                    si = (n % 2) if sample else l
                    Sv = V(St[:, si], [("S", si)])
                    def sbd_update():
                        b.cp("act", V(Sbt[0:64, si, :, 0:64], [("Sb", si)]), V(St[0:64, si], [("S", si)]))
                        b.cp("act", V(Sbt[64:128, si, :, 64:128], [("Sb", si)]), V(St[64:128, si], [("S", si)]))
                    def Sbd(c): return V(Sbt[:, si, c, :], [("Sb", si)])
                    if sample:
                        b.dma("sp", Sv, V(wkvT[l, :, n], []))
                        sbd_update()
                    def XBs(sidx, k, j=None):
                        if sidx < 2:
                            t_ = XBt[:, sidx, k] if j is None else XBt[:, sidx, k, j]
                        else:
                            base = P1flat[:, (sidx - 2) * 2048:(sidx - 1) * 2048].rearrange("p (k j t) -> p k j t", k=4, j=4)
                            t_ = base[:, k] if j is None else base[:, k, j]
                        return V(t_, [("XB", sidx, k)])
                    if n == 0:
                        b.memset("pool", V(JNKt[:, 0:1], list(P1.all().keys) + [("XB", s_, k_) for s_ in (2, 3) for k_ in range(4)]), 0.0)
                    b.cp("act", V(BMt[64:128, :, 0:T], [("BM",)]), V(P5t[64:128, :, cs], P5.all().keys))
                    b.cp("dve", V(KMt[64:128, :, 0:T], [("KM",)]), V(P4t[64:128, :, cs], P4.all().keys))
                    b.cp("act", V(AMt[64:128, :, 0:T], [("AM",)]), V(ARt[64:128, :, 0, cs], [("AR", c_) for c_ in range(8)]))
                    for hg in range(4):
                        psLr = b.bank(True)
                        psL = psLr.re("p (m t) -> p m t", m=4)
                        for j in range(4):
                            h = 4 * hg + j; c = h // 2
                            psG = b.bank().re("p (m t) -> p m t", m=4)
                            if h % 2 == 0:
                                rhs = AR(c)[0:64, :, cs]
                                b.mm(psG[0:T, 0:2, 0:T], P5.c(c)[0:64, cs], rhs)
                                b.mm(psG[0:T, 2:4, 0:T], P4.c(c)[0:64, cs], rhs)
                                b.tt("dve", V(M4t[0:T, h, :, 0:T], [("M4", h)]), psG[0:T, :, 0:T], mask4[0:T, :, 0:T], MUL)
                                b.mm(psL[0:T, j, 0:T], AR(c, 0)[0:64, cs], P5.c(c)[0:64, cs])
                            else:
                                rhs = AR(c)[:, :, cs]
                                b.mm(psG[0:T, 0:2, 0:T], V(BMt[:, c, 0:T], [("BM",)]), rhs)
                                b.mm(psG[0:T, 2:4, 0:T], V(KMt[:, c, 0:T], [("KM",)]), rhs)
                                tmpm = TB(12 + c % 2)[0:T, 0:512].re("p (m t) -> p m t", m=4)[:, :, 0:T]
                                b.cp("act", tmpm, psG[0:T, :, 0:T])
                                b.tt("pool", V(M4t[0:T, h, :, 0:T], [("M4", h)]), tmpm, mask4[0:T, :, 0:T], MUL)
                                b.mm(psL[0:T, j, 0:T], V(AMt[:, c, 0:T], [("AM",)]), P5.c(c)[:, cs])
                        b.tt("dve", XBs(hg, 0)[0:T, :, 0:T], psL[0:T, :, 0:T], maskL4[0:T, :, 0:T], MUL)
                        b.release(psLr)
                        hs = slice(4 * hg, 4 * hg + 4)
                        hk = [("M4", h) for h in range(4 * hg, 4 * hg + 4)]
                        b.tt("pool", V(TTt[0:T, hs, 0:T], [("TTm", hg)]), V(M4t[0:T, hs, 0, 0:T], hk), ident4[0:T, :, 0:T], ADD)
                    for (src, dst, eng) in ((P6, vt, "act"), (P4, kt, "dve"), (P5, bt, "act")):
                        psb = b.bank().cast(BF16)
                        for c in range(8):
                            b.tr(psb[0:T, c * 128:(c + 1) * 128], src.c(c)[:, cs], ident)
                        b.cp(eng, dst[0:T, 0:1024], psb[0:T, 0:1024])
                    yield 1
                    Xc = {hg: (lambda hg: (lambda j: XBs(hg, 0, j)[0:T, 0:T]))(hg) for hg in range(4)}
                    XTc = {hg: (lambda hg: (lambda j: V(M4t[0:T, 4 * hg + j, 0, 0:T], [("M4", 4 * hg + j)])))(hg) for hg in range(4)}
                    for k in range(1, g.M):
                        kx = k % 2
                        for hg in range(4):
                            psX = b.bank().re("p (m t) -> p m t", m=4)
                            for j in range(4):
                                b.mm(psX[0:T, j, 0:T], XTc[hg](j), Xc[hg](j))
                            b.cp("act", XBs(hg, kx)[0:T, :, 0:T], psX[0:T, :, 0:T])
                        if k < g.M - 1:
                            for hg in range(4):
                                psXT = b.bank().re("p (m t) -> p m t", m=4)
                                for j in range(4):
                                    b.mm(psXT[0:T, j, 0:T], Xc[hg](j), XTc[hg](j))
                                b.cp("dve" if hg % 2 == 0 else "act", XBs(hg, 2 + kx)[0:T, :, 0:T], psXT[0:T, :, 0:T])
                        for hg in range(4):
                            Xc[hg] = (lambda hg, kx: (lambda j: XBs(hg, kx, j)[0:T, 0:T]))(hg, kx)
                            XTc[hg] = (lambda hg, kx: (lambda j: XBs(hg, 2 + kx, j)[0:T, 0:T]))(hg, kx)
                        for hg in range(4):
                            psT = b.bank().re("p (m t) -> p m t", m=4)
                            TTv = V(TTt[0:T, 4 * hg:4 * hg + 4, 0:T], [("TTm", hg)])
                            for j in range(4):
                                b.mm(psT[0:T, j, 0:T], Xc[hg](j), V(TTt[0:T, 4 * hg + j, 0:T], [("TTm", hg)]))
                            b.tt("dve", TTv, psT[0:T, :, 0:T], TTv, ADD)
                    if n == NCH - 1:
                        b.memset("pool", V(JNKt[:, 1:2], list(P1.all().keys) + [("XB", s_, k_) for s_ in (2, 3) for k_ in range(4)]), 0.0)
                    yield 2
                    def hsl(h): return slice(h * 64, (h + 1) * 64)
                    psW = [b.bank(), b.bank()]
                    for c in range(8):
                        o = psW[c // 4][0:T, (c % 4) * 128:(c % 4 + 1) * 128]
                        b.mm(o, AR(c, 0)[:, cs], Sbd(c), True, False)
                        for h2 in range(2):
                            h = 2 * c + h2
                            oo = psW[c // 4][0:T, (c % 4) * 128 + h2 * 64:(c % 4) * 128 + h2 * 64 + 64]
                            b.mm(oo, V(M4t[0:T, h, 2, 0:T], [("M4", h)]), vt[0:T, hsl(h)], False, h2 == 1)
                    b.cp("act", Wb[0:T, 0:512], psW[0][0:T, :])
                    b.cp("dve", Wb[0:T, 512:1024], psW[1][0:T, :])
                    psU = [b.bank(), b.bank()]
                    for h in range(16):
                        b.mm(psU[h // 8][0:T, (h % 8) * 64:(h % 8 + 1) * 64], V(TTt[0:T, h, 0:T], [("TTm", h // 4)]), Wb[0:T, hsl(h)])
                    b.cp("act", Ub[0:T, 0:512], psU[0][0:T, :])
                    b.cp("dve", Ub[0:T, 512:1024], psU[1][0:T, :])
                    psY = [b.bank(True), b.bank(True)]
                    for c in range(8):
                        o = psY[c // 4][0:T, (c % 4) * 128:(c % 4 + 1) * 128]
                        b.mm(o, AR(c, 1)[:, cs], Sbd(c), True, False)
                        for h2 in range(2):
                            h = 2 * c + h2
                            oo = psY[c // 4][0:T, (c % 4) * 128 + h2 * 64:(c % 4) * 128 + h2 * 64 + 64]
                            b.mm(oo, V(M4t[0:T, h, 1, 0:T], [("M4", h)]), Ub[0:T, hsl(h)], False, False)
                            b.mm(oo, V(M4t[0:T, h, 3, 0:T], [("M4", h)]), vt[0:T, hsl(h)], False, h2 == 1)
                    psDr = [b.bank(True), b.bank(True)]
                    psD = [psDr[0].re("p (m t) -> p m t", m=4), psDr[1].re("p (m t) -> p m t", m=4)]
                    for c in range(8):
                        o = psD[c // 4][:, c % 4, :]
                        csl = slice(c * 128, (c + 1) * 128)
                        b.mm(o, bt[0:T, csl], Ub[0:T, csl], True, False)
                        b.mm(o, kt[0:T, csl], vt[0:T, csl], False, True)
                    yield 3
                    b.cp("act", Ystg[0:T, 0:512], psY[0][0:T, :])
                    b.cp("dve", Ystg[0:T, 512:1024], psY[1][0:T, :])
                    b.release(psY[0], psY[1])
                    psYT = [b.bank().re("p (m t) -> p m t", m=4), b.bank().re("p (m t) -> p m t", m=4)]
                    for c in range(8):
                        b.tr(psYT[c // 4][:, c % 4, 0:T], Ystg[0:T, c * 128:(c + 1) * 128], V(IDFt[0:T, 0:T], [("IDF",)]))
                    b.cp("act", V(F1t[:, 0:4, cs], F1.rng(0, 4).keys), psYT[0][:, :, 0:T])
                    b.cp("dve", V(F1t[:, 4:8, cs], F1.rng(4, 8).keys), psYT[1][:, :, 0:T])
                    for bk in range(2):
                        for hh in range(2):
                            pp = slice(64 * hh, 64 * hh + 64)
                            sv = V(St[pp, si, 4 * bk:4 * bk + 4, :], [("S", si)])
                            b.tt("dve", sv, psD[bk][pp, :, 64 * hh:64 * hh + 64], sv, ADD)
                    b.release(psDr[0], psDr[1])
                    ptv = V(PTt[:, :, n:n + 1].to_broadcast([128, 8, 64]), [("PT",)])
                    b.tt("pool", Sv, Sv, ptv, MUL)
                    if sample:
                        b.dma("sp", V(wkvs_o[l, :, n], []), Sv)
                    else:
                        sbd_update()
                        if last_prompt and n == NCH - 1:
                            b.dma("sp", V(wkvp_o[l], []), Sv)

                gens = [chunk_gen(n) for n in range(NCH)]
                next(gens[0])
                for n in range(NCH):
                    if ti == 0 and l == 0 and not sample: drain(2)
                    next(gens[n]); next(gens[n])
                    if n + 1 < NCH: next(gens[n + 1])
                    for _ in gens[n]: pass
                b.s.marks.append(("Phase F t%d l%d" % (ti + 4 * sample, l), b.s.total, len(b.s.ops["pe"])))
                if ti == 0 and l == 0 and not sample: drain(5)
                fence_m4(2)
                L2 = ws.get()
                def f_temps(c):
                    return (TB(6), TB(7), TF(0), TF(1), TF(2), TF(3), TF(4)) if c % 2 == 0 else (TBa(0), TBa(1), TFa(2), TFa(4), TFa(6), TFa(8), TFa(10))
                fps = {}
                def f_a(c):
                    ybf, y2, f0, f1, f2, f3, f4 = f_temps(c)
                    b.cp("pool", ybf[:, 0:NT], F1.c(c)[:, 0:NT])
                    b.act(y2[:, 0:NT], F1.c(c)[:, 0:NT], AF.Square)
                    psM = b.bank(True); psE = b.bank()
                    fps[c] = psM
                    b.mm(psM[:, 0:NT], bmean, ybf[:, 0:NT])
                    b.mm(psE[:, 0:NT], bmean, y2[:, 0:NT])
                    b.act(f0[:, 0:NT], psM[:, 0:NT], AF.Square)
                    b.tt("dve", f1[:, 0:NT], psE[:, 0:NT], f0[:, 0:NT], SUB)
                    b.ts("dve", f1[:, 0:NT], f1[:, 0:NT], 0.0, None, MAX)
                    b.act(f1[:, 0:NT], f1[:, 0:NT], AF.Ln, bias=gnepsv)
                    b.act(f2[:, 0:NT], f1[:, 0:NT], AF.Exp, scale=-0.5)
                def f_b(c):
                    ybf, y2, f0, f1, f2, f3, f4 = f_temps(c)
                    psM = fps.pop(c)
                    b.tt("dve", f3[:, 0:NT], F1.c(c)[:, 0:NT], psM[:, 0:NT], SUB)
                    b.release(psM)
                    b.tt("pool", f4[:, 0:NT], f3[:, 0:NT], f2[:, 0:NT], MUL)
                    b.stt(f3[:, 0:NT], f4[:, 0:NT], PRM(l, LNG + c), P7.c(c)[:, 0:NT], MUL, ADD)
                    psg = b.bank()
                    b.mm(psg[:, 0:NT], L2[:, 2048 + c * 128:2048 + (c + 1) * 128], TB(4)[:, 0:NT], True, False)
                    b.mm(psg[:, 0:NT], L2[0:32, 3072 + c * 128:3072 + (c + 1) * 128], TB(5)[0:32, 0:NT], False, True)
                    b.stt(P1.c(c)[:, 0:NT], f3[:, 0:NT], PRM(l, LNB + c), psg[:, 0:NT], ADD, MUL)
                def proj512(wv, c4, ps):
                    for kc in range(8):
                        b.mm(g.cmp(ps), wv[:, kc, c4 * 128:(c4 + 1) * 128], g.cur(XN.c(kc)), kc == 0, kc == 7)
                gM = b.bank(True); gE = b.bank(True)
                gvw = [None]
                def gv_step(c):
                    if c % 4 == 0: gvw[0] = ws.get(1 if c == 0 else 0)[:, 0:4096].re("p (k n) -> p k n", k=8)
                    ps = b.bank(); proj512(gvw[0], c % 4, ps)
                    b.act(AR(c, 1)[:, 0:NT], ps[:, 0:NT], AF.Gelu)
                    gg2 = TB(c % 2)
                    b.tt("pool", gg2[:, 0:NT], AR(c, 1)[:, 0:NT], AR(c, 1)[:, 0:NT], MUL)
                    b.mm(gM[:, 0:NT], ones, AR(c, 1)[:, 0:NT], c == 0, c == 7)
                    b.mm(gE[:, 0:NT], ones, gg2[:, 0:NT], c == 0, c == 7)
                f_a(0)
                for c in range(8):
                    gv_step(c)
                    if c + 1 < 8: f_a(c + 1)
                    f_b(c)
                if not (sample and l == 1): load_l1(1 - l)
                b.s.marks.append(("Phase G t%d l%d" % (ti + 4 * sample, l), b.s.total, len(b.s.ops["pe"])))
                if ti == 0 and l == 0 and not sample: drain(8)
                b.act(TF(0)[:, 0:NT], gM[:, 0:NT], AF.Square, scale=1.0 / D)
                b.stt(TF(1)[:, 0:NT], gE[:, 0:NT], 1.0 / D, TF(0)[:, 0:NT], MUL, SUB)
                b.ts("dve", TF(1)[:, 0:NT], TF(1)[:, 0:NT], 0.0, None, MAX)
                b.act(TF(1)[:, 0:NT], TF(1)[:, 0:NT], AF.Ln, bias=epsv)
                b.act(TF(0)[:, 0:NT], TF(1)[:, 0:NT], AF.Exp, scale=-0.5)
                b.stt(TF(2)[:, 0:NT], gM[:, 0:NT], 1.0 / D, TF(0)[:, 0:NT], MUL, MUL)
                b.release(gM, gE)
                for c in range(8):
                    if c % 4 == 0: wv = ws.get()[:, 0:4096].re("p (k n) -> p k n", k=8)
                    ps = b.bank(); proj512(wv, c % 4, ps)
                    b.act(AR(c, 0)[:, 0:NT], ps[:, 0:NT], AF.Gelu)
                for (arr_, nm) in ((P5, "ga"), (P6, "gb")):
                    for c in range(8):
                        if c % 4 == 0: wv = ws.get()[:, 0:4096].re("p (k n) -> p k n", k=8)
                        ps = b.bank(); proj512(wv, c % 4, ps)
                        b.act(arr_.c(c)[:, 0:NT], ps[:, 0:NT], AF.Sigmoid)
                for c in range(8):
                    t = TF(3 + c % 2)
                    b.tt("dve", t[:, 0:NT], AR(c, 1)[:, 0:NT], TF(0)[:, 0:NT], MUL)
                    b.tt("pool", t[:, 0:NT], t[:, 0:NT], TF(2)[:, 0:NT], SUB)
                    if sample:
                        b.act(t[:, 0:NT], t[:, 0:NT], AF.Identity, bias=PRM(l, VNB + c), scale=PRM(l, VNG + c))
                        b.cp("dve", P4.c(c)[:, 0:NT], t[:, 0:NT])
                        b.dma("sp", V(cvs_o[l, :, c, :], []), t[:, 0:NT])
                    else:
                        b.act(P4.c(c)[:, 0:NT], t[:, 0:NT], AF.Identity, bias=PRM(l, VNB + c), scale=PRM(l, VNG + c))
                for c in range(8):
                    vnt = TB2(0) if c % 2 == 0 else TB2(2)
                    psS = b.bank()
                    for n0 in range(0, NCH, 8):
                        nn = min(8, NCH - n0)
                        psb = b.bank().cast(BF16)
                        for n in range(n0, n0 + nn):
                            b.tr(psb[0:T, (n - n0) * 128:(n - n0 + 1) * 128], P4.c(c)[:, n * T:(n + 1) * T], ident)
                        b.cp("act", vnt[0:T, 0:nn * 128], psb[0:T, 0:nn * 128])
                        for n in range(n0, n0 + nn):
                            o = psS[:, n * T:(n + 1) * T]
                            b.mm(o, vnt[0:T, (n - n0) * 128:(n - n0 + 1) * 128], V(WSMt[0:T, l, c, 0:T], [("WSM", l)]), True, False)
                            b.mm(o, ones[0:1, :], V(BSt[0:1, l, c, 0:T], [("BSR", l)]), False, True)
                    b.tt("dve", P7.c(c)[:, 0:NT], psS[:, 0:NT], AR(c, 0)[:, 0:NT], MUL)
                    b.tt("pool", P7.c(c)[:, 0:NT], P7.c(c)[:, 0:NT], P6.c(c)[:, 0:NT], MUL)
                    b.tt("dve", P1.c(c)[:, 0:NT], P1.c(c)[:, 0:NT], P5.c(c)[:, 0:NT], MUL)
                    b.tt("pool", P1.c(c)[:, 0:NT], P1.c(c)[:, 0:NT], P7.c(c)[:, 0:NT], ADD)

                def dense_out(nblk_w, src_fn, nk, kdiv):
                    for c2 in range(8):
                        if kdiv == 8:
                            if c2 % 4 == 0: wvv = ws.get()[:, 0:4096].re("p (k n) -> p k n", k=8)
                            lw = lambda kc: wvv[:, kc, (c2 % 4) * 128:(c2 % 4 + 1) * 128]
                        else:
                            wvv = ws.get()[:, 0:4096].re("p (k n) -> p k n", k=32)
                            lw = lambda kc: wvv[:, kc, :]
                        ps = b.bank()
                        for kc in range(nk):
                            b.mm(ps[:, 0:NT], lw(kc), src_fn(kc), kc == 0, kc == nk - 1)
                        b.cp("act", F1.c(c2)[:, 0:NT], ps[:, 0:NT])

                def add_normed(gcol):
                    rs_ = rms_stats(g, F1)
                    for c in range(8):
                        t = TF(3 + c % 2)
                        b.stt(t[:, 0:NT], F1.c(c)[:, 0:NT], PRM(l, gcol + c), rs_[:, 0:NT], MUL, MUL)
                        b.tt("pool", H.c(c)[:, 0:NT], H.c(c)[:, 0:NT], t[:, 0:NT], ADD)

                dense_out(2, lambda kc: P1.c(kc)[:, 0:NT], 8, 8)
                add_normed(G_POST)

                b.s.marks.append(("Phase H t%d l%d" % (ti + 4 * sample, l), b.s.total, len(b.s.ops["pe"])))
                if ti == 0 and l == 0 and not sample: drain(8)
                rstd = rms_stats(g, H)
                for c in range(8):
                    b.stt(XN.c(c)[:, 0:NT], H.c(c)[:, 0:NT], PRM(l, F_PRE + c), rstd[:, 0:NT], MUL, MUL)
                ZA = (P4, P5, P6, P7)
                for kb in range(32):
                    if kb % 4 == 0: wv = ws.get()[:, 0:4096].re("p (k n) -> p k n", k=8)
                    ps = b.bank()
                    for kc in range(8):
                        b.mm(ps[:, 0:NT], wv[:, kc, (kb % 4) * 128:(kb % 4 + 1) * 128], XN.c(kc)[:, 0:NT], kc == 0, kc == 7)
                    rl = TB(kb % 4)
                    b.act(rl[:, 0:NT], ps[:, 0:NT], AF.Relu)
                    b.tt("pool", ZA[kb // 8].c(kb % 8)[:, 0:NT], rl[:, 0:NT], rl[:, 0:NT], MUL)
                dense_out(8, lambda kb: ZA[kb // 8].c(kb % 8)[:, 0:NT], 32, 32)
                add_normed(F_POST)

                b.s.marks.append(("Phase I t%d l%d" % (ti + 4 * sample, l), b.s.total, len(b.s.ops["pe"])))
                if ti == 0 and l == 0 and not sample: drain(8)
                for c in range(8):
                    b.cp("pool", XN.c(c)[:, 0:NT], H.c(c)[:, 0:NT])
                for kc in range(2):
                    b.dma("sp", TF(3 + kc)[:, 0:NT], V(pT[l, :, kc, col0:col0 + NT], []))
                    b.cp("pool", TB(8 + kc)[:, 0:NT], TF(3 + kc)[:, 0:NT])
                for c2 in range(8):
                    if c2 % 4 == 0:
                        wpg = ws.get(1)[:, 0:4096].re("p (k n) -> p k n", k=8)
                        wpe = ws.get(1)[:, 0:1024].re("p (k n) -> p k n", k=2)
                    ps1 = b.bank(); ps2 = b.bank()
                    for kc in range(8):
                        b.mm(ps1[:, 0:NT], wpg[:, kc, (c2 % 4) * 128:(c2 % 4 + 1) * 128], XN.c(kc)[:, 0:NT], kc == 0, kc == 7)
                    for kc in range(2):
                        b.mm(ps2[:, 0:NT], wpe[:, kc, (c2 % 4) * 128:(c2 % 4 + 1) * 128], TB(8 + kc)[:, 0:NT], kc == 0, kc == 1)
                    sg = TF(c2 % 2)
                    b.act(sg[:, 0:NT], ps1[:, 0:NT], AF.Sigmoid)
                    b.tt("dve", sg[:, 0:NT], ps2[:, 0:NT], sg[:, 0:NT], MUL)
                    b.tt("pool", H.c(c2)[:, 0:NT], H.c(c2)[:, 0:NT], sg[:, 0:NT], ADD)
                drain(1000)
            b.dma("sp", V(yT[:, :, col0:col0 + NT], []), V(Ht[:, :, 0:NT], H.all().keys))

        b.s.marks.append(("end", b.s.total, len(b.s.ops["pe"])))
        import os
        if os.environ.get("KMARKS"):
            for m in b.s.marks: print("MARK", m)
            lo, hi = [int(x) for x in os.environ.get("KDUMP", "0,0").split(",")]
            for e_ in b.s.log:
                if lo <= e_[0] <= hi: print("OP", e_)
            print("OPS", {e: len(b.s.ops[e]) for e in ENGS}, "waits", {e: sum(len(o.waits) for o in b.s.ops[e]) for e in ENGS})
        sch = b.s
        esem = {e: es.enter_context(nc.semaphore(f"sem_{e}")) for e in ENGS}
        dsem = [es.enter_context(nc.semaphore(f"dsem{i}")) for i in range(sch.ndma)]
        for e in ENGS:
            cnt = 0
            for op in sch.ops[e]:
                if op.sig and not op.dma:
                    cnt += 1
                op.cnt = cnt
        final = {}
        for e in ENGS:
            for op in sch.ops[e]:
                if op.dma: final[op.dsem] = max(final.get(op.dsem, 0), op.dval)

        def run(ename, e):
            for op in sch.ops[ename]:
                for d in op.waits:
                    if d.dma: e.wait_ge(dsem[d.dsem], d.dval)
                    else: e.wait_ge(esem[d.eng], d.cnt)
                ins = op.fn(e)
                if op.dma: ins.then_inc(dsem[op.dsem], 16)
                elif op.sig: ins.then_inc(esem[ename], 1)
            if ename == "sp":
                for i, v in final.items():
                    e.wait_ge(dsem[i], v)

        with nc.Block() as block:
            @block.tensor
            def _(e): run("pe", e)
            @block.scalar
            def _(e): run("act", e)
            @block.vector
            def _(e): run("dve", e)
            @block.gpsimd
            def _(e): run("pool", e)
            @block.sync
            def _(e): run("sp", e)
    return nc


def _fm(a):
    n, d = a.shape
    return np.ascontiguousarray(a.T.reshape(d // 128, 128, n).transpose(1, 0, 2))


def _consts():
    c = np.zeros((128, CW), np.float32)
    i = np.arange(128)
    c[:, C_ID:C_ID + 128] = np.eye(128)
    c[:, C_ONES:C_ONES + 128] = 1.0
    blk = (i[:, None] // 64 == i[None, :] // 64).astype(np.float32)
    c[:, C_BONES:C_BONES + 128] = blk
    c[:, C_BMEAN:C_BMEAN + 128] = blk / 64.0
    su = (i[None, :] > i[:, None]).astype(np.float32)
    iu = (i[None, :] >= i[:, None]).astype(np.float32)
    c[:, C_M4:C_M4 + 512] = np.concatenate([su, iu, su, iu], 1)
    sl = (i[None, :] < i[:, None]).astype(np.float32)
    c[:, C_ML4:C_ML4 + 512] = np.concatenate([sl] * 4, 1)
    c[:, C_ID4:C_ID4 + 512] = np.concatenate([np.eye(128, dtype=np.float32)] * 4, 1)
    rp = np.ones(512, np.float32); rp[::128] = 0.0
    rs = np.ones(64, np.float32); rs[::4] = 0.0
    c[:, C_RSP:C_RSP + 512] = rp[None]; c[:, C_RSS:C_RSS + 64] = rs[None]
    c[:, C_SH:C_SH + 64] = (i[:, None] == (np.arange(64)[None, :] + 64)).astype(np.float32)
    return c


_NC_CACHE = {}


def kernel(x_prompt, x_sample, state_wkv, state_shift, p_prompt, p_sample,
           mix_pre_g, mix_post_g, ffn_pre_g, ffn_post_g, w_in, mu_rkv, mu_wag,
           w0, w1, w2, a0, a1, a2, g1, g2, k_k, k_a, r_k, lnx_g, lnx_b,
           vn_g, vn_b, w_s, b_s, w_out, w_up, w_down, w_pe, w_pg):
    f = lambda a: np.ascontiguousarray(np.asarray(a, dtype=np.float32))
    x_prompt, x_sample, state_wkv, state_shift, p_prompt, p_sample = map(f, (x_prompt, x_sample, state_wkv, state_shift, p_prompt, p_sample))
    prm = np.zeros((128, 2, NPC), np.float32)
    def col(v): return np.asarray(v, np.float32).reshape(-1, 128).T
    for l in range(2):
        mr = np.asarray(mu_rkv[l], np.float32)
        items = [(G_PRE, mix_pre_g[l]), (G_POST, mix_post_g[l]), (F_PRE, ffn_pre_g[l]), (F_POST, ffn_post_g[l]),
                 (MU_R, mr[0:1024]), (MU_K, mr[1024:2048]), (MU_V, mr[2048:3072]),
                 (MU_W, mu_wag[l][0]), (MU_A, mu_wag[l][1]), (MU_G, mu_wag[l][2]),
                 (W0, w0[l]), (A0, a0[l]), (KK, k_k[l]), (KA, k_a[l]), (RK, np.asarray(r_k[l]).reshape(-1)),
                 (LNG, lnx_g[l]), (LNB, lnx_b[l]), (VNG, vn_g[l]), (VNB, vn_b[l])]
        for o, v in items:
            prm[:, l, o:o + 8] = col(v)
    cst = _consts()
    wsT = np.ascontiguousarray(np.asarray(w_s, np.float32).transpose(0, 3, 1, 2))
    bs = np.ascontiguousarray(np.asarray(b_s, np.float32).reshape(2, 1, 8, 128))
    shared = dict(prm=prm, cst=cst, wsT=wsT, bs=bs, w_in=f(w_in), w_out=f(w_out), w_up=f(w_up), w_down=f(w_down),
                  w_pe=f(w_pe), w_pg=f(w_pg), w1=f(w1), w2=f(w2), a1=f(a1), a2=f(a2), g1=f(g1), g2=f(g2))
    in_maps = []
    for i in range(NCORE):
        xs = x_sample[NSQ * i:NSQ * (i + 1)].reshape(NSQ * DEC, D)
        xT = np.concatenate([_fm(x_prompt[i]), _fm(xs)], axis=2)
        pTl = []
        for l in range(2):
            ps_ = p_sample[l, NSQ * i:NSQ * (i + 1)].reshape(NSQ * DEC, 256)
            pTl.append(np.concatenate([_fm(p_prompt[l, i]), _fm(ps_)], axis=2))
        pT = np.stack(pTl)
        shT = np.stack([_fm(state_shift[l, NSQ * i:NSQ * (i + 1)]) for l in range(2)])
        sw = state_wkv[:, NSQ * i:NSQ * (i + 1)].reshape(2, NSQ, 8, 2, 64, 64)
        wkvT = np.ascontiguousarray(sw.transpose(0, 3, 5, 1, 2, 4)).reshape(2, 128, NSQ, 8, 64)
        m = dict(shared); m.update(xT=np.ascontiguousarray(xT), pT=np.ascontiguousarray(pT), shT=np.ascontiguousarray(shT), wkvT=wkvT)
        in_maps.append(m)
    if "nc" not in _NC_CACHE:
        _NC_CACHE["nc"] = build_nc()
    res = run_bass_kernel_spmd(_NC_CACHE["nc"], in_maps, core_ids=list(range(NCORE)))
    R = list(res.results)
    def unfm(a):
        return np.ascontiguousarray(a.transpose(2, 1, 0).reshape(a.shape[2], -1))
    y_prompt = np.stack([unfm(R[i]["yT"][:, :, 0:SEQ]) for i in range(NCORE)])
    y_sample = np.concatenate([unfm(R[i]["yT"][:, :, SEQ:]).reshape(NSQ, DEC, D) for i in range(NCORE)])
    wkv_p = np.stack([np.stack([R[i]["wkvp"][l].reshape(2, 64, 8, 64).transpose(2, 0, 3, 1).reshape(16, 64, 64) for i in range(NCORE)]) for l in range(2)])
    sh_p = np.stack([np.stack([R[i]["shp"][l].T.reshape(D) for i in range(NCORE)]) for l in range(2)])
    wkv_s = np.stack([np.concatenate([R[i]["wkvs"][l].reshape(2, 64, NSQ, 8, 64).transpose(2, 3, 0, 4, 1).reshape(NSQ, 16, 64, 64) for i in range(NCORE)]) for l in range(2)])
    sh_s = np.stack([np.concatenate([R[i]["shs"][l].transpose(2, 1, 0).reshape(NSQ, D) for i in range(NCORE)]) for l in range(2)])
    cv_s = np.stack([np.concatenate([R[i]["cvs"][l].reshape(128, 8, NSQ, DEC).transpose(2, 3, 1, 0).reshape(NSQ, DEC, D) for i in range(NCORE)]) for l in range(2)])
    o = lambda a: np.ascontiguousarray(a, dtype=np.float32)
    return (o(y_prompt), o(y_sample), o(wkv_p), o(sh_p), o(wkv_s), o(sh_s), o(cv_s))
```

```python
import numpy as np
from contextlib import ExitStack
import concourse.bass as bass
import concourse.mybir as mybir
from concourse.bass_utils import run_bass_kernel_spmd

F32, BF16 = mybir.dt.float32, mybir.dt.bfloat16
AF = mybir.ActivationFunctionType
ALU = mybir.AluOpType
MUL, ADD, SUB, MAX = ALU.mult, ALU.add, ALU.subtract, ALU.max

D = 1024; NCORE = 8; SEQ = 2048; NSQ = 16; DEC = 4; NTOK = SEQ + NSQ * DEC
W = 516
LAM = 0.6065306597126334
EPS = 1e-6; GN_EPS = 64e-5
NBLK = 38; SLOTW = 4608
G_PRE, G_POST, F_PRE, F_POST, MU_R, MU_K, MU_V, MU_W, MU_A, MU_G = 0, 8, 16, 24, 32, 40, 48, 56, 64, 72
W0, A0, KK, KA, RK, LNG, LNB, VNG, VNB, OMM_R, OMM_K, OMM_V, OMKA, NPC = 80, 88, 96, 104, 112, 120, 128, 136, 144, 152, 160, 168, 176, 184
C_ID, C_ONES, C_BONES, C_BMEAN, C_M4, C_ML4, C_ID4, C_RSP, C_RSS, C_SH, CW = 0, 128, 256, 384, 512, 1024, 1536, 2048, 2560, 2624, 2688


class V:
    __slots__ = ("ap", "keys")

    def __init__(s, ap, keys):
        s.ap = ap; s.keys = tuple(keys)

    def __getitem__(s, i):
        return V(s.ap[i], s.keys)

    def re(s, pat, **kw):
        return V(s.ap.rearrange(pat, **kw), s.keys)

    def bc(s, shape):
        return V(s.ap.to_broadcast(list(shape)), s.keys)

    def cast(s, dt):
        return V(s.ap.bitcast(dt), s.keys)


class Op:
    __slots__ = ("eng", "fn", "idx", "sig", "waits", "dma", "dsem", "dval", "cnt")


ENGS = ("pe", "act", "dve", "pool", "sp")


class Sched:
    def __init__(s, ndma=24):
        s.ops = {e: [] for e in ENGS}
        s.last_w = {}; s.readers = {}
        s.known = {e: {} for e in ENGS}
        s.ndma = ndma; s.dma_i = 0; s.dma_last = [None] * ndma
        import os
        s.limit = int(float(os.environ.get("KLIMIT", "1e12"))); s.marks = []

    def add(s, eng, fn, r, w, dma=False, cost=256):
        s.total = getattr(s, "total", 0) + 1
        if s.total > s.limit: return None
        op = Op(); op.eng = eng; op.fn = fn; op.idx = len(s.ops[eng]); op.sig = False
        op.waits = []; op.dma = dma; op.cnt = 0; op.dsem = -1; op.dval = 0
        deps = []
        for k in r:
            d = s.last_w.get(k)
            if d is not None: deps.append(d)
        for k in w:
            d = s.last_w.get(k)
            if d is not None: deps.append(d)
            deps.extend(s.readers.get(k, ()))
        if dma:
            ph = s.__dict__.setdefault("hist_" + eng, [])
            acc = cost
            lim = 800 if eng == "pool" else 400
            for (pop, pc) in reversed(ph[-16:]):
                acc += pc
                if acc > lim:
                    deps.append(pop); break
        if dma:
            cnts = s.__dict__.setdefault("dma_cnt", {"pool": 0, "sp": 0})
            base, n = (0, 8) if eng == "pool" else (8, s.ndma - 8)
            i = cnts[eng]; cnts[eng] += 1
            slot = base + i % n; op.dsem = slot; op.dval = 16 * (i // n + 1)
            if s.dma_last[slot] is not None: deps.append(s.dma_last[slot])
            s.dma_last[slot] = op
        kn = s.known[eng]
        best = {}
        for d in deps:
            key, val = (("d", d.dsem), d.dval) if d.dma else (d.eng, d.idx)
            if key not in best or val > best[key][0]: best[key] = (val, d)
        deps = [v[1] for v in best.values()]
        for d in deps:
            if d.dma:
                key = ("d", d.dsem)
                if kn.get(key, 0) >= d.dval: continue
                kn[key] = d.dval; op.waits.append(d)
            else:
                if d.eng == "pe" and eng == "pe": continue
                if kn.get(d.eng, -1) >= d.idx: continue
                kn[d.eng] = d.idx; d.sig = True; op.waits.append(d)
        for k in r: s.readers.setdefault(k, []).append(op)
        for k in w:
            s.last_w[k] = op; s.readers[k] = []
        s.ops[eng].append(op)
        import sys as _sys
        s.__dict__.setdefault("log", []).append((s.total, eng, _sys._getframe(1).f_code.co_name, len(op.waits), tuple(w)[:2]))
        if dma: s.__dict__["hist_" + eng].append((op, cost))
        return op


class Geo:
    def __init__(s, sample):
        s.sample = sample
        if sample: s.NT, s.T, s.NCH, s.NE, s.M = 64, 4, 16, 80, 2
        else: s.NT, s.T, s.NCH, s.NE, s.M = 512, 128, 4, 513, 7
        s.CE = s.NCH * (s.T + 1)

    def cur(s, v):
        if s.sample: return v[:, 0:80].re("p (g t) -> p g t", t=5)[:, :, 1:5]
        return v[:, 1:513]

    def prev(s, v):
        if s.sample: return v[:, 0:80].re("p (g t) -> p g t", t=5)[:, :, 0:4]
        return v[:, 0:512]

    def cmp(s, v):
        if s.sample: return v[:, 0:64].re("p (g t) -> p g t", t=4)
        return v[:, 0:512]

    def c3(s, v):
        return v[:, 0:s.NT].re("p (n t) -> p n t", t=s.T)

    def ccur(s, v):
        return v[:, 0:s.CE].re("p (n t) -> p n t", t=s.T + 1)[:, :, 1:s.T + 1]

    def cprev(s, v):
        return v[:, 0:s.CE].re("p (n t) -> p n t", t=s.T + 1)[:, :, 0:s.T]


class Builder:
    def __init__(b, nc):
        b.nc = nc; b.s = Sched(); b.bank_i = 0; b.reserved = set()

    def _k(b, vs):
        ks = []
        for v in vs:
            if v is None or isinstance(v, (int, float)): continue
            ks.extend(v.keys)
        return ks

    def _a(b, v):
        return v.ap if isinstance(v, V) else v

    def mm(b, out, lhsT, rhs, start=True, stop=True):
        o, l, r = out.ap, lhsT.ap, rhs.ap
        b.s.add("pe", lambda e: e.matmul(o, lhsT=l, rhs=r, start=start, stop=stop), b._k([lhsT, rhs]), b._k([out]))

    def tr(b, out, in_, ident):
        o, i, d = out.ap, in_.ap, ident.ap
        b.s.add("pe", lambda e: e.transpose(o, i, d), b._k([in_, ident]), b._k([out]))

    def act(b, out, in_, func, bias=0.0, scale=1.0):
        o, i, bi, sc = out.ap, in_.ap, b._a(bias), b._a(scale)
        b.s.add("act", lambda e: e.activation(out=o, in_=i, func=func, bias=bi, scale=sc), b._k([in_, bias, scale]), b._k([out]))

    def tt(b, eng, out, in0, in1, op):
        o, x, y = out.ap, in0.ap, in1.ap
        b.s.add(eng, lambda e: e.tensor_tensor(out=o, in0=x, in1=y, op=op), b._k([in0, in1]), b._k([out]))

    def ts(b, eng, out, in0, s1, s2=None, op0=MUL, op1=None):
        o, x, a1, a2 = out.ap, in0.ap, b._a(s1), b._a(s2)
        if op1 is None:
            fn = lambda e: e.tensor_scalar(out=o, in0=x, scalar1=a1, scalar2=None, op0=op0)
        else:
            fn = lambda e: e.tensor_scalar(out=o, in0=x, scalar1=a1, scalar2=a2, op0=op0, op1=op1)
        b.s.add(eng, fn, b._k([in0, s1, s2]), b._k([out]))

    def stt(b, out, in0, sc, in1, op0, op1):
        o, x, a, y = out.ap, in0.ap, b._a(sc), in1.ap
        b.s.add("dve", lambda e: e.scalar_tensor_tensor(out=o, in0=x, scalar=a, in1=y, op0=op0, op1=op1), b._k([in0, sc, in1]), b._k([out]))

    def cp(b, eng, out, in_):
        o, i = out.ap, in_.ap
        if eng == "act":
            b.s.add("act", lambda e: e.activation(out=o, in_=i, func=AF.Copy), b._k([in_]), b._k([out]))
        else:
            b.s.add(eng, lambda e: e.tensor_copy(out=o, in_=i), b._k([in_]), b._k([out]))

    def recip(b, out, in_):
        o, i = out.ap, in_.ap
        b.s.add("dve", lambda e: e.reciprocal(out=o, in_=i), b._k([in_]), b._k([out]))

    def memset(b, eng, out, val):
        o = out.ap
        b.s.add(eng, lambda e: e.memset(o, val), [], b._k([out]))

    def scan(b, out, d0, d1):
        o, x, y = out.ap, d0.ap, d1.ap
        b.s.add("dve", lambda e: e.tensor_tensor_scan(out=o, data0=x, data1=y, initial=0.0, op0=MUL, op1=ADD), b._k([d0, d1]), b._k([out]))

    def dma(b, eng, out, in_, cost=None):
        if cost is None: cost = 256 if eng == "pool" else 16
        o, i = out.ap, in_.ap
        b.s.add(eng, lambda e: e.dma_start(out=o, in_=i), b._k([in_]), b._k([out]), dma=True, cost=cost)

    def bank(b, reserve=False):
        while True:
            i = b.bank_i % 8; b.bank_i += 1
            if i not in b.reserved: break
        if reserve: b.reserved.add(i)
        return V(b.ps[i][:], [("ps", i)])

    def release(b, *vs):
        for v in vs: b.reserved.discard(v.keys[0][1])


def build_nc():
    nc = bass.Bass("TRN2", target_bir_lowering=False)
    b = Builder(nc)

    def din(name, shape, dt=F32):
        return nc.dram_tensor(name, list(shape), dt, kind="ExternalInput").ap()

    def dout(name, shape, dt=F32):
        return nc.dram_tensor(name, list(shape), dt, kind="ExternalOutput").ap()

    xT = din("xT", [128, 8, NTOK]); pT = din("pT", [2, 128, 2, NTOK])
    shT = din("shT", [2, 128, 8, NSQ]); wkvT = din("wkvT", [2, 128, NSQ, 8, 64])
    prm_d = din("prm", [128, 2, NPC]); cst_d = din("cst", [128, CW])
    wsT_d = din("wsT", [2, 128, 8, 128]); bs_d = din("bs", [2, 1, 8, 128])
    w_in = din("w_in", [2, D, 7 * D]); w_out = din("w_out", [2, D, D]); w_up = din("w_up", [2, D, 4 * D])
    w_down = din("w_down", [2, 4 * D, D]); w_pe = din("w_pe", [2, 256, D]); w_pg = din("w_pg", [2, D, D])
    w1 = din("w1", [2, D, 64]); w2 = din("w2", [2, 64, D]); a1 = din("a1", [2, D, 64]); a2 = din("a2", [2, 64, D])
    g1 = din("g1", [2, D, 160]); g2 = din("g2", [2, 160, D])
    yT = dout("yT", [128, 8, NTOK]); wkvp_o = dout("wkvp", [2, 128, 8, 64]); shp_o = dout("shp", [2, 128, 8])
    wkvs_o = dout("wkvs", [2, 128, NSQ, 8, 64]); shs_o = dout("shs", [2, 128, 8, NSQ]); cvs_o = dout("cvs", [2, 128, 8, NSQ * DEC])
    scr = nc.dram_tensor("scr", [2, 2, 128, SLOTW], BF16, kind="Internal").ap()
    nW = {nm: nc.dram_tensor("n_" + nm, list(shp), BF16, kind="Internal").ap() for nm, shp in
          (("w_in", (2, D, 7 * D)), ("w_out", (2, D, D)), ("w_up", (2, D, 4 * D)), ("w_down", (2, 4 * D, D)), ("w_pg", (2, D, D)), ("w_pe", (2, 256, D)))}
    scrD = nc.dram_tensor("scrD", [2, 8, 128, 4096], BF16, kind="Internal").ap()
    srcW = {"w_in": w_in, "w_out": w_out, "w_up": w_up, "w_down": w_down, "w_pg": w_pg, "w_pe": w_pe}

    es = ExitStack()
    with es:
        def sb(name, shape, dt):
            return es.enter_context(nc.sbuf_tensor(name, list(shape), dt))

        Ht = sb("H", [128, 8, W], F32); F1t = sb("F1", [128, 8, W], F32)
        XNt = sb("XN", [128, 8, W], BF16); P1t = sb("P1", [128, 8, W], BF16)
        P4t = sb("P4", [128, 8, W], BF16); P5t = sb("P5", [128, 8, W], BF16)
        P6t = sb("P6", [128, 8, W], BF16); P7t = sb("P7", [128, 8, W], BF16)
        ARt = sb("AR", [128, 8, 2, W], BF16)
        TFt = sb("TF", [128, 5, W], F32); TBt = sb("TB", [128, 14, W], BF16)
        M4t = sb("M4", [128, 16, 4, 128], BF16); TTt = sb("TTm", [128, 16, 128], BF16)
        XBt = sb("XB", [128, 2, 4, 4, 128], BF16)
        NSLOT = 3
        WSt = [sb(f"WS{i}", [128, 4096], BF16) for i in range(NSLOT)]
        CBt = sb("CB", [128, CW], BF16); PRt = sb("PRM", [128, 2, NPC], F32)
        RKt = sb("RKO", [128, 2, 8, 128], BF16); WSMt = sb("WSM", [128, 2, 8, 128], BF16)
        BSt = sb("BSR", [1, 2, 8, 128], BF16)
        St = sb("S", [128, 2, 8, 64], F32); Sbt = sb("Sb", [128, 2, 8, 128], BF16)
        BMt = sb("BM", [128, 8, 128], BF16); KMt = sb("KM", [128, 8, 128], BF16); AMt = sb("AM", [128, 8, 128], BF16)
        IDFt = sb("IDF", [128, 128], F32)
        JNKt = sb("JNK", [128, 4], F32)
        P1flat = P1t[:].rearrange("p a w -> p (a w)")
        M4flat = M4t[:].rearrange("p h m t -> p (h m t)")
        def TBa(i): return V(M4flat[:, i * W:(i + 1) * W], [("TD", i)])
        def TFa(i): return V(M4flat[:, i * W:(i + 2) * W].bitcast(F32), [("TD", i), ("TD", i + 1)])
        XBflat = XBt[:].rearrange("p a k j t -> p (a k j t)")
        def XBa(i): return V(XBflat[:, i * W:(i + 1) * W], [("XD", i)])
        def fence_m4(col):
            b.memset("pool", V(JNKt[:, col:col + 1], [("TD", i) for i in range(15)] + [("M4", h) for h in range(16)]
                               + [("XD", i) for i in range(7)] + [("XB", s_, k_) for s_ in (0, 1) for k_ in range(4)]), 0.0)
        PTt = sb("PT", [128, 8, 16], F32)
        XCt = sb("XC", [128, 2, 8], BF16); RCt = sb("RC", [128, 2, 24], BF16)
        SHt = TFt[:, 3, 0:128].rearrange("p (c n) -> p c n", c=8); SH2t = TFt[:, 4, 0:128].rearrange("p (c n) -> p c n", c=8)
        SHPt = sb("SHP", [128, 8], F32)
        SHIt = TFt[:, 2, 0:128].rearrange("p (c n) -> p c n", c=8)
        b.ps = [es.enter_context(nc.psum_tensor(f"ps{i}", [128, 512], F32)) for i in range(8)]

        class Arr:
            def __init__(s, name, t): s.name = name; s.t = t
            def c(s, i): return V(s.t[:, i, :], [(s.name, i)])
            def all(s): return V(s.t[:], [(s.name, i) for i in range(8)])
            def rng(s, a, e): return V(s.t[:, a:e], [(s.name, i) for i in range(a, e)])

        H, F1, XN, P1, P4, P5, P6, P7 = (Arr(n, t) for n, t in (("H", Ht), ("F1", F1t), ("XN", XNt), ("P1", P1t), ("P4", P4t), ("P5", P5t), ("P6", P6t), ("P7", P7t)))

        def AR(c, j=None):
            if j is None: return V(ARt[:, c], [("AR", c)])
            return V(ARt[:, c, j, :], [("AR", c)])

        def TF(i): return V(TFt[:, i, :], [("TF", i)])
        def TB(i): return V(TBt[:, i, :], [("TB", i)])
        def TB2(i): return V(TBt[:, i:i + 2, :].rearrange("p a w -> p (a w)"), [("TB", i), ("TB", i + 1)])
        def CB(o, n): return V(CBt[:, o:o + n], [("CB",)])
        def PRM(l, col, n=1): return V(PRt[:, l, col:col + n], [("PRM", l)])
        def WS(i): return V(WSt[i][:], [("WS", i)])

        ident = CB(C_ID, 128); ones = CB(C_ONES, 128); bones = CB(C_BONES, 128); bmean = CB(C_BMEAN, 128)
        mask4 = CB(C_M4, 512).re("p (m t) -> p m t", m=4); maskL4 = CB(C_ML4, 512).re("p (m t) -> p m t", m=4)
        ident4 = CB(C_ID4, 512).re("p (m t) -> p m t", m=4)
        shiftI = CB(C_SH, 64)

        b.dma("sp", V(Ht[:, :, 0:512], [("H", i) for i in range(8)]), V(xT[:, :, 0:512], []))
        b.dma("pool", CB(0, CW), V(cst_d, []))
        b.dma("sp", V(PRt[:], [("PRM", 0), ("PRM", 1)]), V(prm_d, []))
        b.dma("sp", V(IDFt[:], [("IDF",)]), V(cst_d[:, C_ID:C_ID + 128], []))
        for l in range(2):
            b.ts("pool", PRM(l, OMM_R, 24), PRM(l, MU_R, 24), -1.0, 1.0, MUL, ADD)
            b.ts("pool", PRM(l, OMKA, 8), PRM(l, KA, 8), -1.0, 1.0, MUL, ADD)
        b.memset("pool", V(St[:], [("S", 0), ("S", 1)]), 0.0)
        b.memset("pool", V(Sbt[:], [("Sb", 0), ("Sb", 1)]), 0.0)
        for (t_, k_) in ((BMt, "BM"), (KMt, "KM"), (AMt, "AM")):
            b.memset("pool", V(t_[:], [(k_,)]), 0.0)
        b.memset("pool", V(XCt[:], [("XC", 0), ("XC", 1)]), 0.0)
        b.memset("pool", V(RCt[:], [("RC", 0), ("RC", 1)]), 0.0)

        def SCR(l, k): return V(scr[l, k], [("scr", l, k)])

        conv = [[], []]
        for l in (1, 0):
            stg = V(F1t[:].rearrange("p a w -> p (a w)"), [("F1", i) for i in range(8)])
            w1s = stg[:, 0:512].re("p (k n) -> p k n", k=8)
            a1s = stg[:, 512:1024].re("p (k n) -> p k n", k=8)
            g1s = stg[:, 1024:2304].re("p (k n) -> p k n", k=8)
            wss = stg[:, 2304:3328].re("p (g t) -> p g t", g=8)
            bss = V(TFt[:].rearrange("p a w -> p (a w)"), [("TF", i) for i in range(5)])[0:1, 0:1024].re("p (g t) -> p g t", g=8)
            b.dma("sp", w1s, V(w1[l].rearrange("(k p) n -> p k n", p=128), []))
            b.dma("sp", a1s, V(a1[l].rearrange("(k p) n -> p k n", p=128), []))
            b.dma("sp", g1s, V(g1[l].rearrange("(k p) n -> p k n", p=128), []))
            b.dma("sp", wss, V(wsT_d[l], []))
            b.dma("sp", bss, V(bs_d[l], []))
            LBflat = V(M4flat[:, 0:4608], [("TD", i) for i in range(9)])
            LB = LBflat.re("p (k n) -> p k n", k=8)
            for (src, mu, o, n) in ((w1s, MU_W, 0, 64), (a1s, MU_A, 128, 64), (g1s, MU_G, 256, 160)):
                b.cp("dve", LB[:, :, o:o + n], src)
                muv = V(PRt[:, l, mu:mu + 8].unsqueeze(2).to_broadcast([128, 8, n]), [("PRM", l)])
                b.tt("dve", LB[:, :, o + n:o + 2 * n], src, muv, MUL)
            b.dma("sp", V(scr[l, 0, :, 0:4608], [("scr", l, 0)]), LBflat)
            b.tt("dve", V(WSMt[:, l], [("WSM", l)]), wss, V(CBt[:, C_M4 + 128:C_M4 + 256].unsqueeze(1).to_broadcast([128, 8, 128]), [("CB",)]), MUL)
            b.cp("dve", V(BSt[0:1, l], [("BSR", l)]), bss)
            for c in range(8):
                b.ts("dve", V(RKt[:, l, c, :], [("RKO", l)]), bones, PRM(l, RK + c), None, MUL)
            cl = conv[l]
            def cv(dst, key, src, cost, cl=cl):
                cl.append((lambda: b.dma("pool", V(dst, [key]), V(src, []), cost=cost)))
            cv(scr[l, 1, 0:64, 0:1024], ("scr", l, 1), w2[l], 8)
            cv(scr[l, 1, 0:64, 1024:2048], ("scr", l, 1), a2[l], 8)
            cv(scr[l, 1, 0:128, 2048:3072], ("scr", l, 1), g2[l, 0:128], 8)
            cv(scr[l, 1, 0:32, 3072:4096], ("scr", l, 1), g2[l, 128:160], 8)
            def natcv(nm):
                rows = srcW[nm].shape[1]
                npart = 4 if rows >= 1024 else 1
                rp = rows // npart
                for i in range(npart):
                    cv(nW[nm][l, i * rp:(i + 1) * rp, :], ("nw", l, nm), srcW[nm][l, i * rp:(i + 1) * rp, :], 40)
            natcv("w_in"); natcv("w_out"); natcv("w_up")
            wdv = w_down[l].rearrange("(k p) n -> p k n", p=128)
            for j in range(8):
                cv(scrD[l, j].rearrange("p (k n) -> p k n", k=32), ("scrD", l, j), wdv[:, :, 128 * j:128 * (j + 1)], 258)
            natcv("w_pg"); natcv("w_pe")

        convq = conv[0] + conv[1]
        def drain(n):
            for _ in range(min(n, len(convq))):
                convq.pop(0)()
        drain(8)

        class WStream:
            def __init__(s): s.seq = []; s.issued = 0; s.used = 0
            def plan(s, items): s.seq.extend(items)
            def _issue(s):
                l, k = s.seq[s.issued]
                sl = s.issued % NSLOT
                wk = [("WS", sl)]
                def nat(nm): return nW[nm][l].rearrange("(k p) n -> p k n", p=128), [("nw", l, nm)]
                if k == 1:
                    for (np_, c0, c1) in [(64, 0, 2048), (128, 2048, 3072), (32, 3072, 4096)]:
                        b.dma("sp", V(WSt[sl][0:np_, c0:c1], wk), V(scr[l, 1, 0:np_, c0:c1], [("scr", l, 1)]))
                elif k < 8:
                    src, sk = nat("w_in")
                    dst = WSt[sl][:, 0:4096].rearrange("p (k n) -> p k n", k=8)
                    for q in range(4):
                        fb = 4 * (k - 2) + q; c, j = fb // 3, fb % 3
                        b.dma("sp", V(dst[:, :, q * 128:(q + 1) * 128], wk), V(src[:, :, j * 1024 + c * 128:j * 1024 + (c + 1) * 128], sk), cost=64)
                elif k < 26 or k in (34, 35):
                    nm, j, off = ("w_in", k - 8, 3072) if k < 16 else (("w_out", k - 16, 0) if k < 18 else (("w_up", k - 18, 0) if k < 26 else ("w_pg", k - 34, 0)))
                    src, sk = nat(nm)
                    dst = WSt[sl][:, 0:4096].rearrange("p (k n) -> p k n", k=8)
                    b.dma("sp", V(dst, wk), V(src[:, :, off + 512 * j:off + 512 * (j + 1)], sk), cost=64)
                elif k < 34:
                    j = k - 26
                    b.dma("sp", V(WSt[sl][:, 0:4096], wk), V(scrD[l, j], [("scrD", l, j)]))
                else:
                    src, sk = nat("w_pe")
                    j = k - 36
                    dst = WSt[sl][:, 0:1024].rearrange("p (k n) -> p k n", k=2)
                    b.dma("sp", V(dst, wk), V(src[:, :, 512 * j:512 * (j + 1)], sk))
                s.issued += 1
            def get(s, ahead=NSLOT - 1):
                while s.issued < min(len(s.seq), s.used + 1 + ahead): s._issue()
                v = WS(s.used % NSLOT); s.used += 1
                return v
        ws = WStream()
        ORDER = [1, 2, 3, 4, 5, 6, 7, 1, 10, 11, 8, 9, 12, 13, 14, 15, 16, 17] + list(range(18, 34)) + [34, 36, 35, 37]
        tiles = [(False, i) for i in range(4)] + [(True, 0)]
        for _t in tiles:
            for l in range(2):
                ws.plan([(l, k) for k in ORDER])

        def rms_stats(g, arr):
            NT = g.NT
            ps = b.bank()
            for c in range(8):
                sq = TB(c % 2)
                b.act(sq[:, 0:NT], arr.c(c)[:, 0:NT], AF.Square)
                b.mm(ps[:, 0:NT], ones, sq[:, 0:NT], c == 0, c == 7)
            b.act(TF(0)[:, 0:NT], ps[:, 0:NT], AF.Ln, bias=V(EPSt[:, 0:1], [("EPSC",)]), scale=1.0 / D)
            b.act(TF(1)[:, 0:NT], TF(0)[:, 0:NT], AF.Exp, scale=-0.5)
            return TF(1)

        EPSt = sb("EPSC", [128, 2], F32)
        b.memset("pool", V(EPSt[:, 0:1], [("EPSC",)]), EPS)
        b.memset("pool", V(EPSt[:, 1:2], [("EPSC",)]), GN_EPS)
        epsv = V(EPSt[:, 0:1], [("EPSC",)]); gnepsv = V(EPSt[:, 1:2], [("EPSC",)])

        def load_l1(ln):
            fence_m4(2)
            b.dma("sp", V(M4flat[:, 0:4608], [("TD", i) for i in range(9)]), V(scr[ln, 0, :, 0:4608], [("scr", ln, 0)]))
        b.s.marks.append(("init_end", b.s.total, len(b.s.ops["pe"])))
        for (sample, ti) in tiles:
            g = Geo(sample); NT, T, NCH = g.NT, g.T, g.NCH
            col0 = SEQ if sample else ti * 512
            last_prompt = (not sample) and ti == 3
            if sample or ti > 0:
                b.dma("sp", V(Ht[:, :, 0:NT], H.all().keys), V(xT[:, :, col0:col0 + NT], []), cost=64)
            for l in range(2):
                b.s.marks.append(("Phase A t%d l%d" % (ti + 4 * sample, l), b.s.total, len(b.s.ops["pe"])))
                if ti == 0 and l == 0 and not sample: drain(0)
                rstd = rms_stats(g, H)
                for c in range(8):
                    b.stt(g.cur(XN.c(c)), g.cmp(H.c(c)), PRM(l, G_PRE + c), g.cmp(rstd), MUL, MUL)
                if sample:
                    b.dma("sp", V(SHIt[:], [("TF", 2)]), V(shT[l], []))
                    b.cp("pool", V(XNt[:, :, 0:80].rearrange("p c (g t) -> p c g t", t=5)[:, :, :, 0], XN.all().keys), V(SHIt[:], [("TF", 2)]))
                    hv = V(Ht[:, :, 0:64].rearrange("p c (g t) -> p c g t", t=4)[:, :, :, 3], H.all().keys)
                    b.tt("pool", V(SHt[:], [("TF", 3)]), hv, V(PRt[:, l, G_PRE:G_PRE + 8].unsqueeze(2).to_broadcast([128, 8, NSQ]), [("PRM", l)]), MUL)
                    rv = V(TFt[:, 1, 0:64].rearrange("p (g t) -> p g t", t=4)[:, :, 3].unsqueeze(1).to_broadcast([128, 8, NSQ]), [("TF", 1)])
                    b.tt("pool", V(SH2t[:], [("TF", 4)]), V(SHt[:], [("TF", 3)]), rv, MUL)
                    b.dma("sp", V(shs_o[l], []), V(SH2t[:], [("TF", 4)]))
                else:
                    b.cp("pool", V(XNt[:, :, 0:1], XN.all().keys), V(XCt[:, l, :].unsqueeze(2), [("XC", l)]))
                    b.cp("pool", V(XCt[:, l, :].unsqueeze(2), [("XC", l)]), V(XNt[:, :, 512:513], XN.all().keys))
                    if last_prompt:
                        b.tt("pool", V(SHt[:, :, 0:1], [("TF", 3)]), V(Ht[:, :, 511:512], H.all().keys), V(PRt[:, l, G_PRE:G_PRE + 8].unsqueeze(2), [("PRM", l)]), MUL)
                        b.ts("pool", V(SHPt[:], [("SHP",)]), V(SHt[:, :, 0], [("TF", 3)]), rstd[:, 511:512], None, MUL)
                        b.dma("sp", V(shp_o[l], []), V(SHPt[:], [("SHP",)]))
                for c in range(8):
                    b.tt("pool", g.cmp(P4.c(c)), g.prev(XN.c(c)), g.cur(XN.c(c)), SUB)
                b.memset("pool", V(F1t[:, :, 0:g.CE].rearrange("p c (n t) -> p c n t", t=T + 1)[:, :, :, 0], F1.all().keys), 0.0)

                b.s.marks.append(("Phase B t%d l%d" % (ti + 4 * sample, l), b.s.total, len(b.s.ops["pe"])))
                if ti == 0 and not sample: drain(4)
                L1f = V(M4flat[:, 0:4608], [("TD", i) for i in range(9)])
                L1 = L1f.re("p (k n) -> p k n", k=8)
                def lora1(o, n, m0, m1, ps):
                    for kc in range(8):
                        b.mm(g.cmp(ps[0:m1 - m0, :]), L1[:, kc, o + m0:o + m1], g.cur(XN.c(kc)), kc == 0, False)
                        b.mm(g.cmp(ps[0:m1 - m0, :]), L1[:, kc, o + n + m0:o + n + m1], g.cmp(P4.c(kc)), False, kc == 7)
                ps = b.bank(); lora1(0, 64, 0, 64, ps)
                b.act(TB(2)[0:64, 0:NT], ps[0:64, 0:NT], AF.Tanh)
                ps = b.bank(); lora1(128, 64, 0, 64, ps)
                b.cp("dve", TB(3)[0:64, 0:NT], ps[0:64, 0:NT])
                ps = b.bank(); lora1(256, 160, 0, 128, ps)
                b.act(TB(4)[:, 0:NT], ps[:, 0:NT], AF.Sigmoid)
                ps = b.bank(); lora1(256, 160, 128, 160, ps)
                b.act(TB(5)[0:32, 0:NT], ps[0:32, 0:NT], AF.Sigmoid)
                L2 = ws.get()
                rsm = CB(C_RSS, 64) if sample else CB(C_RSP, 512)
                for c in range(8):
                    ps = b.bank()
                    b.mm(ps[:, 0:NT], L2[0:64, c * 128:(c + 1) * 128], TB(2)[0:64, 0:NT])
                    b.act(TF(2)[:, 0:NT], ps[:, 0:NT], AF.Sigmoid, bias=PRM(l, W0 + c))
                    for n in range(NCH):
                        b.scan(g.ccur(F1.c(c))[:, n, :], ones[:, 0:T], TF(2)[:, n * T:(n + 1) * T])
                    ps = b.bank()
                    b.mm(ps[:, 0:NT], L2[0:64, 1024 + c * 128:1024 + (c + 1) * 128], TB(3)[0:64, 0:NT])
                    b.act(P1.c(c)[:, 0:NT], ps[:, 0:NT], AF.Sigmoid, bias=PRM(l, A0 + c))

                b.s.marks.append(("Phase C/D t%d l%d" % (ti + 4 * sample, l), b.s.total, len(b.s.ops["pe"])))
                if ti == 0 and l == 0 and not sample: drain(0)
                fence_m4(2)
                wcur = None
                def rkv_block(fb):
                    nonlocal wcur
                    if fb % 4 == 0: wcur = ws.get()
                    return wcur[:, 0:4096].re("p (k n) -> p k n", k=8)[:, :, (fb % 4) * 128:(fb % 4 + 1) * 128]
                def proj_rkv(c, j, outv, eb):
                    wv = rkv_block(3 * c + j)
                    ps = b.bank()
                    NE = g.NE if sample else 512
                    rhs = (lambda kc: XN.c(kc)[:, 0:80]) if sample else (lambda kc: XN.c(kc)[:, 1:513])
                    for kc in range(8):
                        b.mm(ps[:, 0:NE], wv[:, kc, :], rhs(kc), kc == 0, kc == 7)
                    mu = PRM(l, MU_R + 8 * j + c); om = PRM(l, OMM_R + 8 * j + c)
                    if sample:
                        b.act(eb[:, 0:80], ps[:, 0:80], AF.Copy, scale=mu)
                        raw = ps[:, 0:80].re("p (g t) -> p g t", t=5)[:, :, 1:5]
                    else:
                        b.act(eb[:, 1:513], ps[:, 0:512], AF.Copy, scale=mu)
                        rc = V(RCt[:, l, 8 * j + c:8 * j + c + 1], [("RC", l)])
                        b.cp("pool", eb[:, 0:1], rc)
                        b.cp("pool", rc, eb[:, 512:513])
                        raw = ps[:, 0:512]
                    b.stt(g.cmp(outv), raw, om, g.prev(eb), MUL, ADD)
                def d_temps(c):
                    s3 = c % 3
                    if s3 == 0: Pe, Pinv, rr, kraw, ebR, ebK = TB(6), TB(7), TB(8), TB(9), TB(0), TB(1)
                    elif s3 == 1: Pe, Pinv, rr, kraw, ebR, ebK = TBa(0), TBa(1), TBa(2), TBa(3), TBa(14), V(TFt[:, 2, :].bitcast(BF16)[:, 0:W], [("TF", 2)])
                    else: Pe, Pinv, rr, kraw, ebR, ebK = XBa(0), XBa(1), XBa(2), XBa(3), XBa(4), XBa(5)
                    if c % 2 == 0: sqk, kkn, f, kp, rk, bb, sdT, rsT = TB(10), TB(11), TB(12), TB(13), TB(2), TB(3), TF(0), TF(1)
                    else: sqk, kkn, f, kp, rk, bb, sdT, rsT = TBa(4), TBa(5), TBa(6), TBa(7), TBa(8), TBa(9), TFa(10), TFa(12)
                    return Pe, Pinv, rr, kraw, sqk, kkn, f, kp, rk, bb, sdT, rsT, ebR, ebK
                def d_stage1(c):
                    Pe, Pinv, rr, kraw, sqk, kkn, f, kp, rk, bb, sdT, rsT, ebR, ebK = d_temps(c)
                    b.act(Pe[:, 0:g.CE], F1.c(c)[:, 0:g.CE], AF.Exp, scale=-LAM)
                    b.act(g.c3(Pinv), g.ccur(F1.c(c)), AF.Exp, scale=LAM)
                    endv = V(F1t[:, c, 0:g.CE].rearrange("p (n t) -> p n t", t=T + 1)[:, :, T], [("F1", c)])
                    b.act(V(PTt[:, c, 0:NCH], [("PT",)]), endv, AF.Exp, scale=-LAM)
                    proj_rkv(c, 0, rr, ebR)
                    proj_rkv(c, 1, kraw, ebK)
                    proj_rkv(c, 2, P6.c(c), ebR)
                def d_stage2a(c):
                    Pe, Pinv, rr, kraw, sqk, kkn, f, kp, rk, bb, sdT, rsT, ebR, ebK = d_temps(c)
                    b.tt("dve", g.c3(AR(c, 1)), g.c3(rr), g.ccur(Pe), MUL)
                    b.act(sqk[:, 0:NT], kraw[:, 0:NT], AF.Square, scale=PRM(l, KK + c))
                    ps = b.bank()
                    b.mm(ps[:, 0:NT], bones, sqk[:, 0:NT])
                    b.ts("dve", sdT[:, 0:NT], ps[:, 0:NT], 1e-24, None, MAX)
                    b.act(sdT[:, 0:NT], sdT[:, 0:NT], AF.Ln)
                    b.act(rsT[:, 0:NT], sdT[:, 0:NT], AF.Exp, scale=-0.5)
                    b.act(f[:, 0:NT], P1.c(c)[:, 0:NT], AF.Identity, bias=PRM(l, OMKA + c), scale=PRM(l, KA + c))
                    b.stt(kkn[:, 0:NT], kraw[:, 0:NT], PRM(l, KK + c), rsT[:, 0:NT], MUL, MUL)
                def d_stage2b(c):
                    Pe, Pinv, rr, kraw, sqk, kkn, f, kp, rk, bb, sdT, rsT, ebR, ebK = d_temps(c)
                    b.tt("pool", kp[:, 0:NT], kraw[:, 0:NT], f[:, 0:NT], MUL)
                    b.tt("pool", P4.c(c)[:, 0:NT], kp[:, 0:NT], Pinv[:, 0:NT], MUL)
                    b.tt("dve", bb[:, 0:NT], kkn[:, 0:NT], P1.c(c)[:, 0:NT], MUL)
                    b.tt("pool", P5.c(c)[:, 0:NT], bb[:, 0:NT], Pinv[:, 0:NT], MUL)
                    b.stt(g.c3(AR(c, 0)), g.c3(kkn), -1.0, g.cprev(Pe), MUL, MUL)
                    b.tt("pool", rk[:, 0:NT], rr[:, 0:NT], kp[:, 0:NT], MUL)
                def d_stage2c(c):
                    Pe, Pinv, rr, kraw, sqk, kkn, f, kp, rk, bb, sdT, rsT, ebR, ebK = d_temps(c)
                    ps2 = b.bank()
                    b.mm(ps2[:, 0:NT], V(RKt[:, l, c, :], [("RKO", l)]), rk[:, 0:NT])
                    b.tt("dve", P7.c(c)[:, 0:NT], ps2[:, 0:NT], P6.c(c)[:, 0:NT], MUL)

                d_stage1(0); d_stage1(1); d_stage2a(0)
                for c in range(8):
                    if ti == 0 and not sample: drain(1)
                    if c + 2 < 8: d_stage1(c + 2)
                    if c + 1 < 8: d_stage2a(c + 1)
                    d_stage2b(c)
                    if c >= 1: d_stage2c(c - 1)
                d_stage2c(7)
                b.s.marks.append(("Phase E t%d l%d" % (ti + 4 * sample, l), b.s.total, len(b.s.ops["pe"])))
                if ti == 0 and l == 0 and not sample: drain(0)
                fence_m4(3)
                vt, kt, bt, Wb, Ub = TB2(0), TB2(2), TB2(6), TB2(8), TB2(10)
                Ystg = V(TFt[:, 3:5, :].rearrange("p a w -> p (a w)"), [("TF", 3), ("TF", 4)])
                def chunk_gen(n):
                    cs = slice(n * T, (n + 1) * T)
                    si = (n % 2) if sample else l
                    Sv = V(St[:, si], [("S", si)])
                    def sbd_update():
                        b.cp("act", V(Sbt[0:64, si, :, 0:64], [("Sb", si)]), V(St[0:64, si], [("S", si)]))
                        b.cp("act", V(Sbt[64:128, si, :, 64:128], [("Sb", si)]), V(St[64:128, si], [("S", si)]))
                    def Sbd(c): return V(Sbt[:, si, c, :], [("Sb", si)])
                    if sample:
                        b.dma("sp", Sv, V(wkvT[l, :, n], []))
                        sbd_update()
                    def XBs(sidx, k, j=None):
                        if sidx < 2:
                            t_ = XBt[:, sidx, k] if j is None else XBt[:, sidx, k, j]
                        else:
                            base = P1flat[:, (sidx - 2) * 2048:(sidx - 1) * 2048].rearrange("p (k j t) -> p k j t", k=4, j=4)
                            t_ = base[:, k] if j is None else base[:, k, j]
                        return V(t_, [("XB", sidx, k)])
                    if n == 0:
                        b.memset("pool", V(JNKt[:, 0:1], list(P1.all().keys) + [("XB", s_, k_) for s_ in (2, 3) for k_ in range(4)]), 0.0)
                    b.cp("act", V(BMt[64:128, :, 0:T], [("BM",)]), V(P5t[64:128, :, cs], P5.all().keys))
                    b.cp("dve", V(KMt[64:128, :, 0:T], [("KM",)]), V(P4t[64:128, :, cs], P4.all().keys))
                    b.cp("act", V(AMt[64:128, :, 0:T], [("AM",)]), V(ARt[64:128, :, 0, cs], [("AR", c_) for c_ in range(8)]))
                    for hg in range(4):
                        psLr = b.bank(True)
                        psL = psLr.re("p (m t) -> p m t", m=4)
                        for j in range(4):
                            h = 4 * hg + j; c = h // 2
                            psG = b.bank().re("p (m t) -> p m t", m=4)
                            if h % 2 == 0:
                                rhs = AR(c)[0:64, :, cs]
                                b.mm(psG[0:T, 0:2, 0:T], P5.c(c)[0:64, cs], rhs)
                                b.mm(psG[0:T, 2:4, 0:T], P4.c(c)[0:64, cs], rhs)
                                b.tt("dve", V(M4t[0:T, h, :, 0:T], [("M4", h)]), psG[0:T, :, 0:T], mask4[0:T, :, 0:T], MUL)
                                b.mm(psL[0:T, j, 0:T], AR(c, 0)[0:64, cs], P5.c(c)[0:64, cs])
                            else:
                                rhs = AR(c)[:, :, cs]
                                b.mm(psG[0:T, 0:2, 0:T], V(BMt[:, c, 0:T], [("BM",)]), rhs)
                                b.mm(psG[0:T, 2:4, 0:T], V(KMt[:, c, 0:T], [("KM",)]), rhs)
                                tmpm = TB(12 + c % 2)[0:T, 0:512].re("p (m t) -> p m t", m=4)[:, :, 0:T]
                                b.cp("act", tmpm, psG[0:T, :, 0:T])
                                b.tt("pool", V(M4t[0:T, h, :, 0:T], [("M4", h)]), tmpm, mask4[0:T, :, 0:T], MUL)
                                b.mm(psL[0:T, j, 0:T], V(AMt[:, c, 0:T], [("AM",)]), P5.c(c)[:, cs])
                        b.tt("dve", XBs(hg, 0)[0:T, :, 0:T], psL[0:T, :, 0:T], maskL4[0:T, :, 0:T], MUL)
                        b.release(psLr)
                        hs = slice(4 * hg, 4 * hg + 4)
                        hk = [("M4", h) for h in range(4 * hg, 4 * hg + 4)]
                        b.tt("pool", V(TTt[0:T, hs, 0:T], [("TTm", hg)]), V(M4t[0:T, hs, 0, 0:T], hk), ident4[0:T, :, 0:T], ADD)
                    for (src, dst, eng) in ((P6, vt, "act"), (P4, kt, "dve"), (P5, bt, "act")):
                        psb = b.bank().cast(BF16)
                        for c in range(8):
                            b.tr(psb[0:T, c * 128:(c + 1) * 128], src.c(c)[:, cs], ident)
                        b.cp(eng, dst[0:T, 0:1024], psb[0:T, 0:1024])
                    yield 1
                    Xc = {hg: (lambda hg: (lambda j: XBs(hg, 0, j)[0:T, 0:T]))(hg) for hg in range(4)}
                    XTc = {hg: (lambda hg: (lambda j: V(M4t[0:T, 4 * hg + j, 0, 0:T], [("M4", 4 * hg + j)])))(hg) for hg in range(4)}
                    for k in range(1, g.M):
                        kx = k % 2
                        for hg in range(4):
                            psX = b.bank().re("p (m t) -> p m t", m=4)
                            for j in range(4):
                                b.mm(psX[0:T, j, 0:T], XTc[hg](j), Xc[hg](j))
                            b.cp("act", XBs(hg, kx)[0:T, :, 0:T], psX[0:T, :, 0:T])
                        if k < g.M - 1:
                            for hg in range(4):
                                psXT = b.bank().re("p (m t) -> p m t", m=4)
                                for j in range(4):
                                    b.mm(psXT[0:T, j, 0:T], Xc[hg](j), XTc[hg](j))
                                b.cp("dve" if hg % 2 == 0 else "act", XBs(hg, 2 + kx)[0:T, :, 0:T], psXT[0:T, :, 0:T])
                        for hg in range(4):
                            Xc[hg] = (lambda hg, kx: (lambda j: XBs(hg, kx, j)[0:T, 0:T]))(hg, kx)
                            XTc[hg] = (lambda hg, kx: (lambda j: XBs(hg, 2 + kx, j)[0:T, 0:T]))(hg, kx)
                        for hg in range(4):
                            psT = b.bank().re("p (m t) -> p m t", m=4)
                            TTv = V(TTt[0:T, 4 * hg:4 * hg + 4, 0:T], [("TTm", hg)])
                            for j in range(4):
                                b.mm(psT[0:T, j, 0:T], Xc[hg](j), V(TTt[0:T, 4 * hg + j, 0:T], [("TTm", hg)]))
                            b.tt("dve", TTv, psT[0:T, :, 0:T], TTv, ADD)
                    if n == NCH - 1:
                        b.memset("pool", V(JNKt[:, 1:2], list(P1.all().keys) + [("XB", s_, k_) for s_ in (2, 3) for k_ in range(4)]), 0.0)
                    yield 2
                    def hsl(h): return slice(h * 64, (h + 1) * 64)
                    psW = [b.bank(), b.bank()]
                    for c in range(8):
                        o = psW[c // 4][0:T, (c % 4) * 128:(c % 4 + 1) * 128]
                        b.mm(o, AR(c, 0)[:, cs], Sbd(c), True, False)
                        for h2 in range(2):
                            h = 2 * c + h2
                            oo = psW[c // 4][0:T, (c % 4) * 128 + h2 * 64:(c % 4) * 128 + h2 * 64 + 64]
                            b.mm(oo, V(M4t[0:T, h, 2, 0:T], [("M4", h)]), vt[0:T, hsl(h)], False, h2 == 1)
                    b.cp("act", Wb[0:T, 0:512], psW[0][0:T, :])
                    b.cp("dve", Wb[0:T, 512:1024], psW[1][0:T, :])
                    psU = [b.bank(), b.bank()]
                    for h in range(16):
                        b.mm(psU[h // 8][0:T, (h % 8) * 64:(h % 8 + 1) * 64], V(TTt[0:T, h, 0:T], [("TTm", h // 4)]), Wb[0:T, hsl(h)])
                    b.cp("act", Ub[0:T, 0:512], psU[0][0:T, :])
                    b.cp("dve", Ub[0:T, 512:1024], psU[1][0:T, :])
                    psY = [b.bank(True), b.bank(True)]
                    for c in range(8):
                        o = psY[c // 4][0:T, (c % 4) * 128:(c % 4 + 1) * 128]
                        b.mm(o, AR(c, 1)[:, cs], Sbd(c), True, False)
                        for h2 in range(2):
                            h = 2 * c + h2
                            oo = psY[c // 4][0:T, (c % 4) * 128 + h2 * 64:(c % 4) * 128 + h2 * 64 + 64]
                            b.mm(oo, V(M4t[0:T, h, 1, 0:T], [("M4", h)]), Ub[0:T, hsl(h)], False, False)
                            b.mm(oo, V(M4t[0:T, h, 3, 0:T], [("M4", h)]), vt[0:T, hsl(h)], False, h2 == 1)
                    psDr = [b.bank(True), b.bank(True)]
                    psD = [psDr[0].re("p (m t) -> p m t", m=4), psDr[1].re("p (m t) -> p m t", m=4)]
                    for c in range(8):
                        o = psD[c // 4][:, c % 4, :]
                        csl = slice(c * 128, (c + 1) * 128)
                        b.mm(o, bt[0:T, csl], Ub[0:T, csl], True, False)
                        b.mm(o, kt[0:T, csl], vt[0:T, csl], False, True)
                    yield 3
                    b.cp("act", Ystg[0:T, 0:512], psY[0][0:T, :])
                    b.cp("dve", Ystg[0:T, 512:1024], psY[1][0:T, :])
                    b.release(psY[0], psY[1])
                    psYT = [b.bank().re("p (m t) -> p m t", m=4), b.bank().re("p (m t) -> p m t", m=4)]
                    for c in range(8):
                        b.tr(psYT[c // 4][:, c % 4, 0:T], Ystg[0:T, c * 128:(c + 1) * 128], V(IDFt[0:T, 0:T], [("IDF",)]))
                    b.cp("act", V(F1t[:, 0:4, cs], F1.rng(0, 4).keys), psYT[0][:, :, 0:T])
                    b.cp("dve", V(F1t[:, 4:8, cs], F1.rng(4, 8).keys), psYT[1][:, :, 0:T])
                    for bk in range(2):
                        for hh in range(2):
                            pp = slice(64 * hh, 64 * hh + 64)
                            sv = V(St[pp, si, 4 * bk:4 * bk + 4, :], [("S", si)])
                            b.tt("dve", sv, psD[bk][pp, :, 64 * hh:64 * hh + 64], sv, ADD)
                    b.release(psDr[0], psDr[1])
                    ptv = V(PTt[:, :, n:n + 1].to_broadcast([128, 8, 64]), [("PT",)])
                    b.tt("pool", Sv, Sv, ptv, MUL)
                    if sample:
                        b.dma("sp", V(wkvs_o[l, :, n], []), Sv)
                    else:
                        sbd_update()
                        if last_prompt and n == NCH - 1:
                            b.dma("sp", V(wkvp_o[l], []), Sv)

                gens = [chunk_gen(n) for n in range(NCH)]
                next(gens[0])
                for n in range(NCH):
                    if ti == 0 and not sample: drain(2)
                    next(gens[n]); next(gens[n])
                    if n + 1 < NCH: next(gens[n + 1])
                    for _ in gens[n]: pass
                b.s.marks.append(("Phase F t%d l%d" % (ti + 4 * sample, l), b.s.total, len(b.s.ops["pe"])))
                if ti == 0 and not sample: drain(5 if l == 0 else 1000)
                fence_m4(2)
                L2 = ws.get()
                def f_temps(c):
                    return (TB(6), TB(7), TF(0), TF(1), TF(2), TF(3), TF(4)) if c % 2 == 0 else (TBa(0), TBa(1), TFa(2), TFa(4), TFa(6), TFa(8), TFa(10))
                fps = {}
                def f_a(c):
                    ybf, y2, f0, f1, f2, f3, f4 = f_temps(c)
                    b.cp("pool", ybf[:, 0:NT], F1.c(c)[:, 0:NT])
                    b.act(y2[:, 0:NT], F1.c(c)[:, 0:NT], AF.Square)
                    psM = b.bank(True); psE = b.bank()
                    fps[c] = psM
                    b.mm(psM[:, 0:NT], bmean, ybf[:, 0:NT])
                    b.mm(psE[:, 0:NT], bmean, y2[:, 0:NT])
                    b.act(f0[:, 0:NT], psM[:, 0:NT], AF.Square)
                    b.tt("dve", f1[:, 0:NT], psE[:, 0:NT], f0[:, 0:NT], SUB)
                    b.ts("dve", f1[:, 0:NT], f1[:, 0:NT], 0.0, None, MAX)
                    b.act(f1[:, 0:NT], f1[:, 0:NT], AF.Ln, bias=gnepsv)
                    b.act(f2[:, 0:NT], f1[:, 0:NT], AF.Exp, scale=-0.5)
                def f_b(c):
                    ybf, y2, f0, f1, f2, f3, f4 = f_temps(c)
                    psM = fps.pop(c)
                    b.tt("dve", f3[:, 0:NT], F1.c(c)[:, 0:NT], psM[:, 0:NT], SUB)
                    b.release(psM)
                    b.tt("pool", f4[:, 0:NT], f3[:, 0:NT], f2[:, 0:NT], MUL)
                    b.stt(f3[:, 0:NT], f4[:, 0:NT], PRM(l, LNG + c), P7.c(c)[:, 0:NT], MUL, ADD)
                    psg = b.bank()
                    b.mm(psg[:, 0:NT], L2[:, 2048 + c * 128:2048 + (c + 1) * 128], TB(4)[:, 0:NT], True, False)
                    b.mm(psg[:, 0:NT], L2[0:32, 3072 + c * 128:3072 + (c + 1) * 128], TB(5)[0:32, 0:NT], False, True)
                    b.stt(P1.c(c)[:, 0:NT], f3[:, 0:NT], PRM(l, LNB + c), psg[:, 0:NT], ADD, MUL)
                def proj512(wv, c4, ps):
                    for kc in range(8):
                        b.mm(g.cmp(ps), wv[:, kc, c4 * 128:(c4 + 1) * 128], g.cur(XN.c(kc)), kc == 0, kc == 7)
                gM = b.bank(True); gE = b.bank(True)
                gvw = [None]
                def gv_step(c):
                    if c % 4 == 0: gvw[0] = ws.get(1 if c == 0 else 0)[:, 0:4096].re("p (k n) -> p k n", k=8)
                    ps = b.bank(); proj512(gvw[0], c % 4, ps)
                    b.act(AR(c, 1)[:, 0:NT], ps[:, 0:NT], AF.Gelu)
                    gg2 = TB(c % 2)
                    b.tt("pool", gg2[:, 0:NT], AR(c, 1)[:, 0:NT], AR(c, 1)[:, 0:NT], MUL)
                    b.mm(gM[:, 0:NT], ones, AR(c, 1)[:, 0:NT], c == 0, c == 7)
                    b.mm(gE[:, 0:NT], ones, gg2[:, 0:NT], c == 0, c == 7)
                f_a(0)
                for c in range(8):
                    gv_step(c)
                    if c + 1 < 8: f_a(c + 1)
                    f_b(c)
                if not (sample and l == 1): load_l1(1 - l)
                b.s.marks.append(("Phase G t%d l%d" % (ti + 4 * sample, l), b.s.total, len(b.s.ops["pe"])))
                if ti == 0 and l == 0 and not sample: drain(4)
                b.act(TF(0)[:, 0:NT], gM[:, 0:NT], AF.Square, scale=1.0 / D)
                b.stt(TF(1)[:, 0:NT], gE[:, 0:NT], 1.0 / D, TF(0)[:, 0:NT], MUL, SUB)
                b.ts("dve", TF(1)[:, 0:NT], TF(1)[:, 0:NT], 0.0, None, MAX)
                b.act(TF(1)[:, 0:NT], TF(1)[:, 0:NT], AF.Ln, bias=epsv)
                b.act(TF(0)[:, 0:NT], TF(1)[:, 0:NT], AF.Exp, scale=-0.5)
                b.stt(TF(2)[:, 0:NT], gM[:, 0:NT], 1.0 / D, TF(0)[:, 0:NT], MUL, MUL)
                b.release(gM, gE)
                for c in range(8):
                    if c % 4 == 0: wv = ws.get()[:, 0:4096].re("p (k n) -> p k n", k=8)
                    ps = b.bank(); proj512(wv, c % 4, ps)
                    b.act(AR(c, 0)[:, 0:NT], ps[:, 0:NT], AF.Gelu)
                for (arr_, nm) in ((P5, "ga"), (P6, "gb")):
                    for c in range(8):
                        if c % 4 == 0: wv = ws.get()[:, 0:4096].re("p (k n) -> p k n", k=8)
                        ps = b.bank(); proj512(wv, c % 4, ps)
                        b.act(arr_.c(c)[:, 0:NT], ps[:, 0:NT], AF.Sigmoid)
                for c in range(8):
                    t = TF(3 + c % 2)
                    b.tt("dve", t[:, 0:NT], AR(c, 1)[:, 0:NT], TF(0)[:, 0:NT], MUL)
                    b.tt("pool", t[:, 0:NT], t[:, 0:NT], TF(2)[:, 0:NT], SUB)
                    if sample:
                        b.act(t[:, 0:NT], t[:, 0:NT], AF.Identity, bias=PRM(l, VNB + c), scale=PRM(l, VNG + c))
                        b.cp("dve", P4.c(c)[:, 0:NT], t[:, 0:NT])
                        b.dma("sp", V(cvs_o[l, :, c, :], []), t[:, 0:NT])
                    else:
                        b.act(P4.c(c)[:, 0:NT], t[:, 0:NT], AF.Identity, bias=PRM(l, VNB + c), scale=PRM(l, VNG + c))
                for c in range(8):
                    vnt = TB2(0) if c % 2 == 0 else TB2(2)
                    psS = b.bank()
                    for n0 in range(0, NCH, 8):
                        nn = min(8, NCH - n0)
                        psb = b.bank().cast(BF16)
                        for n in range(n0, n0 + nn):
                            b.tr(psb[0:T, (n - n0) * 128:(n - n0 + 1) * 128], P4.c(c)[:, n * T:(n + 1) * T], ident)
                        b.cp("act", vnt[0:T, 0:nn * 128], psb[0:T, 0:nn * 128])
                        for n in range(n0, n0 + nn):
                            o = psS[:, n * T:(n + 1) * T]
                            b.mm(o, vnt[0:T, (n - n0) * 128:(n - n0 + 1) * 128], V(WSMt[0:T, l, c, 0:T], [("WSM", l)]), True, False)
                            b.mm(o, ones[0:1, :], V(BSt[0:1, l, c, 0:T], [("BSR", l)]), False, True)
                    b.tt("dve", P7.c(c)[:, 0:NT], psS[:, 0:NT], AR(c, 0)[:, 0:NT], MUL)
                    b.tt("pool", P7.c(c)[:, 0:NT], P7.c(c)[:, 0:NT], P6.c(c)[:, 0:NT], MUL)
                    b.tt("dve", P1.c(c)[:, 0:NT], P1.c(c)[:, 0:NT], P5.c(c)[:, 0:NT], MUL)
                    b.tt("pool", P1.c(c)[:, 0:NT], P1.c(c)[:, 0:NT], P7.c(c)[:, 0:NT], ADD)

                def dense_out(nblk_w, src_fn, nk, kdiv):
                    for c2 in range(8):
                        if kdiv == 8:
                            if c2 % 4 == 0: wvv = ws.get()[:, 0:4096].re("p (k n) -> p k n", k=8)
                            lw = lambda kc: wvv[:, kc, (c2 % 4) * 128:(c2 % 4 + 1) * 128]
                        else:
                            wvv = ws.get()[:, 0:4096].re("p (k n) -> p k n", k=32)
                            lw = lambda kc: wvv[:, kc, :]
                        ps = b.bank()
                        for kc in range(nk):
                            b.mm(ps[:, 0:NT], lw(kc), src_fn(kc), kc == 0, kc == nk - 1)
                        b.cp("act", F1.c(c2)[:, 0:NT], ps[:, 0:NT])

                def add_normed(gcol):
                    rs_ = rms_stats(g, F1)
                    for c in range(8):
                        t = TF(3 + c % 2)
                        b.stt(t[:, 0:NT], F1.c(c)[:, 0:NT], PRM(l, gcol + c), rs_[:, 0:NT], MUL, MUL)
                        b.tt("pool", H.c(c)[:, 0:NT], H.c(c)[:, 0:NT], t[:, 0:NT], ADD)

                dense_out(2, lambda kc: P1.c(kc)[:, 0:NT], 8, 8)
                add_normed(G_POST)

                b.s.marks.append(("Phase H t%d l%d" % (ti + 4 * sample, l), b.s.total, len(b.s.ops["pe"])))
                if ti == 0 and l == 0 and not sample: drain(4)
                rstd = rms_stats(g, H)
                for c in range(8):
                    b.stt(XN.c(c)[:, 0:NT], H.c(c)[:, 0:NT], PRM(l, F_PRE + c), rstd[:, 0:NT], MUL, MUL)
                ZA = (P4, P5, P6, P7)
                for kb in range(32):
                    if kb % 4 == 0: wv = ws.get()[:, 0:4096].re("p (k n) -> p k n", k=8)
                    ps = b.bank()
                    for kc in range(8):
                        b.mm(ps[:, 0:NT], wv[:, kc, (kb % 4) * 128:(kb % 4 + 1) * 128], XN.c(kc)[:, 0:NT], kc == 0, kc == 7)
                    rl = TB(kb % 4)
                    b.act(rl[:, 0:NT], ps[:, 0:NT], AF.Relu)
                    b.tt("pool", ZA[kb // 8].c(kb % 8)[:, 0:NT], rl[:, 0:NT], rl[:, 0:NT], MUL)
                dense_out(8, lambda kb: ZA[kb // 8].c(kb % 8)[:, 0:NT], 32, 32)
                add_normed(F_POST)

                b.s.marks.append(("Phase I t%d l%d" % (ti + 4 * sample, l), b.s.total, len(b.s.ops["pe"])))
                if ti == 0 and l == 0 and not sample: drain(0)
                for c in range(8):
                    b.cp("pool", XN.c(c)[:, 0:NT], H.c(c)[:, 0:NT])
                for kc in range(2):
                    b.dma("sp", TF(3 + kc)[:, 0:NT], V(pT[l, :, kc, col0:col0 + NT], []))
                    b.cp("pool", TB(8 + kc)[:, 0:NT], TF(3 + kc)[:, 0:NT])
                for c2 in range(8):
                    if c2 % 4 == 0:
                        wpg = ws.get(1)[:, 0:4096].re("p (k n) -> p k n", k=8)
                        wpe = ws.get(1)[:, 0:1024].re("p (k n) -> p k n", k=2)
                    ps1 = b.bank(); ps2 = b.bank()
                    for kc in range(8):
                        b.mm(ps1[:, 0:NT], wpg[:, kc, (c2 % 4) * 128:(c2 % 4 + 1) * 128], XN.c(kc)[:, 0:NT], kc == 0, kc == 7)
                    for kc in range(2):
                        b.mm(ps2[:, 0:NT], wpe[:, kc, (c2 % 4) * 128:(c2 % 4 + 1) * 128], TB(8 + kc)[:, 0:NT], kc == 0, kc == 1)
                    sg = TF(c2 % 2)
                    b.act(sg[:, 0:NT], ps1[:, 0:NT], AF.Sigmoid)
                    b.tt("dve", sg[:, 0:NT], ps2[:, 0:NT], sg[:, 0:NT], MUL)
                    b.tt("pool", H.c(c2)[:, 0:NT], H.c(c2)[:, 0:NT], sg[:, 0:NT], ADD)
                if not (ti == 0 and l == 0 and not sample): drain(1000)
            b.dma("sp", V(yT[:, :, col0:col0 + NT], []), V(Ht[:, :, 0:NT], H.all().keys))

        b.s.marks.append(("end", b.s.total, len(b.s.ops["pe"])))
        import os
        if os.environ.get("KMARKS"):
            for m in b.s.marks: print("MARK", m)
            lo, hi = [int(x) for x in os.environ.get("KDUMP", "0,0").split(",")]
            for e_ in b.s.log:
                if lo <= e_[0] <= hi: print("OP", e_)
            print("OPS", {e: len(b.s.ops[e]) for e in ENGS}, "waits", {e: sum(len(o.waits) for o in b.s.ops[e]) for e in ENGS})
        sch = b.s
        esem = {e: es.enter_context(nc.semaphore(f"sem_{e}")) for e in ENGS}
        dsem = [es.enter_context(nc.semaphore(f"dsem{i}")) for i in range(sch.ndma)]
        for e in ENGS:
            cnt = 0
            for op in sch.ops[e]:
                if op.sig and not op.dma:
                    cnt += 1
                op.cnt = cnt
        final = {}
        for e in ENGS:
            for op in sch.ops[e]:
                if op.dma: final[op.dsem] = max(final.get(op.dsem, 0), op.dval)

        def run(ename, e):
            for op in sch.ops[ename]:
                for d in op.waits:
                    if d.dma: e.wait_ge(dsem[d.dsem], d.dval)
                    else: e.wait_ge(esem[d.eng], d.cnt)
                ins = op.fn(e)
                if op.dma: ins.then_inc(dsem[op.dsem], 16)
                elif op.sig: ins.then_inc(esem[ename], 1)
            if ename == "sp":
                for i, v in final.items():
                    e.wait_ge(dsem[i], v)

        with nc.Block() as block:
            @block.tensor
            def _(e): run("pe", e)
            @block.scalar
            def _(e): run("act", e)
            @block.vector
            def _(e): run("dve", e)
            @block.gpsimd
            def _(e): run("pool", e)
            @block.sync
            def _(e): run("sp", e)
    return nc


def _fm(a):
    n, d = a.shape
    return np.ascontiguousarray(a.T.reshape(d // 128, 128, n).transpose(1, 0, 2))


def _consts():
    c = np.zeros((128, CW), np.float32)
    i = np.arange(128)
    c[:, C_ID:C_ID + 128] = np.eye(128)
    c[:, C_ONES:C_ONES + 128] = 1.0
    blk = (i[:, None] // 64 == i[None, :] // 64).astype(np.float32)
    c[:, C_BONES:C_BONES + 128] = blk
    c[:, C_BMEAN:C_BMEAN + 128] = blk / 64.0
    su = (i[None, :] > i[:, None]).astype(np.float32)
    iu = (i[None, :] >= i[:, None]).astype(np.float32)
    c[:, C_M4:C_M4 + 512] = np.concatenate([su, iu, su, iu], 1)
    sl = (i[None, :] < i[:, None]).astype(np.float32)
    c[:, C_ML4:C_ML4 + 512] = np.concatenate([sl] * 4, 1)
    c[:, C_ID4:C_ID4 + 512] = np.concatenate([np.eye(128, dtype=np.float32)] * 4, 1)
    rp = np.ones(512, np.float32); rp[::128] = 0.0
    rs = np.ones(64, np.float32); rs[::4] = 0.0
    c[:, C_RSP:C_RSP + 512] = rp[None]; c[:, C_RSS:C_RSS + 64] = rs[None]
    c[:, C_SH:C_SH + 64] = (i[:, None] == (np.arange(64)[None, :] + 64)).astype(np.float32)
    return c


_NC_CACHE = {}


def kernel(x_prompt, x_sample, state_wkv, state_shift, p_prompt, p_sample,
           mix_pre_g, mix_post_g, ffn_pre_g, ffn_post_g, w_in, mu_rkv, mu_wag,
           w0, w1, w2, a0, a1, a2, g1, g2, k_k, k_a, r_k, lnx_g, lnx_b,
           vn_g, vn_b, w_s, b_s, w_out, w_up, w_down, w_pe, w_pg):
    f = lambda a: np.ascontiguousarray(np.asarray(a, dtype=np.float32))
    x_prompt, x_sample, state_wkv, state_shift, p_prompt, p_sample = map(f, (x_prompt, x_sample, state_wkv, state_shift, p_prompt, p_sample))
    prm = np.zeros((128, 2, NPC), np.float32)
    def col(v): return np.asarray(v, np.float32).reshape(-1, 128).T
    for l in range(2):
        mr = np.asarray(mu_rkv[l], np.float32)
        items = [(G_PRE, mix_pre_g[l]), (G_POST, mix_post_g[l]), (F_PRE, ffn_pre_g[l]), (F_POST, ffn_post_g[l]),
                 (MU_R, mr[0:1024]), (MU_K, mr[1024:2048]), (MU_V, mr[2048:3072]),
                 (MU_W, mu_wag[l][0]), (MU_A, mu_wag[l][1]), (MU_G, mu_wag[l][2]),
                 (W0, w0[l]), (A0, a0[l]), (KK, k_k[l]), (KA, k_a[l]), (RK, np.asarray(r_k[l]).reshape(-1)),
                 (LNG, lnx_g[l]), (LNB, lnx_b[l]), (VNG, vn_g[l]), (VNB, vn_b[l])]
        for o, v in items:
            prm[:, l, o:o + 8] = col(v)
    cst = _consts()
    wsT = np.ascontiguousarray(np.asarray(w_s, np.float32).transpose(0, 3, 1, 2))
    bs = np.ascontiguousarray(np.asarray(b_s, np.float32).reshape(2, 1, 8, 128))
    shared = dict(prm=prm, cst=cst, wsT=wsT, bs=bs, w_in=f(w_in), w_out=f(w_out), w_up=f(w_up), w_down=f(w_down),
                  w_pe=f(w_pe), w_pg=f(w_pg), w1=f(w1), w2=f(w2), a1=f(a1), a2=f(a2), g1=f(g1), g2=f(g2))
    in_maps = []
    for i in range(NCORE):
        xs = x_sample[NSQ * i:NSQ * (i + 1)].reshape(NSQ * DEC, D)
        xT = np.concatenate([_fm(x_prompt[i]), _fm(xs)], axis=2)
        pTl = []
        for l in range(2):
            ps_ = p_sample[l, NSQ * i:NSQ * (i + 1)].reshape(NSQ * DEC, 256)
            pTl.append(np.concatenate([_fm(p_prompt[l, i]), _fm(ps_)], axis=2))
        pT = np.stack(pTl)
        shT = np.stack([_fm(state_shift[l, NSQ * i:NSQ * (i + 1)]) for l in range(2)])
        sw = state_wkv[:, NSQ * i:NSQ * (i + 1)].reshape(2, NSQ, 8, 2, 64, 64)
        wkvT = np.ascontiguousarray(sw.transpose(0, 3, 5, 1, 2, 4)).reshape(2, 128, NSQ, 8, 64)
        m = dict(shared); m.update(xT=np.ascontiguousarray(xT), pT=np.ascontiguousarray(pT), shT=np.ascontiguousarray(shT), wkvT=wkvT)
        in_maps.append(m)
    if "nc" not in _NC_CACHE:
        _NC_CACHE["nc"] = build_nc()
    res = run_bass_kernel_spmd(_NC_CACHE["nc"], in_maps, core_ids=list(range(NCORE)))
    R = list(res.results)
    def unfm(a):
        return np.ascontiguousarray(a.transpose(2, 1, 0).reshape(a.shape[2], -1))
    y_prompt = np.stack([unfm(R[i]["yT"][:, :, 0:SEQ]) for i in range(NCORE)])
    y_sample = np.concatenate([unfm(R[i]["yT"][:, :, SEQ:]).reshape(NSQ, DEC, D) for i in range(NCORE)])
    wkv_p = np.stack([np.stack([R[i]["wkvp"][l].reshape(2, 64, 8, 64).transpose(2, 0, 3, 1).reshape(16, 64, 64) for i in range(NCORE)]) for l in range(2)])
    sh_p = np.stack([np.stack([R[i]["shp"][l].T.reshape(D) for i in range(NCORE)]) for l in range(2)])
    wkv_s = np.stack([np.concatenate([R[i]["wkvs"][l].reshape(2, 64, NSQ, 8, 64).transpose(2, 3, 0, 4, 1).reshape(NSQ, 16, 64, 64) for i in range(NCORE)]) for l in range(2)])
    sh_s = np.stack([np.concatenate([R[i]["shs"][l].transpose(2, 1, 0).reshape(NSQ, D) for i in range(NCORE)]) for l in range(2)])
    cv_s = np.stack([np.concatenate([R[i]["cvs"][l].reshape(128, 8, NSQ, DEC).transpose(2, 3, 1, 0).reshape(NSQ, DEC, D) for i in range(NCORE)]) for l in range(2)])
    o = lambda a: np.ascontiguousarray(a, dtype=np.float32)
    return (o(y_prompt), o(y_sample), o(wkv_p), o(sh_p), o(wkv_s), o(sh_s), o(cv_s))
```

```python
import numpy as np
from contextlib import ExitStack
import concourse.bass as bass
import concourse.mybir as mybir
from concourse.bass_utils import run_bass_kernel_spmd

F32, BF16 = mybir.dt.float32, mybir.dt.bfloat16
AF = mybir.ActivationFunctionType
ALU = mybir.AluOpType
MUL, ADD, SUB, MAX = ALU.mult, ALU.add, ALU.subtract, ALU.max

D = 1024; NCORE = 8; SEQ = 2048; NSQ = 16; DEC = 4; NTOK = SEQ + NSQ * DEC
W = 516
LAM = 0.6065306597126334
EPS = 1e-6; GN_EPS = 64e-5
NBLK = 38; SLOTW = 4608
G_PRE, G_POST, F_PRE, F_POST, MU_R, MU_K, MU_V, MU_W, MU_A, MU_G = 0, 8, 16, 24, 32, 40, 48, 56, 64, 72
W0, A0, KK, KA, RK, LNG, LNB, VNG, VNB, OMM_R, OMM_K, OMM_V, OMKA, NPC = 80, 88, 96, 104, 112, 120, 128, 136, 144, 152, 160, 168, 176, 184
C_ID, C_ONES, C_BONES, C_BMEAN, C_M4, C_ML4, C_ID4, C_RSP, C_RSS, C_SH, CW = 0, 128, 256, 384, 512, 1024, 1536, 2048, 2560, 2624, 2688


class V:
    __slots__ = ("ap", "keys")

    def __init__(s, ap, keys):
        s.ap = ap; s.keys = tuple(keys)

    def __getitem__(s, i):
        return V(s.ap[i], s.keys)

    def re(s, pat, **kw):
        return V(s.ap.rearrange(pat, **kw), s.keys)

    def bc(s, shape):
        return V(s.ap.to_broadcast(list(shape)), s.keys)

    def cast(s, dt):
        return V(s.ap.bitcast(dt), s.keys)


class Op:
    __slots__ = ("eng", "fn", "idx", "sig", "waits", "dma", "dsem", "dval", "cnt")


ENGS = ("pe", "act", "dve", "pool", "sp")


class Sched:
    def __init__(s, ndma=24):
        s.ops = {e: [] for e in ENGS}
        s.last_w = {}; s.readers = {}
        s.known = {e: {} for e in ENGS}
        s.ndma = ndma; s.dma_i = 0; s.dma_last = [None] * ndma
        import os
        s.limit = int(float(os.environ.get("KLIMIT", "1e12"))); s.marks = []

    def add(s, eng, fn, r, w, dma=False, cost=256):
        s.total = getattr(s, "total", 0) + 1
        if s.total > s.limit: return None
        op = Op(); op.eng = eng; op.fn = fn; op.idx = len(s.ops[eng]); op.sig = False
        op.waits = []; op.dma = dma; op.cnt = 0; op.dsem = -1; op.dval = 0
        deps = []
        for k in r:
            d = s.last_w.get(k)
            if d is not None: deps.append(d)
        for k in w:
            d = s.last_w.get(k)
            if d is not None: deps.append(d)
            deps.extend(s.readers.get(k, ()))
        if dma:
            ph = s.__dict__.setdefault("hist_" + eng, [])
            acc = cost
            lim = 800 if eng == "pool" else 400
            for (pop, pc) in reversed(ph[-16:]):
                acc += pc
                if acc > lim:
                    deps.append(pop); break
        if dma:
            cnts = s.__dict__.setdefault("dma_cnt", {"pool": 0, "sp": 0})
            base, n = (0, 8) if eng == "pool" else (8, s.ndma - 8)
            i = cnts[eng]; cnts[eng] += 1
            slot = base + i % n; op.dsem = slot; op.dval = 16 * (i // n + 1)
            if s.dma_last[slot] is not None: deps.append(s.dma_last[slot])
            s.dma_last[slot] = op
        kn = s.known[eng]
        best = {}
        for d in deps:
            key, val = (("d", d.dsem), d.dval) if d.dma else (d.eng, d.idx)
            if key not in best or val > best[key][0]: best[key] = (val, d)
        deps = [v[1] for v in best.values()]
        for d in deps:
            if d.dma:
                key = ("d", d.dsem)
                if kn.get(key, 0) >= d.dval: continue
                kn[key] = d.dval; op.waits.append(d)
            else:
                if d.eng == "pe" and eng == "pe": continue
                if kn.get(d.eng, -1) >= d.idx: continue
                kn[d.eng] = d.idx; d.sig = True; op.waits.append(d)
        for k in r: s.readers.setdefault(k, []).append(op)
        for k in w:
            s.last_w[k] = op; s.readers[k] = []
        s.ops[eng].append(op)
        import sys as _sys
        s.__dict__.setdefault("log", []).append((s.total, eng, _sys._getframe(1).f_code.co_name, len(op.waits), tuple(w)[:2]))
        if dma: s.__dict__["hist_" + eng].append((op, cost))
        return op


class Geo:
    def __init__(s, sample):
        s.sample = sample
        if sample: s.NT, s.T, s.NCH, s.NE, s.M = 64, 4, 16, 80, 2
        else: s.NT, s.T, s.NCH, s.NE, s.M = 512, 128, 4, 513, 7
        s.CE = s.NCH * (s.T + 1)

    def cur(s, v):
        if s.sample: return v[:, 0:80].re("p (g t) -> p g t", t=5)[:, :, 1:5]
        return v[:, 1:513]

    def prev(s, v):
        if s.sample: return v[:, 0:80].re("p (g t) -> p g t", t=5)[:, :, 0:4]
        return v[:, 0:512]

    def cmp(s, v):
        if s.sample: return v[:, 0:64].re("p (g t) -> p g t", t=4)
        return v[:, 0:512]

    def c3(s, v):
        return v[:, 0:s.NT].re("p (n t) -> p n t", t=s.T)

    def ccur(s, v):
        return v[:, 0:s.CE].re("p (n t) -> p n t", t=s.T + 1)[:, :, 1:s.T + 1]

    def cprev(s, v):
        return v[:, 0:s.CE].re("p (n t) -> p n t", t=s.T + 1)[:, :, 0:s.T]


class Builder:
    def __init__(b, nc):
        b.nc = nc; b.s = Sched(); b.bank_i = 0; b.reserved = set()

    def _k(b, vs):
        ks = []
        for v in vs:
            if v is None or isinstance(v, (int, float)): continue
            ks.extend(v.keys)
        return ks

    def _a(b, v):
        return v.ap if isinstance(v, V) else v

    def mm(b, out, lhsT, rhs, start=True, stop=True):
        o, l, r = out.ap, lhsT.ap, rhs.ap
        b.s.add("pe", lambda e: e.matmul(o, lhsT=l, rhs=r, start=start, stop=stop), b._k([lhsT, rhs]), b._k([out]))

    def tr(b, out, in_, ident):
        o, i, d = out.ap, in_.ap, ident.ap
        b.s.add("pe", lambda e: e.transpose(o, i, d), b._k([in_, ident]), b._k([out]))

    def act(b, out, in_, func, bias=0.0, scale=1.0):
        o, i, bi, sc = out.ap, in_.ap, b._a(bias), b._a(scale)
        b.s.add("act", lambda e: e.activation(out=o, in_=i, func=func, bias=bi, scale=sc), b._k([in_, bias, scale]), b._k([out]))

    def tt(b, eng, out, in0, in1, op):
        o, x, y = out.ap, in0.ap, in1.ap
        b.s.add(eng, lambda e: e.tensor_tensor(out=o, in0=x, in1=y, op=op), b._k([in0, in1]), b._k([out]))

    def ts(b, eng, out, in0, s1, s2=None, op0=MUL, op1=None):
        o, x, a1, a2 = out.ap, in0.ap, b._a(s1), b._a(s2)
        if op1 is None:
            fn = lambda e: e.tensor_scalar(out=o, in0=x, scalar1=a1, scalar2=None, op0=op0)
        else:
            fn = lambda e: e.tensor_scalar(out=o, in0=x, scalar1=a1, scalar2=a2, op0=op0, op1=op1)
        b.s.add(eng, fn, b._k([in0, s1, s2]), b._k([out]))

    def stt(b, out, in0, sc, in1, op0, op1):
        o, x, a, y = out.ap, in0.ap, b._a(sc), in1.ap
        b.s.add("dve", lambda e: e.scalar_tensor_tensor(out=o, in0=x, scalar=a, in1=y, op0=op0, op1=op1), b._k([in0, sc, in1]), b._k([out]))

    def cp(b, eng, out, in_):
        o, i = out.ap, in_.ap
        if eng == "act":
            b.s.add("act", lambda e: e.activation(out=o, in_=i, func=AF.Copy), b._k([in_]), b._k([out]))
        else:
            b.s.add(eng, lambda e: e.tensor_copy(out=o, in_=i), b._k([in_]), b._k([out]))

    def recip(b, out, in_):
        o, i = out.ap, in_.ap
        b.s.add("dve", lambda e: e.reciprocal(out=o, in_=i), b._k([in_]), b._k([out]))

    def memset(b, eng, out, val):
        o = out.ap
        b.s.add(eng, lambda e: e.memset(o, val), [], b._k([out]))

    def scan(b, out, d0, d1):
        o, x, y = out.ap, d0.ap, d1.ap
        b.s.add("dve", lambda e: e.tensor_tensor_scan(out=o, data0=x, data1=y, initial=0.0, op0=MUL, op1=ADD), b._k([d0, d1]), b._k([out]))

    def dma(b, eng, out, in_, cost=None):
        if cost is None: cost = 256 if eng == "pool" else 16
        o, i = out.ap, in_.ap
        b.s.add(eng, lambda e: e.dma_start(out=o, in_=i), b._k([in_]), b._k([out]), dma=True, cost=cost)

    def bank(b, reserve=False):
        while True:
            i = b.bank_i % 8; b.bank_i += 1
            if i not in b.reserved: break
        if reserve: b.reserved.add(i)
        return V(b.ps[i][:], [("ps", i)])

    def release(b, *vs):
        for v in vs: b.reserved.discard(v.keys[0][1])


def build_nc():
    nc = bass.Bass("TRN2", target_bir_lowering=False)
    b = Builder(nc)

    def din(name, shape, dt=F32):
        return nc.dram_tensor(name, list(shape), dt, kind="ExternalInput").ap()

    def dout(name, shape, dt=F32):
        return nc.dram_tensor(name, list(shape), dt, kind="ExternalOutput").ap()

    xT = din("xT", [128, 8, NTOK]); pT = din("pT", [2, 128, 2, NTOK])
    shT = din("shT", [2, 128, 8, NSQ]); wkvT = din("wkvT", [2, 128, NSQ, 8, 64])
    prm_d = din("prm", [128, 2, NPC]); cst_d = din("cst", [128, CW])
    wsT_d = din("wsT", [2, 128, 8, 128]); bs_d = din("bs", [2, 1, 8, 128])
    w_in = din("w_in", [2, D, 7 * D]); w_out = din("w_out", [2, D, D]); w_up = din("w_up", [2, D, 4 * D])
    w_down = din("w_down", [2, 4 * D, D]); w_pe = din("w_pe", [2, 256, D]); w_pg = din("w_pg", [2, D, D])
    w1 = din("w1", [2, D, 64]); w2 = din("w2", [2, 64, D]); a1 = din("a1", [2, D, 64]); a2 = din("a2", [2, 64, D])
    g1 = din("g1", [2, D, 160]); g2 = din("g2", [2, 160, D])
    yT = dout("yT", [128, 8, NTOK]); wkvp_o = dout("wkvp", [2, 128, 8, 64]); shp_o = dout("shp", [2, 128, 8])
    wkvs_o = dout("wkvs", [2, 128, NSQ, 8, 64]); shs_o = dout("shs", [2, 128, 8, NSQ]); cvs_o = dout("cvs", [2, 128, 8, NSQ * DEC])
    scr = nc.dram_tensor("scr", [2, 2, 128, SLOTW], BF16, kind="Internal").ap()
    nW = {nm: nc.dram_tensor("n_" + nm, list(shp), BF16, kind="Internal").ap() for nm, shp in
          (("w_in", (2, D, 7 * D)), ("w_out", (2, D, D)), ("w_up", (2, D, 4 * D)), ("w_down", (2, 4 * D, D)), ("w_pg", (2, D, D)), ("w_pe", (2, 256, D)))}
    scrD = nc.dram_tensor("scrD", [2, 8, 128, 4096], BF16, kind="Internal").ap()
    srcW = {"w_in": w_in, "w_out": w_out, "w_up": w_up, "w_down": w_down, "w_pg": w_pg, "w_pe": w_pe}

    es = ExitStack()
    with es:
        def sb(name, shape, dt):
            return es.enter_context(nc.sbuf_tensor(name, list(shape), dt))

        Ht = sb("H", [128, 8, W], F32); F1t = sb("F1", [128, 8, W], F32)
        XNt = sb("XN", [128, 8, W], BF16); P1t = sb("P1", [128, 8, W], BF16)
        P4t = sb("P4", [128, 8, W], BF16); P5t = sb("P5", [128, 8, W], BF16)
        P6t = sb("P6", [128, 8, W], BF16); P7t = sb("P7", [128, 8, W], BF16)
        ARt = sb("AR", [128, 8, 2, W], BF16)
        TFt = sb("TF", [128, 5, W], F32); TBt = sb("TB", [128, 14, W], BF16)
        M4t = sb("M4", [128, 16, 4, 128], BF16); TTt = sb("TTm", [128, 16, 128], BF16)
        XBt = sb("XB", [128, 2, 4, 4, 128], BF16)
        NSLOT = 3
        WSt = [sb(f"WS{i}", [128, 4096], BF16) for i in range(NSLOT)]
        CBt = sb("CB", [128, CW], BF16); PRt = sb("PRM", [128, 2, NPC], F32)
        RKt = sb("RKO", [128, 2, 8, 128], BF16); WSMt = sb("WSM", [128, 2, 8, 128], BF16)
        BSt = sb("BSR", [1, 2, 8, 128], BF16)
        St = sb("S", [128, 2, 8, 64], F32); Sbt = sb("Sb", [128, 2, 8, 128], BF16)
        BMt = sb("BM", [128, 8, 128], BF16); KMt = sb("KM", [128, 8, 128], BF16); AMt = sb("AM", [128, 8, 128], BF16)
        IDFt = sb("IDF", [128, 128], F32)
        JNKt = sb("JNK", [128, 4], F32)
        P1flat = P1t[:].rearrange("p a w -> p (a w)")
        M4flat = M4t[:].rearrange("p h m t -> p (h m t)")
        def TBa(i): return V(M4flat[:, i * W:(i + 1) * W], [("TD", i)])
        def TFa(i): return V(M4flat[:, i * W:(i + 2) * W].bitcast(F32), [("TD", i), ("TD", i + 1)])
        XBflat = XBt[:].rearrange("p a k j t -> p (a k j t)")
        def XBa(i): return V(XBflat[:, i * W:(i + 1) * W], [("XD", i)])
        def fence_m4(col):
            b.memset("pool", V(JNKt[:, col:col + 1], [("TD", i) for i in range(15)] + [("M4", h) for h in range(16)]
                               + [("XD", i) for i in range(7)] + [("XB", s_, k_) for s_ in (0, 1) for k_ in range(4)]), 0.0)
        PTt = sb("PT", [128, 8, 16], F32)
        XCt = sb("XC", [128, 2, 8], BF16); RCt = sb("RC", [128, 2, 24], BF16)
        SHt = TFt[:, 3, 0:128].rearrange("p (c n) -> p c n", c=8); SH2t = TFt[:, 4, 0:128].rearrange("p (c n) -> p c n", c=8)
        SHPt = sb("SHP", [128, 8], F32)
        SHIt = TFt[:, 2, 0:128].rearrange("p (c n) -> p c n", c=8)
        b.ps = [es.enter_context(nc.psum_tensor(f"ps{i}", [128, 512], F32)) for i in range(8)]

        class Arr:
            def __init__(s, name, t): s.name = name; s.t = t
            def c(s, i): return V(s.t[:, i, :], [(s.name, i)])
            def all(s): return V(s.t[:], [(s.name, i) for i in range(8)])
            def rng(s, a, e): return V(s.t[:, a:e], [(s.name, i) for i in range(a, e)])

        H, F1, XN, P1, P4, P5, P6, P7 = (Arr(n, t) for n, t in (("H", Ht), ("F1", F1t), ("XN", XNt), ("P1", P1t), ("P4", P4t), ("P5", P5t), ("P6", P6t), ("P7", P7t)))

        def AR(c, j=None):
            if j is None: return V(ARt[:, c], [("AR", c)])
            return V(ARt[:, c, j, :], [("AR", c)])

        def TF(i): return V(TFt[:, i, :], [("TF", i)])
        def TB(i): return V(TBt[:, i, :], [("TB", i)])
        def TB2(i): return V(TBt[:, i:i + 2, :].rearrange("p a w -> p (a w)"), [("TB", i), ("TB", i + 1)])
        def CB(o, n): return V(CBt[:, o:o + n], [("CB",)])
        def PRM(l, col, n=1): return V(PRt[:, l, col:col + n], [("PRM", l)])
        def WS(i): return V(WSt[i][:], [("WS", i)])

        ident = CB(C_ID, 128); ones = CB(C_ONES, 128); bones = CB(C_BONES, 128); bmean = CB(C_BMEAN, 128)
        mask4 = CB(C_M4, 512).re("p (m t) -> p m t", m=4); maskL4 = CB(C_ML4, 512).re("p (m t) -> p m t", m=4)
        ident4 = CB(C_ID4, 512).re("p (m t) -> p m t", m=4)
        shiftI = CB(C_SH, 64)

        b.dma("sp", V(Ht[:, :, 0:512], [("H", i) for i in range(8)]), V(xT[:, :, 0:512], []))
        b.dma("pool", CB(0, CW), V(cst_d, []))
        b.dma("sp", V(PRt[:], [("PRM", 0), ("PRM", 1)]), V(prm_d, []))
        b.dma("sp", V(IDFt[:], [("IDF",)]), V(cst_d[:, C_ID:C_ID + 128], []))
        for l in range(2):
            b.ts("pool", PRM(l, OMM_R, 24), PRM(l, MU_R, 24), -1.0, 1.0, MUL, ADD)
            b.ts("pool", PRM(l, OMKA, 8), PRM(l, KA, 8), -1.0, 1.0, MUL, ADD)
        b.memset("pool", V(St[:], [("S", 0), ("S", 1)]), 0.0)
        b.memset("pool", V(Sbt[:], [("Sb", 0), ("Sb", 1)]), 0.0)
        for (t_, k_) in ((BMt, "BM"), (KMt, "KM"), (AMt, "AM")):
            b.memset("pool", V(t_[:], [(k_,)]), 0.0)
        b.memset("pool", V(XCt[:], [("XC", 0), ("XC", 1)]), 0.0)
        b.memset("pool", V(RCt[:], [("RC", 0), ("RC", 1)]), 0.0)

        def SCR(l, k): return V(scr[l, k], [("scr", l, k)])

        conv = [[], []]
        for l in (1, 0):
            stg = V(F1t[:].rearrange("p a w -> p (a w)"), [("F1", i) for i in range(8)])
            w1s = stg[:, 0:512].re("p (k n) -> p k n", k=8)
            a1s = stg[:, 512:1024].re("p (k n) -> p k n", k=8)
            g1s = stg[:, 1024:2304].re("p (k n) -> p k n", k=8)
            wss = stg[:, 2304:3328].re("p (g t) -> p g t", g=8)
            bss = V(TFt[:].rearrange("p a w -> p (a w)"), [("TF", i) for i in range(5)])[0:1, 0:1024].re("p (g t) -> p g t", g=8)
            b.dma("sp", w1s, V(w1[l].rearrange("(k p) n -> p k n", p=128), []))
            b.dma("sp", a1s, V(a1[l].rearrange("(k p) n -> p k n", p=128), []))
            b.dma("sp", g1s, V(g1[l].rearrange("(k p) n -> p k n", p=128), []))
            b.dma("sp", wss, V(wsT_d[l], []))
            b.dma("sp", bss, V(bs_d[l], []))
            LBflat = V(M4flat[:, 0:4608], [("TD", i) for i in range(9)])
            LB = LBflat.re("p (k n) -> p k n", k=8)
            for (src, mu, o, n) in ((w1s, MU_W, 0, 64), (a1s, MU_A, 128, 64), (g1s, MU_G, 256, 160)):
                b.cp("dve", LB[:, :, o:o + n], src)
                muv = V(PRt[:, l, mu:mu + 8].unsqueeze(2).to_broadcast([128, 8, n]), [("PRM", l)])
                b.tt("dve", LB[:, :, o + n:o + 2 * n], src, muv, MUL)
            b.dma("sp", V(scr[l, 0, :, 0:4608], [("scr", l, 0)]), LBflat)
            b.tt("dve", V(WSMt[:, l], [("WSM", l)]), wss, V(CBt[:, C_M4 + 128:C_M4 + 256].unsqueeze(1).to_broadcast([128, 8, 128]), [("CB",)]), MUL)
            b.cp("dve", V(BSt[0:1, l], [("BSR", l)]), bss)
            for c in range(8):
                b.ts("dve", V(RKt[:, l, c, :], [("RKO", l)]), bones, PRM(l, RK + c), None, MUL)
            cl = conv[l]
            def cv(dst, key, src, cost, cl=cl):
                cl.append((lambda: b.dma("pool", V(dst, [key]), V(src, []), cost=cost)))
            cv(scr[l, 1, 0:64, 0:1024], ("scr", l, 1), w2[l], 8)
            cv(scr[l, 1, 0:64, 1024:2048], ("scr", l, 1), a2[l], 8)
            cv(scr[l, 1, 0:128, 2048:3072], ("scr", l, 1), g2[l, 0:128], 8)
            cv(scr[l, 1, 0:32, 3072:4096], ("scr", l, 1), g2[l, 128:160], 8)
            def natcv(nm):
                rows = srcW[nm].shape[1]
                npart = 4 if rows >= 1024 else 1
                rp = rows // npart
                for i in range(npart):
                    cv(nW[nm][l, i * rp:(i + 1) * rp, :], ("nw", l, nm), srcW[nm][l, i * rp:(i + 1) * rp, :], 40)
            natcv("w_in"); natcv("w_out"); natcv("w_up")
            wdv = w_down[l].rearrange("(k p) n -> p k n", p=128)
            for j in range(8):
                cv(scrD[l, j].rearrange("p (k n) -> p k n", k=32), ("scrD", l, j), wdv[:, :, 128 * j:128 * (j + 1)], 258)
            natcv("w_pg"); natcv("w_pe")

        convq = conv[0] + conv[1]
        def drain(n):
            for _ in range(min(n, len(convq))):
                convq.pop(0)()
        drain(8)

        class WStream:
            def __init__(s): s.seq = []; s.issued = 0; s.used = 0
            def plan(s, items): s.seq.extend(items)
            def _issue(s):
                l, k = s.seq[s.issued]
                sl = s.issued % NSLOT
                wk = [("WS", sl)]
                def nat(nm): return nW[nm][l].rearrange("(k p) n -> p k n", p=128), [("nw", l, nm)]
                if k == 1:
                    for (np_, c0, c1) in [(64, 0, 2048), (128, 2048, 3072), (32, 3072, 4096)]:
                        b.dma("sp", V(WSt[sl][0:np_, c0:c1], wk), V(scr[l, 1, 0:np_, c0:c1], [("scr", l, 1)]))
                elif k < 8:
                    src, sk = nat("w_in")
                    dst = WSt[sl][:, 0:4096].rearrange("p (k n) -> p k n", k=8)
                    for q in range(4):
                        fb = 4 * (k - 2) + q; c, j = fb // 3, fb % 3
                        b.dma("sp", V(dst[:, :, q * 128:(q + 1) * 128], wk), V(src[:, :, j * 1024 + c * 128:j * 1024 + (c + 1) * 128], sk), cost=64)
                elif k < 26 or k in (34, 35):
                    nm, j, off = ("w_in", k - 8, 3072) if k < 16 else (("w_out", k - 16, 0) if k < 18 else (("w_up", k - 18, 0) if k < 26 else ("w_pg", k - 34, 0)))
                    src, sk = nat(nm)
                    dst = WSt[sl][:, 0:4096].rearrange("p (k n) -> p k n", k=8)
                    b.dma("sp", V(dst, wk), V(src[:, :, off + 512 * j:off + 512 * (j + 1)], sk), cost=64)
                elif k < 34:
                    j = k - 26
                    b.dma("sp", V(WSt[sl][:, 0:4096], wk), V(scrD[l, j], [("scrD", l, j)]))
                else:
                    src, sk = nat("w_pe")
                    j = k - 36
                    dst = WSt[sl][:, 0:1024].rearrange("p (k n) -> p k n", k=2)
                    b.dma("sp", V(dst, wk), V(src[:, :, 512 * j:512 * (j + 1)], sk))
                s.issued += 1
            def get(s, ahead=NSLOT - 1):
                while s.issued < min(len(s.seq), s.used + 1 + ahead): s._issue()
                v = WS(s.used % NSLOT); s.used += 1
                return v
        ws = WStream()
        ORDER = [1, 2, 3, 4, 5, 6, 7, 1, 10, 11, 8, 9, 12, 13, 14, 15, 16, 17] + list(range(18, 34)) + [34, 36, 35, 37]
        tiles = [(False, i) for i in range(4)] + [(True, 0)]
        for _t in tiles:
            for l in range(2):
                ws.plan([(l, k) for k in ORDER])

        def rms_stats(g, arr):
            NT = g.NT
            ps = b.bank()
            for c in range(8):
                sq = TB(c % 2)
                b.act(sq[:, 0:NT], arr.c(c)[:, 0:NT], AF.Square)
                b.mm(ps[:, 0:NT], ones, sq[:, 0:NT], c == 0, c == 7)
            b.act(TF(0)[:, 0:NT], ps[:, 0:NT], AF.Ln, bias=V(EPSt[:, 0:1], [("EPSC",)]), scale=1.0 / D)
            b.act(TF(1)[:, 0:NT], TF(0)[:, 0:NT], AF.Exp, scale=-0.5)
            return TF(1)

        EPSt = sb("EPSC", [128, 2], F32)
        b.memset("pool", V(EPSt[:, 0:1], [("EPSC",)]), EPS)
        b.memset("pool", V(EPSt[:, 1:2], [("EPSC",)]), GN_EPS)
        epsv = V(EPSt[:, 0:1], [("EPSC",)]); gnepsv = V(EPSt[:, 1:2], [("EPSC",)])

        def load_l1(ln):
            fence_m4(2)
            b.dma("sp", V(M4flat[:, 0:4608], [("TD", i) for i in range(9)]), V(scr[ln, 0, :, 0:4608], [("scr", ln, 0)]))
        b.s.marks.append(("init_end", b.s.total, len(b.s.ops["pe"])))
        for (sample, ti) in tiles:
            g = Geo(sample); NT, T, NCH = g.NT, g.T, g.NCH
            col0 = SEQ if sample else ti * 512
            last_prompt = (not sample) and ti == 3
            if sample or ti > 0:
                b.dma("sp", V(Ht[:, :, 0:NT], H.all().keys), V(xT[:, :, col0:col0 + NT], []), cost=64)
            for l in range(2):
                b.s.marks.append(("Phase A t%d l%d" % (ti + 4 * sample, l), b.s.total, len(b.s.ops["pe"])))
                if ti == 0 and l == 0 and not sample: drain(0)
                rstd = rms_stats(g, H)
                for c in range(8):
                    b.stt(g.cur(XN.c(c)), g.cmp(H.c(c)), PRM(l, G_PRE + c), g.cmp(rstd), MUL, MUL)
                if sample:
                    b.dma("sp", V(SHIt[:], [("TF", 2)]), V(shT[l], []))
                    b.cp("pool", V(XNt[:, :, 0:80].rearrange("p c (g t) -> p c g t", t=5)[:, :, :, 0], XN.all().keys), V(SHIt[:], [("TF", 2)]))
                    hv = V(Ht[:, :, 0:64].rearrange("p c (g t) -> p c g t", t=4)[:, :, :, 3], H.all().keys)
                    b.tt("pool", V(SHt[:], [("TF", 3)]), hv, V(PRt[:, l, G_PRE:G_PRE + 8].unsqueeze(2).to_broadcast([128, 8, NSQ]), [("PRM", l)]), MUL)
                    rv = V(TFt[:, 1, 0:64].rearrange("p (g t) -> p g t", t=4)[:, :, 3].unsqueeze(1).to_broadcast([128, 8, NSQ]), [("TF", 1)])
                    b.tt("pool", V(SH2t[:], [("TF", 4)]), V(SHt[:], [("TF", 3)]), rv, MUL)
                    b.dma("sp", V(shs_o[l], []), V(SH2t[:], [("TF", 4)]))
                else:
                    b.cp("pool", V(XNt[:, :, 0:1], XN.all().keys), V(XCt[:, l, :].unsqueeze(2), [("XC", l)]))
                    b.cp("pool", V(XCt[:, l, :].unsqueeze(2), [("XC", l)]), V(XNt[:, :, 512:513], XN.all().keys))
                    if last_prompt:
                        b.tt("pool", V(SHt[:, :, 0:1], [("TF", 3)]), V(Ht[:, :, 511:512], H.all().keys), V(PRt[:, l, G_PRE:G_PRE + 8].unsqueeze(2), [("PRM", l)]), MUL)
                        b.ts("pool", V(SHPt[:], [("SHP",)]), V(SHt[:, :, 0], [("TF", 3)]), rstd[:, 511:512], None, MUL)
                        b.dma("sp", V(shp_o[l], []), V(SHPt[:], [("SHP",)]))
                for c in range(8):
                    b.tt("pool", g.cmp(P4.c(c)), g.prev(XN.c(c)), g.cur(XN.c(c)), SUB)
                b.memset("pool", V(F1t[:, :, 0:g.CE].rearrange("p c (n t) -> p c n t", t=T + 1)[:, :, :, 0], F1.all().keys), 0.0)

                b.s.marks.append(("Phase B t%d l%d" % (ti + 4 * sample, l), b.s.total, len(b.s.ops["pe"])))
                if ti == 0 and not sample: drain(4)
                L1f = V(M4flat[:, 0:4608], [("TD", i) for i in range(9)])
                L1 = L1f.re("p (k n) -> p k n", k=8)
                def lora1(o, n, m0, m1, ps):
                    for kc in range(8):
                        b.mm(g.cmp(ps[0:m1 - m0, :]), L1[:, kc, o + m0:o + m1], g.cur(XN.c(kc)), kc == 0, False)
                        b.mm(g.cmp(ps[0:m1 - m0, :]), L1[:, kc, o + n + m0:o + n + m1], g.cmp(P4.c(kc)), False, kc == 7)
                ps = b.bank(); lora1(0, 64, 0, 64, ps)
                b.act(TB(2)[0:64, 0:NT], ps[0:64, 0:NT], AF.Tanh)
                ps = b.bank(); lora1(128, 64, 0, 64, ps)
                b.cp("dve", TB(3)[0:64, 0:NT], ps[0:64, 0:NT])
                ps = b.bank(); lora1(256, 160, 0, 128, ps)
                b.act(TB(4)[:, 0:NT], ps[:, 0:NT], AF.Sigmoid)
                ps = b.bank(); lora1(256, 160, 128, 160, ps)
                b.act(TB(5)[0:32, 0:NT], ps[0:32, 0:NT], AF.Sigmoid)
                L2 = ws.get()
                rsm = CB(C_RSS, 64) if sample else CB(C_RSP, 512)
                for c in range(8):
                    ps = b.bank()
                    b.mm(ps[:, 0:NT], L2[0:64, c * 128:(c + 1) * 128], TB(2)[0:64, 0:NT])
                    b.act(TF(2 + c % 2)[:, 0:NT], ps[:, 0:NT], AF.Sigmoid, bias=PRM(l, W0 + c))
                    for n in range(NCH):
                        b.scan(g.ccur(F1.c(c))[:, n, :], ones[:, 0:T], TF(2 + c % 2)[:, n * T:(n + 1) * T])
                    ps = b.bank()
                    b.mm(ps[:, 0:NT], L2[0:64, 1024 + c * 128:1024 + (c + 1) * 128], TB(3)[0:64, 0:NT])
                    b.act(P1.c(c)[:, 0:NT], ps[:, 0:NT], AF.Sigmoid, bias=PRM(l, A0 + c))

                b.s.marks.append(("Phase C/D t%d l%d" % (ti + 4 * sample, l), b.s.total, len(b.s.ops["pe"])))
                if ti == 0 and l == 0 and not sample: drain(0)
                fence_m4(2)
                wcur = None
                def rkv_block(fb):
                    nonlocal wcur
                    if fb % 4 == 0: wcur = ws.get()
                    return wcur[:, 0:4096].re("p (k n) -> p k n", k=8)[:, :, (fb % 4) * 128:(fb % 4 + 1) * 128]
                def proj_rkv(c, j, outv, eb):
                    wv = rkv_block(3 * c + j)
                    ps = b.bank()
                    NE = g.NE if sample else 512
                    rhs = (lambda kc: XN.c(kc)[:, 0:80]) if sample else (lambda kc: XN.c(kc)[:, 1:513])
                    for kc in range(8):
                        b.mm(ps[:, 0:NE], wv[:, kc, :], rhs(kc), kc == 0, kc == 7)
                    mu = PRM(l, MU_R + 8 * j + c); om = PRM(l, OMM_R + 8 * j + c)
                    if sample:
                        b.act(eb[:, 0:80], ps[:, 0:80], AF.Copy, scale=mu)
                        raw = ps[:, 0:80].re("p (g t) -> p g t", t=5)[:, :, 1:5]
                    else:
                        b.act(eb[:, 1:513], ps[:, 0:512], AF.Copy, scale=mu)
                        rc = V(RCt[:, l, 8 * j + c:8 * j + c + 1], [("RC", l)])
                        b.cp("pool", eb[:, 0:1], rc)
                        b.cp("pool", rc, eb[:, 512:513])
                        raw = ps[:, 0:512]
                    b.stt(g.cmp(outv), raw, om, g.prev(eb), MUL, ADD)
                def d_temps(c):
                    s3 = c % 3
                    if s3 == 0: Pe, Pinv, rr, kraw, ebR, ebK = TB(6), TB(7), TB(8), TB(9), TB(0), TB(1)
                    elif s3 == 1: Pe, Pinv, rr, kraw, ebR, ebK = TBa(0), TBa(1), TBa(2), TBa(3), TBa(14), V(TFt[:, 2, :].bitcast(BF16)[:, 0:W], [("TF", 2)])
                    else: Pe, Pinv, rr, kraw, ebR, ebK = XBa(0), XBa(1), XBa(2), XBa(3), XBa(4), XBa(5)
                    if c % 2 == 0: sqk, kkn, f, kp, rk, bb, sdT, rsT = TB(10), TB(11), TB(12), TB(13), TB(2), TB(3), TF(0), TF(1)
                    else: sqk, kkn, f, kp, rk, bb, sdT, rsT = TBa(4), TBa(5), TBa(6), TBa(7), TBa(8), TBa(9), TFa(10), TFa(12)
                    return Pe, Pinv, rr, kraw, sqk, kkn, f, kp, rk, bb, sdT, rsT, ebR, ebK
                def d_stage1(c):
                    Pe, Pinv, rr, kraw, sqk, kkn, f, kp, rk, bb, sdT, rsT, ebR, ebK = d_temps(c)
                    b.act(Pe[:, 0:g.CE], F1.c(c)[:, 0:g.CE], AF.Exp, scale=-LAM)
                    b.act(g.c3(Pinv), g.ccur(F1.c(c)), AF.Exp, scale=LAM)
                    endv = V(F1t[:, c, 0:g.CE].rearrange("p (n t) -> p n t", t=T + 1)[:, :, T], [("F1", c)])
                    b.act(V(PTt[:, c, 0:NCH], [("PT",)]), endv, AF.Exp, scale=-LAM)
                    proj_rkv(c, 0, rr, ebR)
                    proj_rkv(c, 1, kraw, ebK)
                    proj_rkv(c, 2, P6.c(c), ebR)
                def d_stage2a(c):
                    Pe, Pinv, rr, kraw, sqk, kkn, f, kp, rk, bb, sdT, rsT, ebR, ebK = d_temps(c)
                    b.tt("dve", g.c3(AR(c, 1)), g.c3(rr), g.ccur(Pe), MUL)
                    b.act(sqk[:, 0:NT], kraw[:, 0:NT], AF.Square, scale=PRM(l, KK + c))
                    ps = b.bank()
                    b.mm(ps[:, 0:NT], bones, sqk[:, 0:NT])
                    b.ts("dve", sdT[:, 0:NT], ps[:, 0:NT], 1e-24, None, MAX)
                    b.act(sdT[:, 0:NT], sdT[:, 0:NT], AF.Ln)
                    b.act(rsT[:, 0:NT], sdT[:, 0:NT], AF.Exp, scale=-0.5)
                    b.act(f[:, 0:NT], P1.c(c)[:, 0:NT], AF.Identity, bias=PRM(l, OMKA + c), scale=PRM(l, KA + c))
                    b.stt(kkn[:, 0:NT], kraw[:, 0:NT], PRM(l, KK + c), rsT[:, 0:NT], MUL, MUL)
                def d_stage2b(c):
                    Pe, Pinv, rr, kraw, sqk, kkn, f, kp, rk, bb, sdT, rsT, ebR, ebK = d_temps(c)
                    b.tt("pool", kp[:, 0:NT], kraw[:, 0:NT], f[:, 0:NT], MUL)
                    b.tt("pool", P4.c(c)[:, 0:NT], kp[:, 0:NT], Pinv[:, 0:NT], MUL)
                    b.tt("dve", bb[:, 0:NT], kkn[:, 0:NT], P1.c(c)[:, 0:NT], MUL)
                    b.tt("pool", P5.c(c)[:, 0:NT], bb[:, 0:NT], Pinv[:, 0:NT], MUL)
                    b.stt(g.c3(AR(c, 0)), g.c3(kkn), -1.0, g.cprev(Pe), MUL, MUL)
                    b.tt("pool", rk[:, 0:NT], rr[:, 0:NT], kp[:, 0:NT], MUL)
                def d_stage2c(c):
                    Pe, Pinv, rr, kraw, sqk, kkn, f, kp, rk, bb, sdT, rsT, ebR, ebK = d_temps(c)
                    ps2 = b.bank()
                    b.mm(ps2[:, 0:NT], V(RKt[:, l, c, :], [("RKO", l)]), rk[:, 0:NT])
                    b.tt("dve", P7.c(c)[:, 0:NT], ps2[:, 0:NT], P6.c(c)[:, 0:NT], MUL)

                d_stage1(0); d_stage1(1); d_stage2a(0)
                for c in range(8):
                    if ti == 0 and not sample: drain(1)
                    if c + 2 < 8: d_stage1(c + 2)
                    if c + 1 < 8: d_stage2a(c + 1)
                    d_stage2b(c)
                    if c >= 1: d_stage2c(c - 1)
                d_stage2c(7)
                b.s.marks.append(("Phase E t%d l%d" % (ti + 4 * sample, l), b.s.total, len(b.s.ops["pe"])))
                if ti == 0 and l == 0 and not sample: drain(0)
                fence_m4(3)
                vt, kt, bt, Wb, Ub = TB2(0), TB2(2), TB2(6), TB2(8), TB2(10)
                Ystg = V(TFt[:, 3:5, :].rearrange("p a w -> p (a w)"), [("TF", 3), ("TF", 4)])
                def chunk_gen(n):
                    cs = slice(n * T, (n + 1) * T)
                    si = (n % 2) if sample else l
                    Sv = V(St[:, si], [("S", si)])
                    def sbd_update():
                        b.cp("act", V(Sbt[0:64, si, :, 0:64], [("Sb", si)]), V(St[0:64, si], [("S", si)]))
                        b.cp("act", V(Sbt[64:128, si, :, 64:128], [("Sb", si)]), V(St[64:128, si], [("S", si)]))
                    def Sbd(c): return V(Sbt[:, si, c, :], [("Sb", si)])
                    if sample:
                        b.dma("sp", Sv, V(wkvT[l, :, n], []))
                        sbd_update()
                    def XBs(sidx, k, j=None):
                        if sidx < 2:
                            t_ = XBt[:, sidx, k] if j is None else XBt[:, sidx, k, j]
                        else:
                            base = P1flat[:, (sidx - 2) * 2048:(sidx - 1) * 2048].rearrange("p (k j t) -> p k j t", k=4, j=4)
                            t_ = base[:, k] if j is None else base[:, k, j]
                        return V(t_, [("XB", sidx, k)])
                    if n == 0:
                        b.memset("pool", V(JNKt[:, 0:1], list(P1.all().keys) + [("XB", s_, k_) for s_ in (2, 3) for k_ in range(4)]), 0.0)
                    b.cp("act", V(BMt[64:128, :, 0:T], [("BM",)]), V(P5t[64:128, :, cs], P5.all().keys))
                    b.cp("dve", V(KMt[64:128, :, 0:T], [("KM",)]), V(P4t[64:128, :, cs], P4.all().keys))
                    b.cp("act", V(AMt[64:128, :, 0:T], [("AM",)]), V(ARt[64:128, :, 0, cs], [("AR", c_) for c_ in range(8)]))
                    for hg in range(4):
                        psLr = b.bank(True)
                        psL = psLr.re("p (m t) -> p m t", m=4)
                        for j in range(4):
                            h = 4 * hg + j; c = h // 2
                            psG = b.bank().re("p (m t) -> p m t", m=4)
                            if h % 2 == 0:
                                rhs = AR(c)[0:64, :, cs]
                                b.mm(psG[0:T, 0:2, 0:T], P5.c(c)[0:64, cs], rhs)
                                b.mm(psG[0:T, 2:4, 0:T], P4.c(c)[0:64, cs], rhs)
                                b.tt("dve", V(M4t[0:T, h, :, 0:T], [("M4", h)]), psG[0:T, :, 0:T], mask4[0:T, :, 0:T], MUL)
                                b.mm(psL[0:T, j, 0:T], AR(c, 0)[0:64, cs], P5.c(c)[0:64, cs])
                            else:
                                rhs = AR(c)[:, :, cs]
                                b.mm(psG[0:T, 0:2, 0:T], V(BMt[:, c, 0:T], [("BM",)]), rhs)
                                b.mm(psG[0:T, 2:4, 0:T], V(KMt[:, c, 0:T], [("KM",)]), rhs)
                                tmpm = TB(12 + c % 2)[0:T, 0:512].re("p (m t) -> p m t", m=4)[:, :, 0:T]
                                b.cp("act", tmpm, psG[0:T, :, 0:T])
                                b.tt("pool", V(M4t[0:T, h, :, 0:T], [("M4", h)]), tmpm, mask4[0:T, :, 0:T], MUL)
                                b.mm(psL[0:T, j, 0:T], V(AMt[:, c, 0:T], [("AM",)]), P5.c(c)[:, cs])
                        b.tt("dve", XBs(hg, 0)[0:T, :, 0:T], psL[0:T, :, 0:T], maskL4[0:T, :, 0:T], MUL)
                        b.release(psLr)
                        hs = slice(4 * hg, 4 * hg + 4)
                        hk = [("M4", h) for h in range(4 * hg, 4 * hg + 4)]
                        b.tt("pool", V(TTt[0:T, hs, 0:T], [("TTm", hg)]), V(M4t[0:T, hs, 0, 0:T], hk), ident4[0:T, :, 0:T], ADD)
                    for (src, dst, eng) in ((P6, vt, "act"), (P4, kt, "dve"), (P5, bt, "act")):
                        psb = b.bank().cast(BF16)
                        for c in range(8):
                            b.tr(psb[0:T, c * 128:(c + 1) * 128], src.c(c)[:, cs], ident)
                        b.cp(eng, dst[0:T, 0:1024], psb[0:T, 0:1024])
                    yield 1
                    Xc = {hg: (lambda hg: (lambda j: XBs(hg, 0, j)[0:T, 0:T]))(hg) for hg in range(4)}
                    XTc = {hg: (lambda hg: (lambda j: V(M4t[0:T, 4 * hg + j, 0, 0:T], [("M4", 4 * hg + j)])))(hg) for hg in range(4)}
                    for k in range(1, g.M):
                        kx = k % 2
                        for hg in range(4):
                            psX = b.bank().re("p (m t) -> p m t", m=4)
                            for j in range(4):
                                b.mm(psX[0:T, j, 0:T], XTc[hg](j), Xc[hg](j))
                            b.cp("act", XBs(hg, kx)[0:T, :, 0:T], psX[0:T, :, 0:T])
                        if k < g.M - 1:
                            for hg in range(4):
                                psXT = b.bank().re("p (m t) -> p m t", m=4)
                                for j in range(4):
                                    b.mm(psXT[0:T, j, 0:T], Xc[hg](j), XTc[hg](j))
                                b.cp("dve" if hg % 2 == 0 else "act", XBs(hg, 2 + kx)[0:T, :, 0:T], psXT[0:T, :, 0:T])
                        for hg in range(4):
                            Xc[hg] = (lambda hg, kx: (lambda j: XBs(hg, kx, j)[0:T, 0:T]))(hg, kx)
                            XTc[hg] = (lambda hg, kx: (lambda j: XBs(hg, 2 + kx, j)[0:T, 0:T]))(hg, kx)
                        for hg in range(4):
                            psT = b.bank().re("p (m t) -> p m t", m=4)
                            TTv = V(TTt[0:T, 4 * hg:4 * hg + 4, 0:T], [("TTm", hg)])
                            for j in range(4):
                                b.mm(psT[0:T, j, 0:T], Xc[hg](j), V(TTt[0:T, 4 * hg + j, 0:T], [("TTm", hg)]))
                            b.tt("dve", TTv, psT[0:T, :, 0:T], TTv, ADD)
                    if n == NCH - 1:
                        b.memset("pool", V(JNKt[:, 1:2], list(P1.all().keys) + [("XB", s_, k_) for s_ in (2, 3) for k_ in range(4)]), 0.0)
                    yield 2
                    def hsl(h): return slice(h * 64, (h + 1) * 64)
                    psW = [b.bank(), b.bank()]
                    for c in range(8):
                        o = psW[c // 4][0:T, (c % 4) * 128:(c % 4 + 1) * 128]
                        b.mm(o, AR(c, 0)[:, cs], Sbd(c), True, False)
                        for h2 in range(2):
                            h = 2 * c + h2
                            oo = psW[c // 4][0:T, (c % 4) * 128 + h2 * 64:(c % 4) * 128 + h2 * 64 + 64]
                            b.mm(oo, V(M4t[0:T, h, 2, 0:T], [("M4", h)]), vt[0:T, hsl(h)], False, h2 == 1)
                    b.cp("act", Wb[0:T, 0:512], psW[0][0:T, :])
                    b.cp("dve", Wb[0:T, 512:1024], psW[1][0:T, :])
                    psU = [b.bank(), b.bank()]
                    for h in range(16):
                        b.mm(psU[h // 8][0:T, (h % 8) * 64:(h % 8 + 1) * 64], V(TTt[0:T, h, 0:T], [("TTm", h // 4)]), Wb[0:T, hsl(h)])
                    b.cp("act", Ub[0:T, 0:512], psU[0][0:T, :])
                    b.cp("dve", Ub[0:T, 512:1024], psU[1][0:T, :])
                    psY = [b.bank(True), b.bank(True)]
                    for c in range(8):
                        o = psY[c // 4][0:T, (c % 4) * 128:(c % 4 + 1) * 128]
                        b.mm(o, AR(c, 1)[:, cs], Sbd(c), True, False)
                        for h2 in range(2):
                            h = 2 * c + h2
                            oo = psY[c // 4][0:T, (c % 4) * 128 + h2 * 64:(c % 4) * 128 + h2 * 64 + 64]
                            b.mm(oo, V(M4t[0:T, h, 1, 0:T], [("M4", h)]), Ub[0:T, hsl(h)], False, False)
                            b.mm(oo, V(M4t[0:T, h, 3, 0:T], [("M4", h)]), vt[0:T, hsl(h)], False, h2 == 1)
                    psDr = [b.bank(True), b.bank(True)]
                    psD = [psDr[0].re("p (m t) -> p m t", m=4), psDr[1].re("p (m t) -> p m t", m=4)]
                    for c in range(8):
                        o = psD[c // 4][:, c % 4, :]
                        csl = slice(c * 128, (c + 1) * 128)
                        b.mm(o, bt[0:T, csl], Ub[0:T, csl], True, False)
                        b.mm(o, kt[0:T, csl], vt[0:T, csl], False, True)
                    yield 3
                    b.cp("act", Ystg[0:T, 0:512], psY[0][0:T, :])
                    b.cp("dve", Ystg[0:T, 512:1024], psY[1][0:T, :])
                    b.release(psY[0], psY[1])
                    psYT = [b.bank().re("p (m t) -> p m t", m=4), b.bank().re("p (m t) -> p m t", m=4)]
                    for c in range(8):
                        b.tr(psYT[c // 4][:, c % 4, 0:T], Ystg[0:T, c * 128:(c + 1) * 128], V(IDFt[0:T, 0:T], [("IDF",)]))
                    b.cp("act", V(F1t[:, 0:4, cs], F1.rng(0, 4).keys), psYT[0][:, :, 0:T])
                    b.cp("dve", V(F1t[:, 4:8, cs], F1.rng(4, 8).keys), psYT[1][:, :, 0:T])
                    for bk in range(2):
                        for hh in range(2):
                            pp = slice(64 * hh, 64 * hh + 64)
                            sv = V(St[pp, si, 4 * bk:4 * bk + 4, :], [("S", si)])
                            b.tt("dve", sv, psD[bk][pp, :, 64 * hh:64 * hh + 64], sv, ADD)
                    b.release(psDr[0], psDr[1])
                    ptv = V(PTt[:, :, n:n + 1].to_broadcast([128, 8, 64]), [("PT",)])
                    b.tt("pool", Sv, Sv, ptv, MUL)
                    if sample:
                        b.dma("sp", V(wkvs_o[l, :, n], []), Sv)
                    else:
                        sbd_update()
                        if last_prompt and n == NCH - 1:
                            b.dma("sp", V(wkvp_o[l], []), Sv)

                gens = [chunk_gen(n) for n in range(NCH)]
                next(gens[0])
                for n in range(NCH):
                    if ti == 0 and not sample: drain(2)
                    next(gens[n]); next(gens[n])
                    if n + 1 < NCH: next(gens[n + 1])
                    for _ in gens[n]: pass
                b.s.marks.append(("Phase F t%d l%d" % (ti + 4 * sample, l), b.s.total, len(b.s.ops["pe"])))
                if ti == 0 and not sample: drain(5 if l == 0 else 1000)
                fence_m4(2)
                L2 = ws.get()
                def f_temps(c):
                    return (TB(6), TB(7), TF(0), TF(1), TF(2), TF(3), TF(4)) if c % 2 == 0 else (TBa(0), TBa(1), TFa(2), TFa(4), TFa(6), TFa(8), TFa(10))
                fps = {}
                def f_a(c):
                    ybf, y2, f0, f1, f2, f3, f4 = f_temps(c)
                    b.cp("pool", ybf[:, 0:NT], F1.c(c)[:, 0:NT])
                    b.act(y2[:, 0:NT], F1.c(c)[:, 0:NT], AF.Square)
                    psM = b.bank(True); psE = b.bank()
                    fps[c] = psM
                    b.mm(psM[:, 0:NT], bmean, ybf[:, 0:NT])
                    b.mm(psE[:, 0:NT], bmean, y2[:, 0:NT])
                    b.act(f0[:, 0:NT], psM[:, 0:NT], AF.Square)
                    b.tt("dve", f1[:, 0:NT], psE[:, 0:NT], f0[:, 0:NT], SUB)
                    b.ts("dve", f1[:, 0:NT], f1[:, 0:NT], 0.0, None, MAX)
                    b.act(f1[:, 0:NT], f1[:, 0:NT], AF.Ln, bias=gnepsv)
                    b.act(f2[:, 0:NT], f1[:, 0:NT], AF.Exp, scale=-0.5)
                def f_b(c):
                    ybf, y2, f0, f1, f2, f3, f4 = f_temps(c)
                    psM = fps.pop(c)
                    b.tt("dve", f3[:, 0:NT], F1.c(c)[:, 0:NT], psM[:, 0:NT], SUB)
                    b.release(psM)
                    b.tt("pool", f4[:, 0:NT], f3[:, 0:NT], f2[:, 0:NT], MUL)
                    b.stt(f3[:, 0:NT], f4[:, 0:NT], PRM(l, LNG + c), P7.c(c)[:, 0:NT], MUL, ADD)
                    psg = b.bank()
                    b.mm(psg[:, 0:NT], L2[:, 2048 + c * 128:2048 + (c + 1) * 128], TB(4)[:, 0:NT], True, False)
                    b.mm(psg[:, 0:NT], L2[0:32, 3072 + c * 128:3072 + (c + 1) * 128], TB(5)[0:32, 0:NT], False, True)
                    b.stt(P1.c(c)[:, 0:NT], f3[:, 0:NT], PRM(l, LNB + c), psg[:, 0:NT], ADD, MUL)
                def proj512(wv, c4, ps):
                    for kc in range(8):
                        b.mm(g.cmp(ps), wv[:, kc, c4 * 128:(c4 + 1) * 128], g.cur(XN.c(kc)), kc == 0, kc == 7)
                gM = b.bank(True); gE = b.bank(True)
                gvw = [None]
                def gv_step(c):
                    if c % 4 == 0: gvw[0] = ws.get(1 if c == 0 else 0)[:, 0:4096].re("p (k n) -> p k n", k=8)
                    ps = b.bank(); proj512(gvw[0], c % 4, ps)
                    b.act(AR(c, 1)[:, 0:NT], ps[:, 0:NT], AF.Gelu)
                    gg2 = TB(c % 2)
                    b.tt("pool", gg2[:, 0:NT], AR(c, 1)[:, 0:NT], AR(c, 1)[:, 0:NT], MUL)
                    b.mm(gM[:, 0:NT], ones, AR(c, 1)[:, 0:NT], c == 0, c == 7)
                    b.mm(gE[:, 0:NT], ones, gg2[:, 0:NT], c == 0, c == 7)
                f_a(0)
                for c in range(8):
                    gv_step(c)
                    if c + 1 < 8: f_a(c + 1)
                    f_b(c)
                if not (sample and l == 1): load_l1(1 - l)
                b.s.marks.append(("Phase G t%d l%d" % (ti + 4 * sample, l), b.s.total, len(b.s.ops["pe"])))
                if ti == 0 and l == 0 and not sample: drain(4)
                b.act(TF(0)[:, 0:NT], gM[:, 0:NT], AF.Square, scale=1.0 / D)
                b.stt(TF(1)[:, 0:NT], gE[:, 0:NT], 1.0 / D, TF(0)[:, 0:NT], MUL, SUB)
                b.ts("dve", TF(1)[:, 0:NT], TF(1)[:, 0:NT], 0.0, None, MAX)
                b.act(TF(1)[:, 0:NT], TF(1)[:, 0:NT], AF.Ln, bias=epsv)
                b.act(TF(0)[:, 0:NT], TF(1)[:, 0:NT], AF.Exp, scale=-0.5)
                b.stt(TF(2)[:, 0:NT], gM[:, 0:NT], 1.0 / D, TF(0)[:, 0:NT], MUL, MUL)
                b.release(gM, gE)
                for c in range(8):
                    if c % 4 == 0: wv = ws.get()[:, 0:4096].re("p (k n) -> p k n", k=8)
                    ps = b.bank(); proj512(wv, c % 4, ps)
                    b.act(AR(c, 0)[:, 0:NT], ps[:, 0:NT], AF.Gelu)
                for (arr_, nm) in ((P5, "ga"), (P6, "gb")):
                    for c in range(8):
                        if c % 4 == 0: wv = ws.get()[:, 0:4096].re("p (k n) -> p k n", k=8)
                        ps = b.bank(); proj512(wv, c % 4, ps)
                        b.act(arr_.c(c)[:, 0:NT], ps[:, 0:NT], AF.Sigmoid)
                for c in range(8):
                    t = TF(3 + c % 2)
                    b.tt("dve", t[:, 0:NT], AR(c, 1)[:, 0:NT], TF(0)[:, 0:NT], MUL)
                    b.tt("pool", t[:, 0:NT], t[:, 0:NT], TF(2)[:, 0:NT], SUB)
                    if sample:
                        b.act(t[:, 0:NT], t[:, 0:NT], AF.Identity, bias=PRM(l, VNB + c), scale=PRM(l, VNG + c))
                        b.cp("dve", P4.c(c)[:, 0:NT], t[:, 0:NT])
                        b.dma("sp", V(cvs_o[l, :, c, :], []), t[:, 0:NT])
                    else:
                        b.act(P4.c(c)[:, 0:NT], t[:, 0:NT], AF.Identity, bias=PRM(l, VNB + c), scale=PRM(l, VNG + c))
                for c in range(8):
                    vnt = TB2(0) if c % 2 == 0 else TB2(2)
                    psS = b.bank()
                    for n0 in range(0, NCH, 8):
                        nn = min(8, NCH - n0)
                        psb = b.bank().cast(BF16)
                        for n in range(n0, n0 + nn):
                            b.tr(psb[0:T, (n - n0) * 128:(n - n0 + 1) * 128], P4.c(c)[:, n * T:(n + 1) * T], ident)
                        b.cp("act", vnt[0:T, 0:nn * 128], psb[0:T, 0:nn * 128])
                        for n in range(n0, n0 + nn):
                            o = psS[:, n * T:(n + 1) * T]
                            b.mm(o, vnt[0:T, (n - n0) * 128:(n - n0 + 1) * 128], V(WSMt[0:T, l, c, 0:T], [("WSM", l)]), True, False)
                            b.mm(o, ones[0:1, :], V(BSt[0:1, l, c, 0:T], [("BSR", l)]), False, True)
                    b.tt("dve", P7.c(c)[:, 0:NT], psS[:, 0:NT], AR(c, 0)[:, 0:NT], MUL)
                    b.tt("pool", P7.c(c)[:, 0:NT], P7.c(c)[:, 0:NT], P6.c(c)[:, 0:NT], MUL)
                    b.tt("dve", P1.c(c)[:, 0:NT], P1.c(c)[:, 0:NT], P5.c(c)[:, 0:NT], MUL)
                    b.tt("pool", P1.c(c)[:, 0:NT], P1.c(c)[:, 0:NT], P7.c(c)[:, 0:NT], ADD)

                def dense_out(nblk_w, src_fn, nk, kdiv):
                    for c2 in range(8):
                        if kdiv == 8:
                            if c2 % 4 == 0: wvv = ws.get()[:, 0:4096].re("p (k n) -> p k n", k=8)
                            lw = lambda kc: wvv[:, kc, (c2 % 4) * 128:(c2 % 4 + 1) * 128]
                        else:
                            wvv = ws.get()[:, 0:4096].re("p (k n) -> p k n", k=32)
                            lw = lambda kc: wvv[:, kc, :]
                        ps = b.bank()
                        for kc in range(nk):
                            b.mm(ps[:, 0:NT], lw(kc), src_fn(kc), kc == 0, kc == nk - 1)
                        b.cp("act", F1.c(c2)[:, 0:NT], ps[:, 0:NT])

                def add_normed(gcol):
                    rs_ = rms_stats(g, F1)
                    for c in range(8):
                        t = TF(3 + c % 2)
                        b.stt(t[:, 0:NT], F1.c(c)[:, 0:NT], PRM(l, gcol + c), rs_[:, 0:NT], MUL, MUL)
                        b.tt("pool", H.c(c)[:, 0:NT], H.c(c)[:, 0:NT], t[:, 0:NT], ADD)

                dense_out(2, lambda kc: P1.c(kc)[:, 0:NT], 8, 8)
                add_normed(G_POST)

                b.s.marks.append(("Phase H t%d l%d" % (ti + 4 * sample, l), b.s.total, len(b.s.ops["pe"])))
                if ti == 0 and l == 0 and not sample: drain(4)
                for kc in range(2):
                    b.dma("pool", TB(8 + kc)[:, 0:NT], V(pT[l, :, kc, col0:col0 + NT], []), cost=8)
                rstd = rms_stats(g, H)
                for c in range(8):
                    b.stt(XN.c(c)[:, 0:NT], H.c(c)[:, 0:NT], PRM(l, F_PRE + c), rstd[:, 0:NT], MUL, MUL)
                ZA = (P4, P5, P6, P7)
                for kb in range(32):
                    if kb % 4 == 0: wv = ws.get()[:, 0:4096].re("p (k n) -> p k n", k=8)
                    ps = b.bank()
                    for kc in range(8):
                        b.mm(ps[:, 0:NT], wv[:, kc, (kb % 4) * 128:(kb % 4 + 1) * 128], XN.c(kc)[:, 0:NT], kc == 0, kc == 7)
                    rl = TB(kb % 4)
                    b.act(rl[:, 0:NT], ps[:, 0:NT], AF.Relu)
                    b.tt("pool", ZA[kb // 8].c(kb % 8)[:, 0:NT], rl[:, 0:NT], rl[:, 0:NT], MUL)
                dense_out(8, lambda kb: ZA[kb // 8].c(kb % 8)[:, 0:NT], 32, 32)
                add_normed(F_POST)

                b.s.marks.append(("Phase I t%d l%d" % (ti + 4 * sample, l), b.s.total, len(b.s.ops["pe"])))
                if ti == 0 and l == 0 and not sample: drain(0)
                for c in range(8):
                    b.cp("pool", XN.c(c)[:, 0:NT], H.c(c)[:, 0:NT])
                for c2 in range(8):
                    if c2 % 4 == 0:
                        wpg = ws.get(1)[:, 0:4096].re("p (k n) -> p k n", k=8)
                        wpe = ws.get(1)[:, 0:1024].re("p (k n) -> p k n", k=2)
                    ps1 = b.bank(); ps2 = b.bank()
                    for kc in range(8):
                        b.mm(ps1[:, 0:NT], wpg[:, kc, (c2 % 4) * 128:(c2 % 4 + 1) * 128], XN.c(kc)[:, 0:NT], kc == 0, kc == 7)
                    for kc in range(2):
                        b.mm(ps2[:, 0:NT], wpe[:, kc, (c2 % 4) * 128:(c2 % 4 + 1) * 128], TB(8 + kc)[:, 0:NT], kc == 0, kc == 1)
                    sg = TF(c2 % 2)
                    b.act(sg[:, 0:NT], ps1[:, 0:NT], AF.Sigmoid)
                    b.tt("dve", sg[:, 0:NT], ps2[:, 0:NT], sg[:, 0:NT], MUL)
                    b.tt("pool", H.c(c2)[:, 0:NT], H.c(c2)[:, 0:NT], sg[:, 0:NT], ADD)
                if not (ti == 0 and l == 0 and not sample): drain(1000)
            b.dma("sp", V(yT[:, :, col0:col0 + NT], []), V(Ht[:, :, 0:NT], H.all().keys))

        b.s.marks.append(("end", b.s.total, len(b.s.ops["pe"])))
        import os
        if os.environ.get("KMARKS"):
            for m in b.s.marks: print("MARK", m)
            lo, hi = [int(x) for x in os.environ.get("KDUMP", "0,0").split(",")]
            for e_ in b.s.log:
                if lo <= e_[0] <= hi: print("OP", e_)
            print("OPS", {e: len(b.s.ops[e]) for e in ENGS}, "waits", {e: sum(len(o.waits) for o in b.s.ops[e]) for e in ENGS})
        sch = b.s
        esem = {e: es.enter_context(nc.semaphore(f"sem_{e}")) for e in ENGS}
        dsem = [es.enter_context(nc.semaphore(f"dsem{i}")) for i in range(sch.ndma)]
        for e in ENGS:
            cnt = 0
            for op in sch.ops[e]:
                if op.sig and not op.dma:
                    cnt += 1
                op.cnt = cnt
        final = {}
        for e in ENGS:
            for op in sch.ops[e]:
                if op.dma: final[op.dsem] = max(final.get(op.dsem, 0), op.dval)

        def run(ename, e):
            for op in sch.ops[ename]:
                for d in op.waits:
                    if d.dma: e.wait_ge(dsem[d.dsem], d.dval)
                    else: e.wait_ge(esem[d.eng], d.cnt)
                ins = op.fn(e)
                if op.dma: ins.then_inc(dsem[op.dsem], 16)
                elif op.sig: ins.then_inc(esem[ename], 1)
            if ename == "sp":
                for i, v in final.items():
                    e.wait_ge(dsem[i], v)

        with nc.Block() as block:
            @block.tensor
            def _(e): run("pe", e)
            @block.scalar
            def _(e): run("act", e)
            @block.vector
            def _(e): run("dve", e)
            @block.gpsimd
            def _(e): run("pool", e)
            @block.sync
            def _(e): run("sp", e)
    return nc


def _fm(a):
    n, d = a.shape
    return np.ascontiguousarray(a.T.reshape(d // 128, 128, n).transpose(1, 0, 2))


def _consts():
    c = np.zeros((128, CW), np.float32)
    i = np.arange(128)
    c[:, C_ID:C_ID + 128] = np.eye(128)
    c[:, C_ONES:C_ONES + 128] = 1.0
    blk = (i[:, None] // 64 == i[None, :] // 64).astype(np.float32)
    c[:, C_BONES:C_BONES + 128] = blk
    c[:, C_BMEAN:C_BMEAN + 128] = blk / 64.0
    su = (i[None, :] > i[:, None]).astype(np.float32)
    iu = (i[None, :] >= i[:, None]).astype(np.float32)
    c[:, C_M4:C_M4 + 512] = np.concatenate([su, iu, su, iu], 1)
    sl = (i[None, :] < i[:, None]).astype(np.float32)
    c[:, C_ML4:C_ML4 + 512] = np.concatenate([sl] * 4, 1)
    c[:, C_ID4:C_ID4 + 512] = np.concatenate([np.eye(128, dtype=np.float32)] * 4, 1)
    rp = np.ones(512, np.float32); rp[::128] = 0.0
    rs = np.ones(64, np.float32); rs[::4] = 0.0
    c[:, C_RSP:C_RSP + 512] = rp[None]; c[:, C_RSS:C_RSS + 64] = rs[None]
    c[:, C_SH:C_SH + 64] = (i[:, None] == (np.arange(64)[None, :] + 64)).astype(np.float32)
    return c


_NC_CACHE = {}


def kernel(x_prompt, x_sample, state_wkv, state_shift, p_prompt, p_sample,
           mix_pre_g, mix_post_g, ffn_pre_g, ffn_post_g, w_in, mu_rkv, mu_wag,
           w0, w1, w2, a0, a1, a2, g1, g2, k_k, k_a, r_k, lnx_g, lnx_b,
           vn_g, vn_b, w_s, b_s, w_out, w_up, w_down, w_pe, w_pg):
    f = lambda a: np.ascontiguousarray(np.asarray(a, dtype=np.float32))
    x_prompt, x_sample, state_wkv, state_shift, p_prompt, p_sample = map(f, (x_prompt, x_sample, state_wkv, state_shift, p_prompt, p_sample))
    prm = np.zeros((128, 2, NPC), np.float32)
    def col(v): return np.asarray(v, np.float32).reshape(-1, 128).T
    for l in range(2):
        mr = np.asarray(mu_rkv[l], np.float32)
        items = [(G_PRE, mix_pre_g[l]), (G_POST, mix_post_g[l]), (F_PRE, ffn_pre_g[l]), (F_POST, ffn_post_g[l]),
                 (MU_R, mr[0:1024]), (MU_K, mr[1024:2048]), (MU_V, mr[2048:3072]),
                 (MU_W, mu_wag[l][0]), (MU_A, mu_wag[l][1]), (MU_G, mu_wag[l][2]),
                 (W0, w0[l]), (A0, a0[l]), (KK, k_k[l]), (KA, k_a[l]), (RK, np.asarray(r_k[l]).reshape(-1)),
                 (LNG, lnx_g[l]), (LNB, lnx_b[l]), (VNG, vn_g[l]), (VNB, vn_b[l])]
        for o, v in items:
            prm[:, l, o:o + 8] = col(v)
    cst = _consts()
    wsT = np.ascontiguousarray(np.asarray(w_s, np.float32).transpose(0, 3, 1, 2))
    bs = np.ascontiguousarray(np.asarray(b_s, np.float32).reshape(2, 1, 8, 128))
    shared = dict(prm=prm, cst=cst, wsT=wsT, bs=bs, w_in=f(w_in), w_out=f(w_out), w_up=f(w_up), w_down=f(w_down),
                  w_pe=f(w_pe), w_pg=f(w_pg), w1=f(w1), w2=f(w2), a1=f(a1), a2=f(a2), g1=f(g1), g2=f(g2))
    in_maps = []
    for i in range(NCORE):
        xs = x_sample[NSQ * i:NSQ * (i + 1)].reshape(NSQ * DEC, D)
        xT = np.concatenate([_fm(x_prompt[i]), _fm(xs)], axis=2)
        pTl = []
        for l in range(2):
            ps_ = p_sample[l, NSQ * i:NSQ * (i + 1)].reshape(NSQ * DEC, 256)
            pTl.append(np.concatenate([_fm(p_prompt[l, i]), _fm(ps_)], axis=2))
        pT = np.stack(pTl)
        shT = np.stack([_fm(state_shift[l, NSQ * i:NSQ * (i + 1)]) for l in range(2)])
        sw = state_wkv[:, NSQ * i:NSQ * (i + 1)].reshape(2, NSQ, 8, 2, 64, 64)
        wkvT = np.ascontiguousarray(sw.transpose(0, 3, 5, 1, 2, 4)).reshape(2, 128, NSQ, 8, 64)
        m = dict(shared); m.update(xT=np.ascontiguousarray(xT), pT=np.ascontiguousarray(pT), shT=np.ascontiguousarray(shT), wkvT=wkvT)
        in_maps.append(m)
    if "nc" not in _NC_CACHE:
        _NC_CACHE["nc"] = build_nc()
    res = run_bass_kernel_spmd(_NC_CACHE["nc"], in_maps, core_ids=list(range(NCORE)))
    R = list(res.results)
    def unfm(a):
        return np.ascontiguousarray(a.transpose(2, 1, 0).reshape(a.shape[2], -1))
    y_prompt = np.stack([unfm(R[i]["yT"][:, :, 0:SEQ]) for i in range(NCORE)])
    y_sample = np.concatenate([unfm(R[i]["yT"][:, :, SEQ:]).reshape(NSQ, DEC, D) for i in range(NCORE)])
    wkv_p = np.stack([np.stack([R[i]["wkvp"][l].reshape(2, 64, 8, 64).transpose(2, 0, 3, 1).reshape(16, 64, 64) for i in range(NCORE)]) for l in range(2)])
    sh_p = np.stack([np.stack([R[i]["shp"][l].T.reshape(D) for i in range(NCORE)]) for l in range(2)])
    wkv_s = np.stack([np.concatenate([R[i]["wkvs"][l].reshape(2, 64, NSQ, 8, 64).transpose(2, 3, 0, 4, 1).reshape(NSQ, 16, 64, 64) for i in range(NCORE)]) for l in range(2)])
    sh_s = np.stack([np.concatenate([R[i]["shs"][l].transpose(2, 1, 0).reshape(NSQ, D) for i in range(NCORE)]) for l in range(2)])
    cv_s = np.stack([np.concatenate([R[i]["cvs"][l].reshape(128, 8, NSQ, DEC).transpose(2, 3, 1, 0).reshape(NSQ, DEC, D) for i in range(NCORE)]) for l in range(2)])
    o = lambda a: np.ascontiguousarray(a, dtype=np.float32)
    return (o(y_prompt), o(y_sample), o(wkv_p), o(sh_p), o(wkv_s), o(sh_s), o(cv_s))
```

```python
import numpy as np
from contextlib import ExitStack
import concourse.bass as bass
import concourse.mybir as mybir
from concourse.bass_utils import run_bass_kernel_spmd

F32, BF16 = mybir.dt.float32, mybir.dt.bfloat16
AF = mybir.ActivationFunctionType
ALU = mybir.AluOpType
MUL, ADD, SUB, MAX = ALU.mult, ALU.add, ALU.subtract, ALU.max

D = 1024; NCORE = 8; SEQ = 2048; NSQ = 16; DEC = 4; NTOK = SEQ + NSQ * DEC
W = 516
LAM = 0.6065306597126334
EPS = 1e-6; GN_EPS = 64e-5
NBLK = 38; SLOTW = 4608
G_PRE, G_POST, F_PRE, F_POST, MU_R, MU_K, MU_V, MU_W, MU_A, MU_G = 0, 8, 16, 24, 32, 40, 48, 56, 64, 72
W0, A0, KK, KA, RK, LNG, LNB, VNG, VNB, OMM_R, OMM_K, OMM_V, OMKA, NPC = 80, 88, 96, 104, 112, 120, 128, 136, 144, 152, 160, 168, 176, 184
C_ID, C_ONES, C_BONES, C_BMEAN, C_M4, C_ML4, C_ID4, C_RSP, C_RSS, C_SH, CW = 0, 128, 256, 384, 512, 1024, 1536, 2048, 2560, 2624, 2688


class V:
    __slots__ = ("ap", "keys")

    def __init__(s, ap, keys):
        s.ap = ap; s.keys = tuple(keys)

    def __getitem__(s, i):
        return V(s.ap[i], s.keys)

    def re(s, pat, **kw):
        return V(s.ap.rearrange(pat, **kw), s.keys)

    def bc(s, shape):
        return V(s.ap.to_broadcast(list(shape)), s.keys)

    def cast(s, dt):
        return V(s.ap.bitcast(dt), s.keys)


class Op:
    __slots__ = ("eng", "fn", "idx", "sig", "waits", "dma", "dsem", "dval", "cnt")


ENGS = ("pe", "act", "dve", "pool", "sp")


class Sched:
    def __init__(s, ndma=24):
        s.ops = {e: [] for e in ENGS}
        s.last_w = {}; s.readers = {}
        s.known = {e: {} for e in ENGS}
        s.ndma = ndma; s.dma_i = 0; s.dma_last = [None] * ndma
        import os
        s.limit = int(float(os.environ.get("KLIMIT", "1e12"))); s.marks = []

    def add(s, eng, fn, r, w, dma=False, cost=256):
        s.total = getattr(s, "total", 0) + 1
        if s.total > s.limit: return None
        op = Op(); op.eng = eng; op.fn = fn; op.idx = len(s.ops[eng]); op.sig = False
        op.waits = []; op.dma = dma; op.cnt = 0; op.dsem = -1; op.dval = 0
        deps = []
        for k in r:
            d = s.last_w.get(k)
            if d is not None: deps.append(d)
        for k in w:
            d = s.last_w.get(k)
            if d is not None: deps.append(d)
            deps.extend(s.readers.get(k, ()))
        if dma:
            ph = s.__dict__.setdefault("hist_" + eng, [])
            acc = cost
            lim = 800 if eng == "pool" else 400
            for (pop, pc) in reversed(ph[-16:]):
                acc += pc
                if acc > lim:
                    deps.append(pop); break
        if dma:
            cnts = s.__dict__.setdefault("dma_cnt", {"pool": 0, "sp": 0})
            base, n = (0, 8) if eng == "pool" else (8, s.ndma - 8)
            i = cnts[eng]; cnts[eng] += 1
            slot = base + i % n; op.dsem = slot; op.dval = 16 * (i // n + 1)
            if s.dma_last[slot] is not None: deps.append(s.dma_last[slot])
            s.dma_last[slot] = op
        kn = s.known[eng]
        best = {}
        for d in deps:
            key, val = (("d", d.dsem), d.dval) if d.dma else (d.eng, d.idx)
            if key not in best or val > best[key][0]: best[key] = (val, d)
        deps = [v[1] for v in best.values()]
        for d in deps:
            if d.dma:
                key = ("d", d.dsem)
                if kn.get(key, 0) >= d.dval: continue
                kn[key] = d.dval; op.waits.append(d)
            else:
                if d.eng == "pe" and eng == "pe": continue
                if kn.get(d.eng, -1) >= d.idx: continue
                kn[d.eng] = d.idx; d.sig = True; op.waits.append(d)
        for k in r: s.readers.setdefault(k, []).append(op)
        for k in w:
            s.last_w[k] = op; s.readers[k] = []
        s.ops[eng].append(op)
        import sys as _sys
        s.__dict__.setdefault("log", []).append((s.total, eng, _sys._getframe(1).f_code.co_name, len(op.waits), tuple(w)[:2]))
        if dma: s.__dict__["hist_" + eng].append((op, cost))
        return op


class Geo:
    def __init__(s, sample):
        s.sample = sample
        if sample: s.NT, s.T, s.NCH, s.NE, s.M = 64, 4, 16, 80, 2
        else: s.NT, s.T, s.NCH, s.NE, s.M = 512, 128, 4, 513, 7
        s.CE = s.NCH * (s.T + 1)

    def cur(s, v):
        if s.sample: return v[:, 0:80].re("p (g t) -> p g t", t=5)[:, :, 1:5]
        return v[:, 1:513]

    def prev(s, v):
        if s.sample: return v[:, 0:80].re("p (g t) -> p g t", t=5)[:, :, 0:4]
        return v[:, 0:512]

    def cmp(s, v):
        if s.sample: return v[:, 0:64].re("p (g t) -> p g t", t=4)
        return v[:, 0:512]

    def c3(s, v):
        return v[:, 0:s.NT].re("p (n t) -> p n t", t=s.T)

    def ccur(s, v):
        return v[:, 0:s.CE].re("p (n t) -> p n t", t=s.T + 1)[:, :, 1:s.T + 1]

    def cprev(s, v):
        return v[:, 0:s.CE].re("p (n t) -> p n t", t=s.T + 1)[:, :, 0:s.T]


class Builder:
    def __init__(b, nc):
        b.nc = nc; b.s = Sched(); b.bank_i = 0; b.reserved = set()

    def _k(b, vs):
        ks = []
        for v in vs:
            if v is None or isinstance(v, (int, float)): continue
            ks.extend(v.keys)
        return ks

    def _a(b, v):
        return v.ap if isinstance(v, V) else v

    def mm(b, out, lhsT, rhs, start=True, stop=True):
        o, l, r = out.ap, lhsT.ap, rhs.ap
        b.s.add("pe", lambda e: e.matmul(o, lhsT=l, rhs=r, start=start, stop=stop), b._k([lhsT, rhs]), b._k([out]))

    def tr(b, out, in_, ident):
        o, i, d = out.ap, in_.ap, ident.ap
        b.s.add("pe", lambda e: e.transpose(o, i, d), b._k([in_, ident]), b._k([out]))

    def act(b, out, in_, func, bias=0.0, scale=1.0):
        o, i, bi, sc = out.ap, in_.ap, b._a(bias), b._a(scale)
        b.s.add("act", lambda e: e.activation(out=o, in_=i, func=func, bias=bi, scale=sc), b._k([in_, bias, scale]), b._k([out]))

    def tt(b, eng, out, in0, in1, op):
        o, x, y = out.ap, in0.ap, in1.ap
        b.s.add(eng, lambda e: e.tensor_tensor(out=o, in0=x, in1=y, op=op), b._k([in0, in1]), b._k([out]))

    def ts(b, eng, out, in0, s1, s2=None, op0=MUL, op1=None):
        o, x, a1, a2 = out.ap, in0.ap, b._a(s1), b._a(s2)
        if op1 is None:
            fn = lambda e: e.tensor_scalar(out=o, in0=x, scalar1=a1, scalar2=None, op0=op0)
        else:
            fn = lambda e: e.tensor_scalar(out=o, in0=x, scalar1=a1, scalar2=a2, op0=op0, op1=op1)
        b.s.add(eng, fn, b._k([in0, s1, s2]), b._k([out]))

    def stt(b, out, in0, sc, in1, op0, op1):
        o, x, a, y = out.ap, in0.ap, b._a(sc), in1.ap
        b.s.add("dve", lambda e: e.scalar_tensor_tensor(out=o, in0=x, scalar=a, in1=y, op0=op0, op1=op1), b._k([in0, sc, in1]), b._k([out]))

    def cp(b, eng, out, in_):
        o, i = out.ap, in_.ap
        if eng == "act":
            b.s.add("act", lambda e: e.activation(out=o, in_=i, func=AF.Copy), b._k([in_]), b._k([out]))
        else:
            b.s.add(eng, lambda e: e.tensor_copy(out=o, in_=i), b._k([in_]), b._k([out]))

    def recip(b, out, in_):
        o, i = out.ap, in_.ap
        b.s.add("dve", lambda e: e.reciprocal(out=o, in_=i), b._k([in_]), b._k([out]))

    def memset(b, eng, out, val):
        o = out.ap
        b.s.add(eng, lambda e: e.memset(o, val), [], b._k([out]))

    def scan(b, out, d0, d1):
        o, x, y = out.ap, d0.ap, d1.ap
        b.s.add("dve", lambda e: e.tensor_tensor_scan(out=o, data0=x, data1=y, initial=0.0, op0=MUL, op1=ADD), b._k([d0, d1]), b._k([out]))

    def dma(b, eng, out, in_, cost=None):
        if cost is None: cost = 256 if eng == "pool" else 16
        o, i = out.ap, in_.ap
        b.s.add(eng, lambda e: e.dma_start(out=o, in_=i), b._k([in_]), b._k([out]), dma=True, cost=cost)

    def bank(b, reserve=False):
        while True:
            i = b.bank_i % 8; b.bank_i += 1
            if i not in b.reserved: break
        if reserve: b.reserved.add(i)
        return V(b.ps[i][:], [("ps", i)])

    def release(b, *vs):
        for v in vs: b.reserved.discard(v.keys[0][1])


def build_nc():
    nc = bass.Bass("TRN2", target_bir_lowering=False)
    b = Builder(nc)

    def din(name, shape, dt=F32):
        return nc.dram_tensor(name, list(shape), dt, kind="ExternalInput").ap()

    def dout(name, shape, dt=F32):
        return nc.dram_tensor(name, list(shape), dt, kind="ExternalOutput").ap()

    xT = din("xT", [128, 8, NTOK]); pT = din("pT", [2, 128, 2, NTOK])
    shT = din("shT", [2, 128, 8, NSQ]); wkvT = din("wkvT", [2, 128, NSQ, 8, 64])
    prm_d = din("prm", [128, 2, NPC]); cst_d = din("cst", [128, CW])
    wsT_d = din("wsT", [2, 128, 8, 128]); bs_d = din("bs", [2, 1, 8, 128])
    w_in = din("w_in", [2, D, 7 * D]); w_out = din("w_out", [2, D, D]); w_up = din("w_up", [2, D, 4 * D])
    w_down = din("w_down", [2, 4 * D, D]); w_pe = din("w_pe", [2, 256, D]); w_pg = din("w_pg", [2, D, D])
    w1 = din("w1", [2, D, 64]); w2 = din("w2", [2, 64, D]); a1 = din("a1", [2, D, 64]); a2 = din("a2", [2, 64, D])
    g1 = din("g1", [2, D, 160]); g2 = din("g2", [2, 160, D])
    yT = dout("yT", [128, 8, NTOK]); wkvp_o = dout("wkvp", [2, 128, 8, 64]); shp_o = dout("shp", [2, 128, 8])
    wkvs_o = dout("wkvs", [2, 128, NSQ, 8, 64]); shs_o = dout("shs", [2, 128, 8, NSQ]); cvs_o = dout("cvs", [2, 128, 8, NSQ * DEC])
    scr = nc.dram_tensor("scr", [2, 2, 128, SLOTW], BF16, kind="Internal").ap()
    nW = {nm: nc.dram_tensor("n_" + nm, list(shp), BF16, kind="Internal").ap() for nm, shp in
          (("w_in", (2, D, 7 * D)), ("w_out", (2, D, D)), ("w_up", (2, D, 4 * D)), ("w_down", (2, 4 * D, D)), ("w_pg", (2, D, D)), ("w_pe", (2, 256, D)))}
    scrD = nc.dram_tensor("scrD", [2, 8, 128, 4096], BF16, kind="Internal").ap()
    srcW = {"w_in": w_in, "w_out": w_out, "w_up": w_up, "w_down": w_down, "w_pg": w_pg, "w_pe": w_pe}

    es = ExitStack()
    with es:
        def sb(name, shape, dt):
            return es.enter_context(nc.sbuf_tensor(name, list(shape), dt))

        Ht = sb("H", [128, 8, W], F32); F1t = sb("F1", [128, 8, W], F32)
        XNt = sb("XN", [128, 8, W], BF16); P1t = sb("P1", [128, 8, W], BF16)
        P4t = sb("P4", [128, 8, W], BF16); P5t = sb("P5", [128, 8, W], BF16)
        P6t = sb("P6", [128, 8, W], BF16); P7t = sb("P7", [128, 8, W], BF16)
        ARt = sb("AR", [128, 8, 2, W], BF16)
        TFt = sb("TF", [128, 5, W], F32); TBt = sb("TB", [128, 14, W], BF16)
        M4t = sb("M4", [128, 16, 4, 128], BF16); TTt = sb("TTm", [128, 16, 128], BF16)
        XBt = sb("XB", [128, 2, 4, 4, 128], BF16)
        NSLOT = 3
        WSt = [sb(f"WS{i}", [128, 4096], BF16) for i in range(NSLOT)]
        CBt = sb("CB", [128, CW], BF16); PRt = sb("PRM", [128, 2, NPC], F32)
        RKt = sb("RKO", [128, 2, 8, 128], BF16); WSMt = sb("WSM", [128, 2, 8, 128], BF16)
        BSt = sb("BSR", [1, 2, 8, 128], BF16)
        St = sb("S", [128, 2, 8, 64], F32); Sbt = sb("Sb", [128, 2, 8, 128], BF16)
        BMt = sb("BM", [128, 8, 128], BF16); KMt = sb("KM", [128, 8, 128], BF16); AMt = sb("AM", [128, 8, 128], BF16)
        IDFt = sb("IDF", [128, 128], F32)
        JNKt = sb("JNK", [128, 4], F32)
        P1flat = P1t[:].rearrange("p a w -> p (a w)")
        M4flat = M4t[:].rearrange("p h m t -> p (h m t)")
        def TBa(i): return V(M4flat[:, i * W:(i + 1) * W], [("TD", i)])
        def TFa(i): return V(M4flat[:, i * W:(i + 2) * W].bitcast(F32), [("TD", i), ("TD", i + 1)])
        XBflat = XBt[:].rearrange("p a k j t -> p (a k j t)")
        def XBa(i): return V(XBflat[:, i * W:(i + 1) * W], [("XD", i)])
        def fence_m4(col):
            b.memset("pool", V(JNKt[:, col:col + 1], [("TD", i) for i in range(15)] + [("M4", h) for h in range(16)]
                               + [("XD", i) for i in range(7)] + [("XB", s_, k_) for s_ in (0, 1) for k_ in range(4)]), 0.0)
        PTt = sb("PT", [128, 8, 16], F32)
        XCt = sb("XC", [128, 2, 8], BF16); RCt = sb("RC", [128, 2, 24], BF16)
        SHt = TFt[:, 3, 0:128].rearrange("p (c n) -> p c n", c=8); SH2t = TFt[:, 4, 0:128].rearrange("p (c n) -> p c n", c=8)
        SHPt = sb("SHP", [128, 8], F32)
        SHIt = TFt[:, 2, 0:128].rearrange("p (c n) -> p c n", c=8)
        b.ps = [es.enter_context(nc.psum_tensor(f"ps{i}", [128, 512], F32)) for i in range(8)]

        class Arr:
            def __init__(s, name, t): s.name = name; s.t = t
            def c(s, i): return V(s.t[:, i, :], [(s.name, i)])
            def all(s): return V(s.t[:], [(s.name, i) for i in range(8)])
            def rng(s, a, e): return V(s.t[:, a:e], [(s.name, i) for i in range(a, e)])

        H, F1, XN, P1, P4, P5, P6, P7 = (Arr(n, t) for n, t in (("H", Ht), ("F1", F1t), ("XN", XNt), ("P1", P1t), ("P4", P4t), ("P5", P5t), ("P6", P6t), ("P7", P7t)))

        def AR(c, j=None):
            if j is None: return V(ARt[:, c], [("AR", c)])
            return V(ARt[:, c, j, :], [("AR", c)])

        def TF(i): return V(TFt[:, i, :], [("TF", i)])
        def TB(i): return V(TBt[:, i, :], [("TB", i)])
        def TB2(i): return V(TBt[:, i:i + 2, :].rearrange("p a w -> p (a w)"), [("TB", i), ("TB", i + 1)])
        def CB(o, n): return V(CBt[:, o:o + n], [("CB",)])
        def PRM(l, col, n=1): return V(PRt[:, l, col:col + n], [("PRM", l)])
        def WS(i): return V(WSt[i][:], [("WS", i)])

        ident = CB(C_ID, 128); ones = CB(C_ONES, 128); bones = CB(C_BONES, 128); bmean = CB(C_BMEAN, 128)
        mask4 = CB(C_M4, 512).re("p (m t) -> p m t", m=4); maskL4 = CB(C_ML4, 512).re("p (m t) -> p m t", m=4)
        ident4 = CB(C_ID4, 512).re("p (m t) -> p m t", m=4)
        shiftI = CB(C_SH, 64)

        b.dma("sp", V(Ht[:, :, 0:512], [("H", i) for i in range(8)]), V(xT[:, :, 0:512], []))
        b.dma("pool", CB(0, CW), V(cst_d, []))
        b.dma("sp", V(PRt[:], [("PRM", 0), ("PRM", 1)]), V(prm_d, []))
        b.dma("sp", V(IDFt[:], [("IDF",)]), V(cst_d[:, C_ID:C_ID + 128], []))
        for l in range(2):
            b.ts("pool", PRM(l, OMM_R, 24), PRM(l, MU_R, 24), -1.0, 1.0, MUL, ADD)
            b.ts("pool", PRM(l, OMKA, 8), PRM(l, KA, 8), -1.0, 1.0, MUL, ADD)
        b.memset("pool", V(St[:], [("S", 0), ("S", 1)]), 0.0)
        b.memset("pool", V(Sbt[:], [("Sb", 0), ("Sb", 1)]), 0.0)
        for (t_, k_) in ((BMt, "BM"), (KMt, "KM"), (AMt, "AM")):
            b.memset("pool", V(t_[:], [(k_,)]), 0.0)
        b.memset("pool", V(XCt[:], [("XC", 0), ("XC", 1)]), 0.0)
        b.memset("pool", V(RCt[:], [("RC", 0), ("RC", 1)]), 0.0)

        def SCR(l, k): return V(scr[l, k], [("scr", l, k)])

        conv = [[], []]
        for l in (1, 0):
            stg = V(F1t[:].rearrange("p a w -> p (a w)"), [("F1", i) for i in range(8)])
            w1s = stg[:, 0:512].re("p (k n) -> p k n", k=8)
            a1s = stg[:, 512:1024].re("p (k n) -> p k n", k=8)
            g1s = stg[:, 1024:2304].re("p (k n) -> p k n", k=8)
            wss = stg[:, 2304:3328].re("p (g t) -> p g t", g=8)
            bss = V(TFt[:].rearrange("p a w -> p (a w)"), [("TF", i) for i in range(5)])[0:1, 0:1024].re("p (g t) -> p g t", g=8)
            b.dma("sp", w1s, V(w1[l].rearrange("(k p) n -> p k n", p=128), []))
            b.dma("sp", a1s, V(a1[l].rearrange("(k p) n -> p k n", p=128), []))
            b.dma("sp", g1s, V(g1[l].rearrange("(k p) n -> p k n", p=128), []))
            b.dma("sp", wss, V(wsT_d[l], []))
            b.dma("sp", bss, V(bs_d[l], []))
            LBflat = V(M4flat[:, 0:4608], [("TD", i) for i in range(9)])
            LB = LBflat.re("p (k n) -> p k n", k=8)
            for (src, mu, o, n) in ((w1s, MU_W, 0, 64), (a1s, MU_A, 128, 64), (g1s, MU_G, 256, 160)):
                b.cp("dve", LB[:, :, o:o + n], src)
                muv = V(PRt[:, l, mu:mu + 8].unsqueeze(2).to_broadcast([128, 8, n]), [("PRM", l)])
                b.tt("dve", LB[:, :, o + n:o + 2 * n], src, muv, MUL)
            b.dma("sp", V(scr[l, 0, :, 0:4608], [("scr", l, 0)]), LBflat)
            b.tt("dve", V(WSMt[:, l], [("WSM", l)]), wss, V(CBt[:, C_M4 + 128:C_M4 + 256].unsqueeze(1).to_broadcast([128, 8, 128]), [("CB",)]), MUL)
            b.cp("dve", V(BSt[0:1, l], [("BSR", l)]), bss)
            for c in range(8):
                b.ts("dve", V(RKt[:, l, c, :], [("RKO", l)]), bones, PRM(l, RK + c), None, MUL)
            cl = conv[l]
            def cv(dst, key, src, cost, cl=cl):
                cl.append((lambda: b.dma("pool", V(dst, [key]), V(src, []), cost=cost)))
            cv(scr[l, 1, 0:64, 0:1024], ("scr", l, 1), w2[l], 8)
            cv(scr[l, 1, 0:64, 1024:2048], ("scr", l, 1), a2[l], 8)
            cv(scr[l, 1, 0:128, 2048:3072], ("scr", l, 1), g2[l, 0:128], 8)
            cv(scr[l, 1, 0:32, 3072:4096], ("scr", l, 1), g2[l, 128:160], 8)
            def natcv(nm):
                rows = srcW[nm].shape[1]
                npart = 4 if rows >= 1024 else 1
                rp = rows // npart
                for i in range(npart):
                    cv(nW[nm][l, i * rp:(i + 1) * rp, :], ("nw", l, nm), srcW[nm][l, i * rp:(i + 1) * rp, :], 40)
            natcv("w_in"); natcv("w_out"); natcv("w_up")
            wdv = w_down[l].rearrange("(k p) n -> p k n", p=128)
            for j in range(8):
                cv(scrD[l, j].rearrange("p (k n) -> p k n", k=32), ("scrD", l, j), wdv[:, :, 128 * j:128 * (j + 1)], 258)
            natcv("w_pg"); natcv("w_pe")

        convq = conv[0] + conv[1]
        def drain(n):
            for _ in range(min(n, len(convq))):
                convq.pop(0)()
        drain(8)

        class WStream:
            def __init__(s): s.seq = []; s.issued = 0; s.used = 0
            def plan(s, items): s.seq.extend(items)
            def _issue(s):
                l, k = s.seq[s.issued]
                sl = s.issued % NSLOT
                wk = [("WS", sl)]
                def nat(nm): return nW[nm][l].rearrange("(k p) n -> p k n", p=128), [("nw", l, nm)]
                if k == 1:
                    for (np_, c0, c1) in [(64, 0, 2048), (128, 2048, 3072), (32, 3072, 4096)]:
                        b.dma("sp", V(WSt[sl][0:np_, c0:c1], wk), V(scr[l, 1, 0:np_, c0:c1], [("scr", l, 1)]))
                elif k < 8:
                    src, sk = nat("w_in")
                    dst = WSt[sl][:, 0:4096].rearrange("p (k n) -> p k n", k=8)
                    for q in range(4):
                        fb = 4 * (k - 2) + q; c, j = fb // 3, fb % 3
                        b.dma("sp", V(dst[:, :, q * 128:(q + 1) * 128], wk), V(src[:, :, j * 1024 + c * 128:j * 1024 + (c + 1) * 128], sk), cost=64)
                elif k < 26 or k in (34, 35):
                    nm, j, off = ("w_in", k - 8, 3072) if k < 16 else (("w_out", k - 16, 0) if k < 18 else (("w_up", k - 18, 0) if k < 26 else ("w_pg", k - 34, 0)))
                    src, sk = nat(nm)
                    dst = WSt[sl][:, 0:4096].rearrange("p (k n) -> p k n", k=8)
                    b.dma("sp", V(dst, wk), V(src[:, :, off + 512 * j:off + 512 * (j + 1)], sk), cost=64)
                elif k < 34:
                    j = k - 26
                    b.dma("sp", V(WSt[sl][:, 0:4096], wk), V(scrD[l, j], [("scrD", l, j)]))
                else:
                    src, sk = nat("w_pe")
                    j = k - 36
                    dst = WSt[sl][:, 0:1024].rearrange("p (k n) -> p k n", k=2)
                    b.dma("sp", V(dst, wk), V(src[:, :, 512 * j:512 * (j + 1)], sk))
                s.issued += 1
            def get(s, ahead=NSLOT - 1):
                while s.issued < min(len(s.seq), s.used + 1 + ahead): s._issue()
                v = WS(s.used % NSLOT); s.used += 1
                return v
        ws = WStream()
        ORDER = [1, 2, 3, 4, 5, 6, 7, 1, 10, 11, 8, 9, 12, 13, 14, 15, 16, 17] + list(range(18, 34)) + [34, 36, 35, 37]
        tiles = [(False, i) for i in range(4)] + [(True, 0)]
        for _t in tiles:
            for l in range(2):
                ws.plan([(l, k) for k in ORDER])

        def rms_stats(g, arr):
            NT = g.NT
            ps = b.bank()
            for c in range(8):
                sq = TB(c % 2)
                b.act(sq[:, 0:NT], arr.c(c)[:, 0:NT], AF.Square)
                b.mm(ps[:, 0:NT], ones, sq[:, 0:NT], c == 0, c == 7)
            b.act(TF(0)[:, 0:NT], ps[:, 0:NT], AF.Ln, bias=V(EPSt[:, 0:1], [("EPSC",)]), scale=1.0 / D)
            b.act(TF(1)[:, 0:NT], TF(0)[:, 0:NT], AF.Exp, scale=-0.5)
            return TF(1)

        EPSt = sb("EPSC", [128, 2], F32)
        b.memset("pool", V(EPSt[:, 0:1], [("EPSC",)]), EPS)
        b.memset("pool", V(EPSt[:, 1:2], [("EPSC",)]), GN_EPS)
        epsv = V(EPSt[:, 0:1], [("EPSC",)]); gnepsv = V(EPSt[:, 1:2], [("EPSC",)])

        def load_l1(ln):
            fence_m4(2)
            b.dma("sp", V(M4flat[:, 0:4608], [("TD", i) for i in range(9)]), V(scr[ln, 0, :, 0:4608], [("scr", ln, 0)]))
        b.s.marks.append(("init_end", b.s.total, len(b.s.ops["pe"])))
        for (sample, ti) in tiles:
            g = Geo(sample); NT, T, NCH = g.NT, g.T, g.NCH
            col0 = SEQ if sample else ti * 512
            last_prompt = (not sample) and ti == 3
            if sample or ti > 0:
                b.dma("sp", V(Ht[:, :, 0:NT], H.all().keys), V(xT[:, :, col0:col0 + NT], []), cost=64)
            for l in range(2):
                b.s.marks.append(("Phase A t%d l%d" % (ti + 4 * sample, l), b.s.total, len(b.s.ops["pe"])))
                if ti == 0 and l == 0 and not sample: drain(0)
                rstd = rms_stats(g, H)
                for c in range(8):
                    b.stt(g.cur(XN.c(c)), g.cmp(H.c(c)), PRM(l, G_PRE + c), g.cmp(rstd), MUL, MUL)
                if sample:
                    b.dma("sp", V(SHIt[:], [("TF", 2)]), V(shT[l], []))
                    b.cp("pool", V(XNt[:, :, 0:80].rearrange("p c (g t) -> p c g t", t=5)[:, :, :, 0], XN.all().keys), V(SHIt[:], [("TF", 2)]))
                    hv = V(Ht[:, :, 0:64].rearrange("p c (g t) -> p c g t", t=4)[:, :, :, 3], H.all().keys)
                    b.tt("pool", V(SHt[:], [("TF", 3)]), hv, V(PRt[:, l, G_PRE:G_PRE + 8].unsqueeze(2).to_broadcast([128, 8, NSQ]), [("PRM", l)]), MUL)
                    rv = V(TFt[:, 1, 0:64].rearrange("p (g t) -> p g t", t=4)[:, :, 3].unsqueeze(1).to_broadcast([128, 8, NSQ]), [("TF", 1)])
                    b.tt("pool", V(SH2t[:], [("TF", 4)]), V(SHt[:], [("TF", 3)]), rv, MUL)
                    b.dma("sp", V(shs_o[l], []), V(SH2t[:], [("TF", 4)]))
                else:
                    b.cp("pool", V(XNt[:, :, 0:1], XN.all().keys), V(XCt[:, l, :].unsqueeze(2), [("XC", l)]))
                    b.cp("pool", V(XCt[:, l, :].unsqueeze(2), [("XC", l)]), V(XNt[:, :, 512:513], XN.all().keys))
                    if last_prompt:
                        b.tt("pool", V(SHt[:, :, 0:1], [("TF", 3)]), V(Ht[:, :, 511:512], H.all().keys), V(PRt[:, l, G_PRE:G_PRE + 8].unsqueeze(2), [("PRM", l)]), MUL)
                        b.ts("pool", V(SHPt[:], [("SHP",)]), V(SHt[:, :, 0], [("TF", 3)]), rstd[:, 511:512], None, MUL)
                        b.dma("sp", V(shp_o[l], []), V(SHPt[:], [("SHP",)]))
                for c in range(8):
                    b.tt("pool", g.cmp(P4.c(c)), g.prev(XN.c(c)), g.cur(XN.c(c)), SUB)
                b.memset("pool", V(F1t[:, :, 0:g.CE].rearrange("p c (n t) -> p c n t", t=T + 1)[:, :, :, 0], F1.all().keys), 0.0)

                b.s.marks.append(("Phase B t%d l%d" % (ti + 4 * sample, l), b.s.total, len(b.s.ops["pe"])))
                if ti == 0 and not sample: drain(4)
                L1f = V(M4flat[:, 0:4608], [("TD", i) for i in range(9)])
                L1 = L1f.re("p (k n) -> p k n", k=8)
                def lora1(o, n, m0, m1, ps):
                    for kc in range(8):
                        b.mm(g.cmp(ps[0:m1 - m0, :]), L1[:, kc, o + m0:o + m1], g.cur(XN.c(kc)), kc == 0, False)
                        b.mm(g.cmp(ps[0:m1 - m0, :]), L1[:, kc, o + n + m0:o + n + m1], g.cmp(P4.c(kc)), False, kc == 7)
                ps = b.bank(); lora1(0, 64, 0, 64, ps)
                b.act(TB(2)[0:64, 0:NT], ps[0:64, 0:NT], AF.Tanh)
                ps = b.bank(); lora1(128, 64, 0, 64, ps)
                b.cp("dve", TB(3)[0:64, 0:NT], ps[0:64, 0:NT])
                ps = b.bank(); lora1(256, 160, 0, 128, ps)
                b.act(TB(4)[:, 0:NT], ps[:, 0:NT], AF.Sigmoid)
                ps = b.bank(); lora1(256, 160, 128, 160, ps)
                b.act(TB(5)[0:32, 0:NT], ps[0:32, 0:NT], AF.Sigmoid)
                L2 = ws.get()
                rsm = CB(C_RSS, 64) if sample else CB(C_RSP, 512)
                for c in range(8):
                    ps = b.bank()
                    b.mm(ps[:, 0:NT], L2[0:64, c * 128:(c + 1) * 128], TB(2)[0:64, 0:NT])
                    b.act(TF(2 + c % 2)[:, 0:NT], ps[:, 0:NT], AF.Sigmoid, bias=PRM(l, W0 + c))
                    for n in range(NCH):
                        b.scan(g.ccur(F1.c(c))[:, n, :], ones[:, 0:T], TF(2 + c % 2)[:, n * T:(n + 1) * T])
                    ps = b.bank()
                    b.mm(ps[:, 0:NT], L2[0:64, 1024 + c * 128:1024 + (c + 1) * 128], TB(3)[0:64, 0:NT])
                    b.act(P1.c(c)[:, 0:NT], ps[:, 0:NT], AF.Sigmoid, bias=PRM(l, A0 + c))

                b.s.marks.append(("Phase C/D t%d l%d" % (ti + 4 * sample, l), b.s.total, len(b.s.ops["pe"])))
                if ti == 0 and l == 0 and not sample: drain(0)
                fence_m4(2)
                wcur = None
                def rkv_block(fb):
                    nonlocal wcur
                    if fb % 4 == 0: wcur = ws.get()
                    return wcur[:, 0:4096].re("p (k n) -> p k n", k=8)[:, :, (fb % 4) * 128:(fb % 4 + 1) * 128]
                def proj_rkv(c, j, outv, eb):
                    wv = rkv_block(3 * c + j)
                    ps = b.bank()
                    NE = g.NE if sample else 512
                    rhs = (lambda kc: XN.c(kc)[:, 0:80]) if sample else (lambda kc: XN.c(kc)[:, 1:513])
                    for kc in range(8):
                        b.mm(ps[:, 0:NE], wv[:, kc, :], rhs(kc), kc == 0, kc == 7)
                    mu = PRM(l, MU_R + 8 * j + c); om = PRM(l, OMM_R + 8 * j + c)
                    if sample:
                        b.act(eb[:, 0:80], ps[:, 0:80], AF.Copy, scale=mu)
                        raw = ps[:, 0:80].re("p (g t) -> p g t", t=5)[:, :, 1:5]
                    else:
                        b.act(eb[:, 1:513], ps[:, 0:512], AF.Copy, scale=mu)
                        rc = V(RCt[:, l, 8 * j + c:8 * j + c + 1], [("RC", l)])
                        b.cp("pool", eb[:, 0:1], rc)
                        b.cp("pool", rc, eb[:, 512:513])
                        raw = ps[:, 0:512]
                    b.stt(g.cmp(outv), raw, om, g.prev(eb), MUL, ADD)
                def d_temps(c):
                    s3 = c % 3
                    if s3 == 0: Pe, Pinv, rr, kraw, ebR, ebK = TB(6), TB(7), TB(8), TB(9), TB(0), TB(1)
                    elif s3 == 1: Pe, Pinv, rr, kraw, ebR, ebK = TBa(0), TBa(1), TBa(2), TBa(3), TBa(14), V(TFt[:, 2, :].bitcast(BF16)[:, 0:W], [("TF", 2)])
                    else: Pe, Pinv, rr, kraw, ebR, ebK = XBa(0), XBa(1), XBa(2), XBa(3), XBa(4), XBa(5)
                    if c % 2 == 0: sqk, kkn, f, kp, rk, bb, sdT, rsT = TB(10), TB(11), TB(12), TB(13), TB(2), TB(3), TF(0), TF(1)
                    else: sqk, kkn, f, kp, rk, bb, sdT, rsT = TBa(4), TBa(5), TBa(6), TBa(7), TBa(8), TBa(9), TFa(10), TFa(12)
                    return Pe, Pinv, rr, kraw, sqk, kkn, f, kp, rk, bb, sdT, rsT, ebR, ebK
                def d_stage1(c):
                    Pe, Pinv, rr, kraw, sqk, kkn, f, kp, rk, bb, sdT, rsT, ebR, ebK = d_temps(c)
                    b.act(Pe[:, 0:g.CE], F1.c(c)[:, 0:g.CE], AF.Exp, scale=-LAM)
                    b.act(g.c3(Pinv), g.ccur(F1.c(c)), AF.Exp, scale=LAM)
                    endv = V(F1t[:, c, 0:g.CE].rearrange("p (n t) -> p n t", t=T + 1)[:, :, T], [("F1", c)])
                    b.act(V(PTt[:, c, 0:NCH], [("PT",)]), endv, AF.Exp, scale=-LAM)
                    proj_rkv(c, 0, rr, ebR)
                    proj_rkv(c, 1, kraw, ebK)
                    proj_rkv(c, 2, P6.c(c), ebR)
                def d_stage2a(c):
                    Pe, Pinv, rr, kraw, sqk, kkn, f, kp, rk, bb, sdT, rsT, ebR, ebK = d_temps(c)
                    b.tt("dve", g.c3(AR(c, 1)), g.c3(rr), g.ccur(Pe), MUL)
                    b.act(sqk[:, 0:NT], kraw[:, 0:NT], AF.Square, scale=PRM(l, KK + c))
                    ps = b.bank()
                    b.mm(ps[:, 0:NT], bones, sqk[:, 0:NT])
                    b.ts("dve", sdT[:, 0:NT], ps[:, 0:NT], 1e-24, None, MAX)
                    b.act(sdT[:, 0:NT], sdT[:, 0:NT], AF.Ln)
                    b.act(rsT[:, 0:NT], sdT[:, 0:NT], AF.Exp, scale=-0.5)
                    b.act(f[:, 0:NT], P1.c(c)[:, 0:NT], AF.Identity, bias=PRM(l, OMKA + c), scale=PRM(l, KA + c))
                    b.stt(kkn[:, 0:NT], kraw[:, 0:NT], PRM(l, KK + c), rsT[:, 0:NT], MUL, MUL)
                def d_stage2b(c):
                    Pe, Pinv, rr, kraw, sqk, kkn, f, kp, rk, bb, sdT, rsT, ebR, ebK = d_temps(c)
                    b.tt("pool", kp[:, 0:NT], kraw[:, 0:NT], f[:, 0:NT], MUL)
                    b.tt("pool", P4.c(c)[:, 0:NT], kp[:, 0:NT], Pinv[:, 0:NT], MUL)
                    b.tt("dve", bb[:, 0:NT], kkn[:, 0:NT], P1.c(c)[:, 0:NT], MUL)
                    b.tt("pool", P5.c(c)[:, 0:NT], bb[:, 0:NT], Pinv[:, 0:NT], MUL)
                    b.stt(g.c3(AR(c, 0)), g.c3(kkn), -1.0, g.cprev(Pe), MUL, MUL)
                    b.tt("pool", rk[:, 0:NT], rr[:, 0:NT], kp[:, 0:NT], MUL)
                def d_stage2c(c):
                    Pe, Pinv, rr, kraw, sqk, kkn, f, kp, rk, bb, sdT, rsT, ebR, ebK = d_temps(c)
                    ps2 = b.bank()
                    b.mm(ps2[:, 0:NT], V(RKt[:, l, c, :], [("RKO", l)]), rk[:, 0:NT])
                    b.tt("dve", P7.c(c)[:, 0:NT], ps2[:, 0:NT], P6.c(c)[:, 0:NT], MUL)

                d_stage1(0); d_stage1(1); d_stage2a(0)
                for c in range(8):
                    if ti == 0 and not sample: drain(1)
                    if c + 2 < 8: d_stage1(c + 2)
                    if c + 1 < 8: d_stage2a(c + 1)
                    d_stage2b(c)
                    if c >= 1: d_stage2c(c - 1)
                d_stage2c(7)
                b.s.marks.append(("Phase E t%d l%d" % (ti + 4 * sample, l), b.s.total, len(b.s.ops["pe"])))
                if ti == 0 and l == 0 and not sample: drain(0)
                fence_m4(3)
                vt, kt, bt, Wb, Ub = TB2(0), TB2(2), TB2(6), TB2(8), TB2(10)
                Ystg = V(TFt[:, 3:5, :].rearrange("p a w -> p (a w)"), [("TF", 3), ("TF", 4)])
                def chunk_gen(n):
                    cs = slice(n * T, (n + 1) * T)
                    si = (n % 2) if sample else l
                    Sv = V(St[:, si], [("S", si)])
                    def sbd_update():
                        b.cp("act", V(Sbt[0:64, si, :, 0:64], [("Sb", si)]), V(St[0:64, si], [("S", si)]))
                        b.cp("act", V(Sbt[64:128, si, :, 64:128], [("Sb", si)]), V(St[64:128, si], [("S", si)]))
                    def Sbd(c): return V(Sbt[:, si, c, :], [("Sb", si)])
                    if sample:
                        b.dma("sp", Sv, V(wkvT[l, :, n], []))
                        sbd_update()
                    def XBs(sidx, k, j=None):
                        if sidx < 2:
                            t_ = XBt[:, sidx, k] if j is None else XBt[:, sidx, k, j]
                        else:
                            base = P1flat[:, (sidx - 2) * 2048:(sidx - 1) * 2048].rearrange("p (k j t) -> p k j t", k=4, j=4)
                            t_ = base[:, k] if j is None else base[:, k, j]
                        return V(t_, [("XB", sidx, k)])
                    if n == 0:
                        b.memset("pool", V(JNKt[:, 0:1], list(P1.all().keys) + [("XB", s_, k_) for s_ in (2, 3) for k_ in range(4)]), 0.0)
                    b.cp("act", V(BMt[64:128, :, 0:T], [("BM",)]), V(P5t[64:128, :, cs], P5.all().keys))
                    b.cp("dve", V(KMt[64:128, :, 0:T], [("KM",)]), V(P4t[64:128, :, cs], P4.all().keys))
                    b.cp("act", V(AMt[64:128, :, 0:T], [("AM",)]), V(ARt[64:128, :, 0, cs], [("AR", c_) for c_ in range(8)]))
                    for hg in range(4):
                        psLr = b.bank(True)
                        psL = psLr.re("p (m t) -> p m t", m=4)
                        for j in range(4):
                            h = 4 * hg + j; c = h // 2
                            psG = b.bank().re("p (m t) -> p m t", m=4)
                            if h % 2 == 0:
                                rhs = AR(c)[0:64, :, cs]
                                b.mm(psG[0:T, 0:2, 0:T], P5.c(c)[0:64, cs], rhs)
                                b.mm(psG[0:T, 2:4, 0:T], P4.c(c)[0:64, cs], rhs)
                                b.tt("dve", V(M4t[0:T, h, :, 0:T], [("M4", h)]), psG[0:T, :, 0:T], mask4[0:T, :, 0:T], MUL)
                                b.mm(psL[0:T, j, 0:T], AR(c, 0)[0:64, cs], P5.c(c)[0:64, cs])
                            else:
                                rhs = AR(c)[:, :, cs]
                                b.mm(psG[0:T, 0:2, 0:T], V(BMt[:, c, 0:T], [("BM",)]), rhs)
                                b.mm(psG[0:T, 2:4, 0:T], V(KMt[:, c, 0:T], [("KM",)]), rhs)
                                tmpm = TB(12 + c % 2)[0:T, 0:512].re("p (m t) -> p m t", m=4)[:, :, 0:T]
                                b.cp("act", tmpm, psG[0:T, :, 0:T])
                                b.tt("pool", V(M4t[0:T, h, :, 0:T], [("M4", h)]), tmpm, mask4[0:T, :, 0:T], MUL)
                                b.mm(psL[0:T, j, 0:T], V(AMt[:, c, 0:T], [("AM",)]), P5.c(c)[:, cs])
                        b.tt("dve", XBs(hg, 0)[0:T, :, 0:T], psL[0:T, :, 0:T], maskL4[0:T, :, 0:T], MUL)
                        b.release(psLr)
                        hs = slice(4 * hg, 4 * hg + 4)
                        hk = [("M4", h) for h in range(4 * hg, 4 * hg + 4)]
                        b.tt("pool", V(TTt[0:T, hs, 0:T], [("TTm", hg)]), V(M4t[0:T, hs, 0, 0:T], hk), ident4[0:T, :, 0:T], ADD)
                    for (src, dst, eng) in ((P6, vt, "act"), (P4, kt, "dve"), (P5, bt, "act")):
                        psb = b.bank().cast(BF16)
                        for c in range(8):
                            b.tr(psb[0:T, c * 128:(c + 1) * 128], src.c(c)[:, cs], ident)
                        b.cp(eng, dst[0:T, 0:1024], psb[0:T, 0:1024])
                    yield 1
                    Xc = {hg: (lambda hg: (lambda j: XBs(hg, 0, j)[0:T, 0:T]))(hg) for hg in range(4)}
                    XTc = {hg: (lambda hg: (lambda j: V(M4t[0:T, 4 * hg + j, 0, 0:T], [("M4", 4 * hg + j)])))(hg) for hg in range(4)}
                    for k in range(1, g.M):
                        kx = k % 2
                        for hg in range(4):
                            psX = b.bank().re("p (m t) -> p m t", m=4)
                            for j in range(4):
                                b.mm(psX[0:T, j, 0:T], XTc[hg](j), Xc[hg](j))
                            b.cp("act", XBs(hg, kx)[0:T, :, 0:T], psX[0:T, :, 0:T])
                        if k < g.M - 1:
                            for hg in range(4):
                                psXT = b.bank().re("p (m t) -> p m t", m=4)
                                for j in range(4):
                                    b.mm(psXT[0:T, j, 0:T], Xc[hg](j), XTc[hg](j))
                                b.cp("dve" if hg % 2 == 0 else "act", XBs(hg, 2 + kx)[0:T, :, 0:T], psXT[0:T, :, 0:T])
                        for hg in range(4):
                            Xc[hg] = (lambda hg, kx: (lambda j: XBs(hg, kx, j)[0:T, 0:T]))(hg, kx)
                            XTc[hg] = (lambda hg, kx: (lambda j: XBs(hg, 2 + kx, j)[0:T, 0:T]))(hg, kx)
                        for hg in range(4):
                            psT = b.bank().re("p (m t) -> p m t", m=4)
                            TTv = V(TTt[0:T, 4 * hg:4 * hg + 4, 0:T], [("TTm", hg)])
                            for j in range(4):
                                b.mm(psT[0:T, j, 0:T], Xc[hg](j), V(TTt[0:T, 4 * hg + j, 0:T], [("TTm", hg)]))
                            b.tt("dve", TTv, psT[0:T, :, 0:T], TTv, ADD)
                    if n == NCH - 1:
                        b.memset("pool", V(JNKt[:, 1:2], list(P1.all().keys) + [("XB", s_, k_) for s_ in (2, 3) for k_ in range(4)]), 0.0)
                    yield 2
                    def hsl(h): return slice(h * 64, (h + 1) * 64)
                    psW = [b.bank(), b.bank()]
                    for c in range(8):
                        o = psW[c // 4][0:T, (c % 4) * 128:(c % 4 + 1) * 128]
                        b.mm(o, AR(c, 0)[:, cs], Sbd(c), True, False)
                        for h2 in range(2):
                            h = 2 * c + h2
                            oo = psW[c // 4][0:T, (c % 4) * 128 + h2 * 64:(c % 4) * 128 + h2 * 64 + 64]
                            b.mm(oo, V(M4t[0:T, h, 2, 0:T], [("M4", h)]), vt[0:T, hsl(h)], False, h2 == 1)
                    b.cp("act", Wb[0:T, 0:512], psW[0][0:T, :])
                    b.cp("dve", Wb[0:T, 512:1024], psW[1][0:T, :])
                    psU = [b.bank(), b.bank()]
                    for h in range(16):
                        b.mm(psU[h // 8][0:T, (h % 8) * 64:(h % 8 + 1) * 64], V(TTt[0:T, h, 0:T], [("TTm", h // 4)]), Wb[0:T, hsl(h)])
                    b.cp("act", Ub[0:T, 0:512], psU[0][0:T, :])
                    b.cp("dve", Ub[0:T, 512:1024], psU[1][0:T, :])
                    psY = [b.bank(True), b.bank(True)]
                    for c in range(8):
                        o = psY[c // 4][0:T, (c % 4) * 128:(c % 4 + 1) * 128]
                        b.mm(o, AR(c, 1)[:, cs], Sbd(c), True, False)
                        for h2 in range(2):
                            h = 2 * c + h2
                            oo = psY[c // 4][0:T, (c % 4) * 128 + h2 * 64:(c % 4) * 128 + h2 * 64 + 64]
                            b.mm(oo, V(M4t[0:T, h, 1, 0:T], [("M4", h)]), Ub[0:T, hsl(h)], False, False)
                            b.mm(oo, V(M4t[0:T, h, 3, 0:T], [("M4", h)]), vt[0:T, hsl(h)], False, h2 == 1)
                    psDr = [b.bank(True), b.bank(True)]
                    psD = [psDr[0].re("p (m t) -> p m t", m=4), psDr[1].re("p (m t) -> p m t", m=4)]
                    for c in range(8):
                        o = psD[c // 4][:, c % 4, :]
                        csl = slice(c * 128, (c + 1) * 128)
                        b.mm(o, bt[0:T, csl], Ub[0:T, csl], True, False)
                        b.mm(o, kt[0:T, csl], vt[0:T, csl], False, True)
                    yield 3
                    b.cp("act", Ystg[0:T, 0:512], psY[0][0:T, :])
                    b.cp("dve", Ystg[0:T, 512:1024], psY[1][0:T, :])
                    b.release(psY[0], psY[1])
                    psYT = [b.bank().re("p (m t) -> p m t", m=4), b.bank().re("p (m t) -> p m t", m=4)]
                    for c in range(8):
                        b.tr(psYT[c // 4][:, c % 4, 0:T], Ystg[0:T, c * 128:(c + 1) * 128], V(IDFt[0:T, 0:T], [("IDF",)]))
                    b.cp("act", V(F1t[:, 0:4, cs], F1.rng(0, 4).keys), psYT[0][:, :, 0:T])
                    b.cp("dve", V(F1t[:, 4:8, cs], F1.rng(4, 8).keys), psYT[1][:, :, 0:T])
                    for bk in range(2):
                        for hh in range(2):
                            pp = slice(64 * hh, 64 * hh + 64)
                            sv = V(St[pp, si, 4 * bk:4 * bk + 4, :], [("S", si)])
                            b.tt("dve", sv, psD[bk][pp, :, 64 * hh:64 * hh + 64], sv, ADD)
                    b.release(psDr[0], psDr[1])
                    ptv = V(PTt[:, :, n:n + 1].to_broadcast([128, 8, 64]), [("PT",)])
                    b.tt("pool", Sv, Sv, ptv, MUL)
                    if sample:
                        b.dma("sp", V(wkvs_o[l, :, n], []), Sv)
                    else:
                        sbd_update()
                        if last_prompt and n == NCH - 1:
                            b.dma("sp", V(wkvp_o[l], []), Sv)

                gens = [chunk_gen(n) for n in range(NCH)]
                next(gens[0])
                for n in range(NCH):
                    if ti == 0 and not sample: drain(2)
                    next(gens[n]); next(gens[n])
                    if n + 1 < NCH: next(gens[n + 1])
                    for _ in gens[n]: pass
                b.s.marks.append(("Phase F t%d l%d" % (ti + 4 * sample, l), b.s.total, len(b.s.ops["pe"])))
                if ti == 0 and not sample: drain(5 if l == 0 else 1000)
                fence_m4(2)
                L2 = ws.get()
                def f_temps(c):
                    return (TB(6), TB(7), TF(0), TF(1), TF(2), TF(3), TF(4)) if c % 2 == 0 else (TBa(0), TBa(1), TFa(2), TFa(4), TFa(6), TFa(8), TFa(10))
                fps = {}
                def f_a(c):
                    ybf, y2, f0, f1, f2, f3, f4 = f_temps(c)
                    b.cp("pool", ybf[:, 0:NT], F1.c(c)[:, 0:NT])
                    b.act(y2[:, 0:NT], F1.c(c)[:, 0:NT], AF.Square)
                    psM = b.bank(True); psE = b.bank()
                    fps[c] = psM
                    b.mm(psM[:, 0:NT], bmean, ybf[:, 0:NT])
                    b.mm(psE[:, 0:NT], bmean, y2[:, 0:NT])
                    b.act(f0[:, 0:NT], psM[:, 0:NT], AF.Square)
                    b.tt("dve", f1[:, 0:NT], psE[:, 0:NT], f0[:, 0:NT], SUB)
                    b.ts("dve", f1[:, 0:NT], f1[:, 0:NT], 0.0, None, MAX)
                    b.act(f1[:, 0:NT], f1[:, 0:NT], AF.Ln, bias=gnepsv)
                    b.act(f2[:, 0:NT], f1[:, 0:NT], AF.Exp, scale=-0.5)
                def f_b(c):
                    ybf, y2, f0, f1, f2, f3, f4 = f_temps(c)
                    psM = fps.pop(c)
                    b.tt("dve", f3[:, 0:NT], F1.c(c)[:, 0:NT], psM[:, 0:NT], SUB)
                    b.release(psM)
                    b.tt("pool", f4[:, 0:NT], f3[:, 0:NT], f2[:, 0:NT], MUL)
                    b.stt(f3[:, 0:NT], f4[:, 0:NT], PRM(l, LNG + c), P7.c(c)[:, 0:NT], MUL, ADD)
                    psg = b.bank()
                    b.mm(psg[:, 0:NT], L2[:, 2048 + c * 128:2048 + (c + 1) * 128], TB(4)[:, 0:NT], True, False)
                    b.mm(psg[:, 0:NT], L2[0:32, 3072 + c * 128:3072 + (c + 1) * 128], TB(5)[0:32, 0:NT], False, True)
                    b.stt(P1.c(c)[:, 0:NT], f3[:, 0:NT], PRM(l, LNB + c), psg[:, 0:NT], ADD, MUL)
                def proj512(wv, c4, ps):
                    for kc in range(8):
                        b.mm(g.cmp(ps), wv[:, kc, c4 * 128:(c4 + 1) * 128], g.cur(XN.c(kc)), kc == 0, kc == 7)
                gM = b.bank(True); gE = b.bank(True)
                gvw = [None]
                def gv_step(c):
                    if c % 4 == 0: gvw[0] = ws.get(1 if c == 0 else 0)[:, 0:4096].re("p (k n) -> p k n", k=8)
                    ps = b.bank(); proj512(gvw[0], c % 4, ps)
                    b.act(AR(c, 1)[:, 0:NT], ps[:, 0:NT], AF.Gelu)
                    gg2 = TB(c % 2)
                    b.tt("pool", gg2[:, 0:NT], AR(c, 1)[:, 0:NT], AR(c, 1)[:, 0:NT], MUL)
                    b.mm(gM[:, 0:NT], ones, AR(c, 1)[:, 0:NT], c == 0, c == 7)
                    b.mm(gE[:, 0:NT], ones, gg2[:, 0:NT], c == 0, c == 7)
                f_a(0)
                for c in range(8):
                    if c % 2 == 0:
                        gv_step(c); gv_step(c + 1)
                    if c + 1 < 8: f_a(c + 1)
                    f_b(c)
                if not (sample and l == 1): load_l1(1 - l)
                b.s.marks.append(("Phase G t%d l%d" % (ti + 4 * sample, l), b.s.total, len(b.s.ops["pe"])))
                if ti == 0 and l == 0 and not sample: drain(4)
                b.act(TF(0)[:, 0:NT], gM[:, 0:NT], AF.Square, scale=1.0 / D)
                b.stt(TF(1)[:, 0:NT], gE[:, 0:NT], 1.0 / D, TF(0)[:, 0:NT], MUL, SUB)
                b.ts("dve", TF(1)[:, 0:NT], TF(1)[:, 0:NT], 0.0, None, MAX)
                b.act(TF(1)[:, 0:NT], TF(1)[:, 0:NT], AF.Ln, bias=epsv)
                b.act(TF(0)[:, 0:NT], TF(1)[:, 0:NT], AF.Exp, scale=-0.5)
                b.stt(TF(2)[:, 0:NT], gM[:, 0:NT], 1.0 / D, TF(0)[:, 0:NT], MUL, MUL)
                b.release(gM, gE)
                for c in range(8):
                    if c % 4 == 0: wv = ws.get()[:, 0:4096].re("p (k n) -> p k n", k=8)
                    ps = b.bank(); proj512(wv, c % 4, ps)
                    b.act(AR(c, 0)[:, 0:NT], ps[:, 0:NT], AF.Gelu)
                for (arr_, nm) in ((P5, "ga"), (P6, "gb")):
                    for c in range(8):
                        if c % 4 == 0: wv = ws.get()[:, 0:4096].re("p (k n) -> p k n", k=8)
                        ps = b.bank(); proj512(wv, c % 4, ps)
                        b.act(arr_.c(c)[:, 0:NT], ps[:, 0:NT], AF.Sigmoid)
                for c in range(8):
                    t = TF(3 + c % 2)
                    b.tt("dve", t[:, 0:NT], AR(c, 1)[:, 0:NT], TF(0)[:, 0:NT], MUL)
                    b.tt("pool", t[:, 0:NT], t[:, 0:NT], TF(2)[:, 0:NT], SUB)
                    if sample:
                        b.act(t[:, 0:NT], t[:, 0:NT], AF.Identity, bias=PRM(l, VNB + c), scale=PRM(l, VNG + c))
                        b.cp("dve", P4.c(c)[:, 0:NT], t[:, 0:NT])
                        b.dma("sp", V(cvs_o[l, :, c, :], []), t[:, 0:NT])
                    else:
                        b.act(P4.c(c)[:, 0:NT], t[:, 0:NT], AF.Identity, bias=PRM(l, VNB + c), scale=PRM(l, VNG + c))
                for c in range(8):
                    vnt = TB2(0) if c % 2 == 0 else TB2(2)
                    psS = b.bank()
                    for n0 in range(0, NCH, 8):
                        nn = min(8, NCH - n0)
                        psb = b.bank().cast(BF16)
                        for n in range(n0, n0 + nn):
                            b.tr(psb[0:T, (n - n0) * 128:(n - n0 + 1) * 128], P4.c(c)[:, n * T:(n + 1) * T], ident)
                        b.cp("act", vnt[0:T, 0:nn * 128], psb[0:T, 0:nn * 128])
                        for n in range(n0, n0 + nn):
                            o = psS[:, n * T:(n + 1) * T]
                            b.mm(o, vnt[0:T, (n - n0) * 128:(n - n0 + 1) * 128], V(WSMt[0:T, l, c, 0:T], [("WSM", l)]), True, False)
                            b.mm(o, ones[0:1, :], V(BSt[0:1, l, c, 0:T], [("BSR", l)]), False, True)
                    b.tt("dve", P7.c(c)[:, 0:NT], psS[:, 0:NT], AR(c, 0)[:, 0:NT], MUL)
                    b.tt("pool", P7.c(c)[:, 0:NT], P7.c(c)[:, 0:NT], P6.c(c)[:, 0:NT], MUL)
                    b.tt("dve", P1.c(c)[:, 0:NT], P1.c(c)[:, 0:NT], P5.c(c)[:, 0:NT], MUL)
                    b.tt("pool", P1.c(c)[:, 0:NT], P1.c(c)[:, 0:NT], P7.c(c)[:, 0:NT], ADD)

                def dense_out(nblk_w, src_fn, nk, kdiv):
                    for c2 in range(8):
                        if kdiv == 8:
                            if c2 % 4 == 0: wvv = ws.get()[:, 0:4096].re("p (k n) -> p k n", k=8)
                            lw = lambda kc: wvv[:, kc, (c2 % 4) * 128:(c2 % 4 + 1) * 128]
                        else:
                            wvv = ws.get()[:, 0:4096].re("p (k n) -> p k n", k=32)
                            lw = lambda kc: wvv[:, kc, :]
                        ps = b.bank()
                        for kc in range(nk):
                            b.mm(ps[:, 0:NT], lw(kc), src_fn(kc), kc == 0, kc == nk - 1)
                        b.cp("act", F1.c(c2)[:, 0:NT], ps[:, 0:NT])

                def add_normed(gcol):
                    rs_ = rms_stats(g, F1)
                    for c in range(8):
                        t = TF(3 + c % 2)
                        b.stt(t[:, 0:NT], F1.c(c)[:, 0:NT], PRM(l, gcol + c), rs_[:, 0:NT], MUL, MUL)
                        b.tt("pool", H.c(c)[:, 0:NT], H.c(c)[:, 0:NT], t[:, 0:NT], ADD)

                dense_out(2, lambda kc: P1.c(kc)[:, 0:NT], 8, 8)
                add_normed(G_POST)

                b.s.marks.append(("Phase H t%d l%d" % (ti + 4 * sample, l), b.s.total, len(b.s.ops["pe"])))
                if ti == 0 and l == 0 and not sample: drain(4)
                rstd = rms_stats(g, H)
                for c in range(8):
                    b.stt(XN.c(c)[:, 0:NT], H.c(c)[:, 0:NT], PRM(l, F_PRE + c), rstd[:, 0:NT], MUL, MUL)
                ZA = (P4, P5, P6, P7)
                for kb in range(32):
                    if kb % 4 == 0: wv = ws.get()[:, 0:4096].re("p (k n) -> p k n", k=8)
                    ps = b.bank()
                    for kc in range(8):
                        b.mm(ps[:, 0:NT], wv[:, kc, (kb % 4) * 128:(kb % 4 + 1) * 128], XN.c(kc)[:, 0:NT], kc == 0, kc == 7)
                    rl = TB(kb % 4)
                    b.act(rl[:, 0:NT], ps[:, 0:NT], AF.Relu)
                    b.tt("pool", ZA[kb // 8].c(kb % 8)[:, 0:NT], rl[:, 0:NT], rl[:, 0:NT], MUL)
                dense_out(8, lambda kb: ZA[kb // 8].c(kb % 8)[:, 0:NT], 32, 32)
                add_normed(F_POST)

                b.s.marks.append(("Phase I t%d l%d" % (ti + 4 * sample, l), b.s.total, len(b.s.ops["pe"])))
                if ti == 0 and l == 0 and not sample: drain(0)
                for c in range(8):
                    b.cp("pool", XN.c(c)[:, 0:NT], H.c(c)[:, 0:NT])
                for kc in range(2):
                    b.dma("sp", TF(3 + kc)[:, 0:NT], V(pT[l, :, kc, col0:col0 + NT], []))
                    b.cp("pool", TB(8 + kc)[:, 0:NT], TF(3 + kc)[:, 0:NT])
                for c2 in range(8):
                    if c2 % 4 == 0:
                        wpg = ws.get(1)[:, 0:4096].re("p (k n) -> p k n", k=8)
                        wpe = ws.get(1)[:, 0:1024].re("p (k n) -> p k n", k=2)
                    ps1 = b.bank(); ps2 = b.bank()
                    for kc in range(8):
                        b.mm(ps1[:, 0:NT], wpg[:, kc, (c2 % 4) * 128:(c2 % 4 + 1) * 128], XN.c(kc)[:, 0:NT], kc == 0, kc == 7)
                    for kc in range(2):
                        b.mm(ps2[:, 0:NT], wpe[:, kc, (c2 % 4) * 128:(c2 % 4 + 1) * 128], TB(8 + kc)[:, 0:NT], kc == 0, kc == 1)
                    sg = TF(c2 % 2)
                    b.act(sg[:, 0:NT], ps1[:, 0:NT], AF.Sigmoid)
                    b.tt("dve", sg[:, 0:NT], ps2[:, 0:NT], sg[:, 0:NT], MUL)
                    b.tt("pool", H.c(c2)[:, 0:NT], H.c(c2)[:, 0:NT], sg[:, 0:NT], ADD)
                if not (ti == 0 and l == 0 and not sample): drain(1000)
            b.dma("sp", V(yT[:, :, col0:col0 + NT], []), V(Ht[:, :, 0:NT], H.all().keys))

        b.s.marks.append(("end", b.s.total, len(b.s.ops["pe"])))
        import os
        if os.environ.get("KMARKS"):
            for m in b.s.marks: print("MARK", m)
            lo, hi = [int(x) for x in os.environ.get("KDUMP", "0,0").split(",")]
            for e_ in b.s.log:
                if lo <= e_[0] <= hi: print("OP", e_)
            print("OPS", {e: len(b.s.ops[e]) for e in ENGS}, "waits", {e: sum(len(o.waits) for o in b.s.ops[e]) for e in ENGS})
        sch = b.s
        esem = {e: es.enter_context(nc.semaphore(f"sem_{e}")) for e in ENGS}
        dsem = [es.enter_context(nc.semaphore(f"dsem{i}")) for i in range(sch.ndma)]
        for e in ENGS:
            cnt = 0
            for op in sch.ops[e]:
                if op.sig and not op.dma:
                    cnt += 1
                op.cnt = cnt
        final = {}
        for e in ENGS:
            for op in sch.ops[e]:
                if op.dma: final[op.dsem] = max(final.get(op.dsem, 0), op.dval)

        def run(ename, e):
            for op in sch.ops[ename]:
                for d in op.waits:
                    if d.dma: e.wait_ge(dsem[d.dsem], d.dval)
                    else: e.wait_ge(esem[d.eng], d.cnt)
                ins = op.fn(e)
                if op.dma: ins.then_inc(dsem[op.dsem], 16)
                elif op.sig: ins.then_inc(esem[ename], 1)
            if ename == "sp":
                for i, v in final.items():
                    e.wait_ge(dsem[i], v)

        with nc.Block() as block:
            @block.tensor
            def _(e): run("pe", e)
            @block.scalar
            def _(e): run("act", e)
            @block.vector
            def _(e): run("dve", e)
            @block.gpsimd
            def _(e): run("pool", e)
            @block.sync
            def _(e): run("sp", e)
    return nc


def _fm(a):
    n, d = a.shape
    return np.ascontiguousarray(a.T.reshape(d // 128, 128, n).transpose(1, 0, 2))


def _consts():
    c = np.zeros((128, CW), np.float32)
    i = np.arange(128)
    c[:, C_ID:C_ID + 128] = np.eye(128)
    c[:, C_ONES:C_ONES + 128] = 1.0
    blk = (i[:, None] // 64 == i[None, :] // 64).astype(np.float32)
    c[:, C_BONES:C_BONES + 128] = blk
    c[:, C_BMEAN:C_BMEAN + 128] = blk / 64.0
    su = (i[None, :] > i[:, None]).astype(np.float32)
    iu = (i[None, :] >= i[:, None]).astype(np.float32)
    c[:, C_M4:C_M4 + 512] = np.concatenate([su, iu, su, iu], 1)
    sl = (i[None, :] < i[:, None]).astype(np.float32)
    c[:, C_ML4:C_ML4 + 512] = np.concatenate([sl] * 4, 1)
    c[:, C_ID4:C_ID4 + 512] = np.concatenate([np.eye(128, dtype=np.float32)] * 4, 1)
    rp = np.ones(512, np.float32); rp[::128] = 0.0
    rs = np.ones(64, np.float32); rs[::4] = 0.0
    c[:, C_RSP:C_RSP + 512] = rp[None]; c[:, C_RSS:C_RSS + 64] = rs[None]
    c[:, C_SH:C_SH + 64] = (i[:, None] == (np.arange(64)[None, :] + 64)).astype(np.float32)
    return c


_NC_CACHE = {}


def kernel(x_prompt, x_sample, state_wkv, state_shift, p_prompt, p_sample,
           mix_pre_g, mix_post_g, ffn_pre_g, ffn_post_g, w_in, mu_rkv, mu_wag,
           w0, w1, w2, a0, a1, a2, g1, g2, k_k, k_a, r_k, lnx_g, lnx_b,
           vn_g, vn_b, w_s, b_s, w_out, w_up, w_down, w_pe, w_pg):
    f = lambda a: np.ascontiguousarray(np.asarray(a, dtype=np.float32))
    x_prompt, x_sample, state_wkv, state_shift, p_prompt, p_sample = map(f, (x_prompt, x_sample, state_wkv, state_shift, p_prompt, p_sample))
    prm = np.zeros((128, 2, NPC), np.float32)
    def col(v): return np.asarray(v, np.float32).reshape(-1, 128).T
    for l in range(2):
        mr = np.asarray(mu_rkv[l], np.float32)
        items = [(G_PRE, mix_pre_g[l]), (G_POST, mix_post_g[l]), (F_PRE, ffn_pre_g[l]), (F_POST, ffn_post_g[l]),
                 (MU_R, mr[0:1024]), (MU_K, mr[1024:2048]), (MU_V, mr[2048:3072]),
                 (MU_W, mu_wag[l][0]), (MU_A, mu_wag[l][1]), (MU_G, mu_wag[l][2]),
                 (W0, w0[l]), (A0, a0[l]), (KK, k_k[l]), (KA, k_a[l]), (RK, np.asarray(r_k[l]).reshape(-1)),
                 (LNG, lnx_g[l]), (LNB, lnx_b[l]), (VNG, vn_g[l]), (VNB, vn_b[l])]
        for o, v in items:
            prm[:, l, o:o + 8] = col(v)
    cst = _consts()
    wsT = np.ascontiguousarray(np.asarray(w_s, np.float32).transpose(0, 3, 1, 2))
    bs = np.ascontiguousarray(np.asarray(b_s, np.float32).reshape(2, 1, 8, 128))
    shared = dict(prm=prm, cst=cst, wsT=wsT, bs=bs, w_in=f(w_in), w_out=f(w_out), w_up=f(w_up), w_down=f(w_down),
                  w_pe=f(w_pe), w_pg=f(w_pg), w1=f(w1), w2=f(w2), a1=f(a1), a2=f(a2), g1=f(g1), g2=f(g2))
    in_maps = []
    for i in range(NCORE):
        xs = x_sample[NSQ * i:NSQ * (i + 1)].reshape(NSQ * DEC, D)
        xT = np.concatenate([_fm(x_prompt[i]), _fm(xs)], axis=2)
        pTl = []
        for l in range(2):
            ps_ = p_sample[l, NSQ * i:NSQ * (i + 1)].reshape(NSQ * DEC, 256)
            pTl.append(np.concatenate([_fm(p_prompt[l, i]), _fm(ps_)], axis=2))
        pT = np.stack(pTl)
        shT = np.stack([_fm(state_shift[l, NSQ * i:NSQ * (i + 1)]) for l in range(2)])
        sw = state_wkv[:, NSQ * i:NSQ * (i + 1)].reshape(2, NSQ, 8, 2, 64, 64)
        wkvT = np.ascontiguousarray(sw.transpose(0, 3, 5, 1, 2, 4)).reshape(2, 128, NSQ, 8, 64)
        m = dict(shared); m.update(xT=np.ascontiguousarray(xT), pT=np.ascontiguousarray(pT), shT=np.ascontiguousarray(shT), wkvT=wkvT)
        in_maps.append(m)
    if "nc" not in _NC_CACHE:
        _NC_CACHE["nc"] = build_nc()
    res = run_bass_kernel_spmd(_NC_CACHE["nc"], in_maps, core_ids=list(range(NCORE)))
    R = list(res.results)
    def unfm(a):
        return np.ascontiguousarray(a.transpose(2, 1, 0).reshape(a.shape[2], -1))
    y_prompt = np.stack([unfm(R[i]["yT"][:, :, 0:SEQ]) for i in range(NCORE)])
    y_sample = np.concatenate([unfm(R[i]["yT"][:, :, SEQ:]).reshape(NSQ, DEC, D) for i in range(NCORE)])
    wkv_p = np.stack([np.stack([R[i]["wkvp"][l].reshape(2, 64, 8, 64).transpose(2, 0, 3, 1).reshape(16, 64, 64) for i in range(NCORE)]) for l in range(2)])
    sh_p = np.stack([np.stack([R[i]["shp"][l].T.reshape(D) for i in range(NCORE)]) for l in range(2)])
    wkv_s = np.stack([np.concatenate([R[i]["wkvs"][l].reshape(2, 64, NSQ, 8, 64).transpose(2, 3, 0, 4, 1).reshape(NSQ, 16, 64, 64) for i in range(NCORE)]) for l in range(2)])
    sh_s = np.stack([np.concatenate([R[i]["shs"][l].transpose(2, 1, 0).reshape(NSQ, D) for i in range(NCORE)]) for l in range(2)])
    cv_s = np.stack([np.concatenate([R[i]["cvs"][l].reshape(128, 8, NSQ, DEC).transpose(2, 3, 1, 0).reshape(NSQ, DEC, D) for i in range(NCORE)]) for l in range(2)])
    o = lambda a: np.ascontiguousarray(a, dtype=np.float32)
    return (o(y_prompt), o(y_sample), o(wkv_p), o(sh_p), o(wkv_s), o(sh_s), o(cv_s))
```

```python
import numpy as np
from contextlib import ExitStack
import concourse.bass as bass
import concourse.mybir as mybir
from concourse.bass_utils import run_bass_kernel_spmd

F32, BF16 = mybir.dt.float32, mybir.dt.bfloat16
AF = mybir.ActivationFunctionType
ALU = mybir.AluOpType
MUL, ADD, SUB, MAX = ALU.mult, ALU.add, ALU.subtract, ALU.max

D = 1024; NCORE = 8; SEQ = 2048; NSQ = 16; DEC = 4; NTOK = SEQ + NSQ * DEC
W = 516
LAM = 0.6065306597126334
EPS = 1e-6; GN_EPS = 64e-5
NBLK = 38; SLOTW = 4608
G_PRE, G_POST, F_PRE, F_POST, MU_R, MU_K, MU_V, MU_W, MU_A, MU_G = 0, 8, 16, 24, 32, 40, 48, 56, 64, 72
W0, A0, KK, KA, RK, LNG, LNB, VNG, VNB, OMM_R, OMM_K, OMM_V, OMKA, NPC = 80, 88, 96, 104, 112, 120, 128, 136, 144, 152, 160, 168, 176, 184
C_ID, C_ONES, C_BONES, C_BMEAN, C_M4, C_ML4, C_ID4, C_RSP, C_RSS, C_SH, CW = 0, 128, 256, 384, 512, 1024, 1536, 2048, 2560, 2624, 2688


class V:
    __slots__ = ("ap", "keys")

    def __init__(s, ap, keys):
        s.ap = ap; s.keys = tuple(keys)

    def __getitem__(s, i):
        return V(s.ap[i], s.keys)

    def re(s, pat, **kw):
        return V(s.ap.rearrange(pat, **kw), s.keys)

    def bc(s, shape):
        return V(s.ap.to_broadcast(list(shape)), s.keys)

    def cast(s, dt):
        return V(s.ap.bitcast(dt), s.keys)


class Op:
    __slots__ = ("eng", "fn", "idx", "sig", "waits", "dma", "dsem", "dval", "cnt")


ENGS = ("pe", "act", "dve", "pool", "sp")


class Sched:
    def __init__(s, ndma=24):
        s.ops = {e: [] for e in ENGS}
        s.last_w = {}; s.readers = {}
        s.known = {e: {} for e in ENGS}
        s.ndma = ndma; s.dma_i = 0; s.dma_last = [None] * ndma
        import os
        s.limit = int(float(os.environ.get("KLIMIT", "1e12"))); s.marks = []

    def add(s, eng, fn, r, w, dma=False, cost=256):
        s.total = getattr(s, "total", 0) + 1
        if s.total > s.limit: return None
        op = Op(); op.eng = eng; op.fn = fn; op.idx = len(s.ops[eng]); op.sig = False
        op.waits = []; op.dma = dma; op.cnt = 0; op.dsem = -1; op.dval = 0
        deps = []
        for k in r:
            d = s.last_w.get(k)
            if d is not None: deps.append(d)
        for k in w:
            d = s.last_w.get(k)
            if d is not None: deps.append(d)
            deps.extend(s.readers.get(k, ()))
        if dma:
            ph = s.__dict__.setdefault("hist_" + eng, [])
            acc = cost
            lim = 800 if eng == "pool" else 400
            for (pop, pc) in reversed(ph[-16:]):
                acc += pc
                if acc > lim:
                    deps.append(pop); break
        if dma:
            cnts = s.__dict__.setdefault("dma_cnt", {"pool": 0, "sp": 0})
            base, n = (0, 8) if eng == "pool" else (8, s.ndma - 8)
            i = cnts[eng]; cnts[eng] += 1
            slot = base + i % n; op.dsem = slot; op.dval = 16 * (i // n + 1)
            if s.dma_last[slot] is not None: deps.append(s.dma_last[slot])
            s.dma_last[slot] = op
        kn = s.known[eng]
        best = {}
        for d in deps:
            key, val = (("d", d.dsem), d.dval) if d.dma else (d.eng, d.idx)
            if key not in best or val > best[key][0]: best[key] = (val, d)
        deps = [v[1] for v in best.values()]
        for d in deps:
            if d.dma:
                key = ("d", d.dsem)
                if kn.get(key, 0) >= d.dval: continue
                kn[key] = d.dval; op.waits.append(d)
            else:
                if d.eng == "pe" and eng == "pe": continue
                if kn.get(d.eng, -1) >= d.idx: continue
                kn[d.eng] = d.idx; d.sig = True; op.waits.append(d)
        for k in r: s.readers.setdefault(k, []).append(op)
        for k in w:
            s.last_w[k] = op; s.readers[k] = []
        s.ops[eng].append(op)
        import sys as _sys
        s.__dict__.setdefault("log", []).append((s.total, eng, _sys._getframe(1).f_code.co_name, len(op.waits), tuple(w)[:2]))
        if dma: s.__dict__["hist_" + eng].append((op, cost))
        return op


class Geo:
    def __init__(s, sample):
        s.sample = sample
        if sample: s.NT, s.T, s.NCH, s.NE, s.M = 64, 4, 16, 80, 2
        else: s.NT, s.T, s.NCH, s.NE, s.M = 512, 128, 4, 513, 7
        s.CE = s.NCH * (s.T + 1)

    def cur(s, v):
        if s.sample: return v[:, 0:80].re("p (g t) -> p g t", t=5)[:, :, 1:5]
        return v[:, 1:513]

    def prev(s, v):
        if s.sample: return v[:, 0:80].re("p (g t) -> p g t", t=5)[:, :, 0:4]
        return v[:, 0:512]

    def cmp(s, v):
        if s.sample: return v[:, 0:64].re("p (g t) -> p g t", t=4)
        return v[:, 0:512]

    def c3(s, v):
        return v[:, 0:s.NT].re("p (n t) -> p n t", t=s.T)

    def ccur(s, v):
        return v[:, 0:s.CE].re("p (n t) -> p n t", t=s.T + 1)[:, :, 1:s.T + 1]

    def cprev(s, v):
        return v[:, 0:s.CE].re("p (n t) -> p n t", t=s.T + 1)[:, :, 0:s.T]


class Builder:
    def __init__(b, nc):
        b.nc = nc; b.s = Sched(); b.bank_i = 0; b.reserved = set()

    def _k(b, vs):
        ks = []
        for v in vs:
            if v is None or isinstance(v, (int, float)): continue
            ks.extend(v.keys)
        return ks

    def _a(b, v):
        return v.ap if isinstance(v, V) else v

    def mm(b, out, lhsT, rhs, start=True, stop=True):
        o, l, r = out.ap, lhsT.ap, rhs.ap
        b.s.add("pe", lambda e: e.matmul(o, lhsT=l, rhs=r, start=start, stop=stop), b._k([lhsT, rhs]), b._k([out]))

    def tr(b, out, in_, ident):
        o, i, d = out.ap, in_.ap, ident.ap
        b.s.add("pe", lambda e: e.transpose(o, i, d), b._k([in_, ident]), b._k([out]))

    def act(b, out, in_, func, bias=0.0, scale=1.0):
        o, i, bi, sc = out.ap, in_.ap, b._a(bias), b._a(scale)
        b.s.add("act", lambda e: e.activation(out=o, in_=i, func=func, bias=bi, scale=sc), b._k([in_, bias, scale]), b._k([out]))

    def tt(b, eng, out, in0, in1, op):
        o, x, y = out.ap, in0.ap, in1.ap
        b.s.add(eng, lambda e: e.tensor_tensor(out=o, in0=x, in1=y, op=op), b._k([in0, in1]), b._k([out]))

    def ts(b, eng, out, in0, s1, s2=None, op0=MUL, op1=None):
        o, x, a1, a2 = out.ap, in0.ap, b._a(s1), b._a(s2)
        if op1 is None:
            fn = lambda e: e.tensor_scalar(out=o, in0=x, scalar1=a1, scalar2=None, op0=op0)
        else:
            fn = lambda e: e.tensor_scalar(out=o, in0=x, scalar1=a1, scalar2=a2, op0=op0, op1=op1)
        b.s.add(eng, fn, b._k([in0, s1, s2]), b._k([out]))

    def stt(b, out, in0, sc, in1, op0, op1):
        o, x, a, y = out.ap, in0.ap, b._a(sc), in1.ap
        b.s.add("dve", lambda e: e.scalar_tensor_tensor(out=o, in0=x, scalar=a, in1=y, op0=op0, op1=op1), b._k([in0, sc, in1]), b._k([out]))

    def cp(b, eng, out, in_):
        o, i = out.ap, in_.ap
        if eng == "act":
            b.s.add("act", lambda e: e.activation(out=o, in_=i, func=AF.Copy), b._k([in_]), b._k([out]))
        else:
            b.s.add(eng, lambda e: e.tensor_copy(out=o, in_=i), b._k([in_]), b._k([out]))

    def recip(b, out, in_):
        o, i = out.ap, in_.ap
        b.s.add("dve", lambda e: e.reciprocal(out=o, in_=i), b._k([in_]), b._k([out]))

    def memset(b, eng, out, val):
        o = out.ap
        b.s.add(eng, lambda e: e.memset(o, val), [], b._k([out]))

    def scan(b, out, d0, d1):
        o, x, y = out.ap, d0.ap, d1.ap
        b.s.add("dve", lambda e: e.tensor_tensor_scan(out=o, data0=x, data1=y, initial=0.0, op0=MUL, op1=ADD), b._k([d0, d1]), b._k([out]))

    def dma(b, eng, out, in_, cost=None):
        if cost is None: cost = 256 if eng == "pool" else 16
        o, i = out.ap, in_.ap
        b.s.add(eng, lambda e: e.dma_start(out=o, in_=i), b._k([in_]), b._k([out]), dma=True, cost=cost)

    def bank(b, reserve=False):
        while True:
            i = b.bank_i % 8; b.bank_i += 1
            if i not in b.reserved: break
        if reserve: b.reserved.add(i)
        return V(b.ps[i][:], [("ps", i)])

    def release(b, *vs):
        for v in vs: b.reserved.discard(v.keys[0][1])


def build_nc():
    nc = bass.Bass("TRN2", target_bir_lowering=False)
    b = Builder(nc)

    def din(name, shape, dt=F32):
        return nc.dram_tensor(name, list(shape), dt, kind="ExternalInput").ap()

    def dout(name, shape, dt=F32):
        return nc.dram_tensor(name, list(shape), dt, kind="ExternalOutput").ap()

    xT = din("xT", [128, 8, NTOK]); pT = din("pT", [2, 128, 2, NTOK])
    shT = din("shT", [2, 128, 8, NSQ]); wkvT = din("wkvT", [2, 128, NSQ, 8, 64])
    prm_d = din("prm", [128, 2, NPC]); cst_d = din("cst", [128, CW])
    wsT_d = din("wsT", [2, 128, 8, 128]); bs_d = din("bs", [2, 1, 8, 128])
    w_in = din("w_in", [2, D, 7 * D]); w_out = din("w_out", [2, D, D]); w_up = din("w_up", [2, D, 4 * D])
    w_down = din("w_down", [2, 4 * D, D]); w_pe = din("w_pe", [2, 256, D]); w_pg = din("w_pg", [2, D, D])
    w1 = din("w1", [2, D, 64]); w2 = din("w2", [2, 64, D]); a1 = din("a1", [2, D, 64]); a2 = din("a2", [2, 64, D])
    g1 = din("g1", [2, D, 160]); g2 = din("g2", [2, 160, D])
    yT = dout("yT", [128, 8, NTOK]); wkvp_o = dout("wkvp", [2, 128, 8, 64]); shp_o = dout("shp", [2, 128, 8])
    wkvs_o = dout("wkvs", [2, 128, NSQ, 8, 64]); shs_o = dout("shs", [2, 128, 8, NSQ]); cvs_o = dout("cvs", [2, 128, 8, NSQ * DEC])
    scr = nc.dram_tensor("scr", [2, 2, 128, SLOTW], BF16, kind="Internal").ap()
    nW = {nm: nc.dram_tensor("n_" + nm, list(shp), BF16, kind="Internal").ap() for nm, shp in
          (("w_in", (2, D, 7 * D)), ("w_out", (2, D, D)), ("w_up", (2, D, 4 * D)), ("w_down", (2, 4 * D, D)), ("w_pg", (2, D, D)), ("w_pe", (2, 256, D)))}
    scrD = nc.dram_tensor("scrD", [2, 8, 128, 4096], BF16, kind="Internal").ap()
    srcW = {"w_in": w_in, "w_out": w_out, "w_up": w_up, "w_down": w_down, "w_pg": w_pg, "w_pe": w_pe}

    es = ExitStack()
    with es:
        def sb(name, shape, dt):
            return es.enter_context(nc.sbuf_tensor(name, list(shape), dt))

        Ht = sb("H", [128, 8, W], F32); F1t = sb("F1", [128, 8, W], F32)
        XNt = sb("XN", [128, 8, W], BF16); P1t = sb("P1", [128, 8, W], BF16)
        P4t = sb("P4", [128, 8, W], BF16); P5t = sb("P5", [128, 8, W], BF16)
        P6t = sb("P6", [128, 8, W], BF16); P7t = sb("P7", [128, 8, W], BF16)
        ARt = sb("AR", [128, 8, 2, W], BF16)
        TFt = sb("TF", [128, 5, W], F32); TBt = sb("TB", [128, 14, W], BF16)
        M4t = sb("M4", [128, 16, 4, 128], BF16); TTt = sb("TTm", [128, 16, 128], BF16)
        XBt = sb("XB", [128, 2, 4, 4, 128], BF16)
        NSLOT = 3
        WSt = [sb(f"WS{i}", [128, 4096], BF16) for i in range(NSLOT)]
        CBt = sb("CB", [128, CW], BF16); PRt = sb("PRM", [128, 2, NPC], F32)
        RKt = sb("RKO", [128, 2, 8, 128], BF16); WSMt = sb("WSM", [128, 2, 8, 128], BF16)
        BSt = sb("BSR", [1, 2, 8, 128], BF16)
        St = sb("S", [128, 2, 8, 64], F32); Sbt = sb("Sb", [128, 2, 8, 128], BF16)
        BMt = sb("BM", [128, 8, 128], BF16); KMt = sb("KM", [128, 8, 128], BF16); AMt = sb("AM", [128, 8, 128], BF16)
        IDFt = sb("IDF", [128, 128], F32)
        JNKt = sb("JNK", [128, 4], F32)
        P1flat = P1t[:].rearrange("p a w -> p (a w)")
        M4flat = M4t[:].rearrange("p h m t -> p (h m t)")
        def TBa(i): return V(M4flat[:, i * W:(i + 1) * W], [("TD", i)])
        def TFa(i): return V(M4flat[:, i * W:(i + 2) * W].bitcast(F32), [("TD", i), ("TD", i + 1)])
        XBflat = XBt[:].rearrange("p a k j t -> p (a k j t)")
        def XBa(i): return V(XBflat[:, i * W:(i + 1) * W], [("XD", i)])
        def fence_m4(col):
            b.memset("pool", V(JNKt[:, col:col + 1], [("TD", i) for i in range(15)] + [("M4", h) for h in range(16)]
                               + [("XD", i) for i in range(7)] + [("XB", s_, k_) for s_ in (0, 1) for k_ in range(4)]), 0.0)
        PTt = sb("PT", [128, 8, 16], F32)
        XCt = sb("XC", [128, 2, 8], BF16); RCt = sb("RC", [128, 2, 24], BF16)
        SHt = TFt[:, 3, 0:128].rearrange("p (c n) -> p c n", c=8); SH2t = TFt[:, 4, 0:128].rearrange("p (c n) -> p c n", c=8)
        SHPt = sb("SHP", [128, 8], F32)
        SHIt = TFt[:, 2, 0:128].rearrange("p (c n) -> p c n", c=8)
        b.ps = [es.enter_context(nc.psum_tensor(f"ps{i}", [128, 512], F32)) for i in range(8)]

        class Arr:
            def __init__(s, name, t): s.name = name; s.t = t
            def c(s, i): return V(s.t[:, i, :], [(s.name, i)])
            def all(s): return V(s.t[:], [(s.name, i) for i in range(8)])
            def rng(s, a, e): return V(s.t[:, a:e], [(s.name, i) for i in range(a, e)])

        H, F1, XN, P1, P4, P5, P6, P7 = (Arr(n, t) for n, t in (("H", Ht), ("F1", F1t), ("XN", XNt), ("P1", P1t), ("P4", P4t), ("P5", P5t), ("P6", P6t), ("P7", P7t)))

        def AR(c, j=None):
            if j is None: return V(ARt[:, c], [("AR", c)])
            return V(ARt[:, c, j, :], [("AR", c)])

        def TF(i): return V(TFt[:, i, :], [("TF", i)])
        def TB(i): return V(TBt[:, i, :], [("TB", i)])
        def TB2(i): return V(TBt[:, i:i + 2, :].rearrange("p a w -> p (a w)"), [("TB", i), ("TB", i + 1)])
        def CB(o, n): return V(CBt[:, o:o + n], [("CB",)])
        def PRM(l, col, n=1): return V(PRt[:, l, col:col + n], [("PRM", l)])
        def WS(i): return V(WSt[i][:], [("WS", i)])

        ident = CB(C_ID, 128); ones = CB(C_ONES, 128); bones = CB(C_BONES, 128); bmean = CB(C_BMEAN, 128)
        mask4 = CB(C_M4, 512).re("p (m t) -> p m t", m=4); maskL4 = CB(C_ML4, 512).re("p (m t) -> p m t", m=4)
        ident4 = CB(C_ID4, 512).re("p (m t) -> p m t", m=4)
        shiftI = CB(C_SH, 64)

        b.dma("sp", V(Ht[:, :, 0:512], [("H", i) for i in range(8)]), V(xT[:, :, 0:512], []))
        b.dma("pool", CB(0, CW), V(cst_d, []))
        b.dma("sp", V(PRt[:], [("PRM", 0), ("PRM", 1)]), V(prm_d, []))
        b.dma("sp", V(IDFt[:], [("IDF",)]), V(cst_d[:, C_ID:C_ID + 128], []))
        for l in range(2):
            b.ts("pool", PRM(l, OMM_R, 24), PRM(l, MU_R, 24), -1.0, 1.0, MUL, ADD)
            b.ts("pool", PRM(l, OMKA, 8), PRM(l, KA, 8), -1.0, 1.0, MUL, ADD)
        b.memset("pool", V(St[:], [("S", 0), ("S", 1)]), 0.0)
        b.memset("pool", V(Sbt[:], [("Sb", 0), ("Sb", 1)]), 0.0)
        for (t_, k_) in ((BMt, "BM"), (KMt, "KM"), (AMt, "AM")):
            b.memset("pool", V(t_[:], [(k_,)]), 0.0)
        b.memset("pool", V(XCt[:], [("XC", 0), ("XC", 1)]), 0.0)
        b.memset("pool", V(RCt[:], [("RC", 0), ("RC", 1)]), 0.0)

        def SCR(l, k): return V(scr[l, k], [("scr", l, k)])

        conv = [[], []]
        for l in (1, 0):
            stg = V(F1t[:].rearrange("p a w -> p (a w)"), [("F1", i) for i in range(8)])
            w1s = stg[:, 0:512].re("p (k n) -> p k n", k=8)
            a1s = stg[:, 512:1024].re("p (k n) -> p k n", k=8)
            g1s = stg[:, 1024:2304].re("p (k n) -> p k n", k=8)
            wss = stg[:, 2304:3328].re("p (g t) -> p g t", g=8)
            bss = V(TFt[:].rearrange("p a w -> p (a w)"), [("TF", i) for i in range(5)])[0:1, 0:1024].re("p (g t) -> p g t", g=8)
            b.dma("sp", w1s, V(w1[l].rearrange("(k p) n -> p k n", p=128), []))
            b.dma("sp", a1s, V(a1[l].rearrange("(k p) n -> p k n", p=128), []))
            b.dma("sp", g1s, V(g1[l].rearrange("(k p) n -> p k n", p=128), []))
            b.dma("sp", wss, V(wsT_d[l], []))
            b.dma("sp", bss, V(bs_d[l], []))
            LBflat = V(M4flat[:, 0:4608], [("TD", i) for i in range(9)])
            LB = LBflat.re("p (k n) -> p k n", k=8)
            for (src, mu, o, n) in ((w1s, MU_W, 0, 64), (a1s, MU_A, 128, 64), (g1s, MU_G, 256, 160)):
                b.cp("dve", LB[:, :, o:o + n], src)
                muv = V(PRt[:, l, mu:mu + 8].unsqueeze(2).to_broadcast([128, 8, n]), [("PRM", l)])
                b.tt("dve", LB[:, :, o + n:o + 2 * n], src, muv, MUL)
            b.dma("sp", V(scr[l, 0, :, 0:4608], [("scr", l, 0)]), LBflat)
            b.tt("dve", V(WSMt[:, l], [("WSM", l)]), wss, V(CBt[:, C_M4 + 128:C_M4 + 256].unsqueeze(1).to_broadcast([128, 8, 128]), [("CB",)]), MUL)
            b.cp("dve", V(BSt[0:1, l], [("BSR", l)]), bss)
            for c in range(8):
                b.ts("dve", V(RKt[:, l, c, :], [("RKO", l)]), bones, PRM(l, RK + c), None, MUL)
            cl = conv[l]
            def cv(dst, key, src, cost, cl=cl):
                cl.append((lambda: b.dma("pool", V(dst, [key]), V(src, []), cost=cost)))
            cv(scr[l, 1, 0:64, 0:1024], ("scr", l, 1), w2[l], 8)
            cv(scr[l, 1, 0:64, 1024:2048], ("scr", l, 1), a2[l], 8)
            cv(scr[l, 1, 0:128, 2048:3072], ("scr", l, 1), g2[l, 0:128], 8)
            cv(scr[l, 1, 0:32, 3072:4096], ("scr", l, 1), g2[l, 128:160], 8)
            def natcv(nm):
                rows = srcW[nm].shape[1]
                npart = 4 if rows >= 1024 else 1
                rp = rows // npart
                for i in range(npart):
                    cv(nW[nm][l, i * rp:(i + 1) * rp, :], ("nw", l, nm), srcW[nm][l, i * rp:(i + 1) * rp, :], 40)
            natcv("w_in"); natcv("w_out"); natcv("w_up")
            wdv = w_down[l].rearrange("(k p) n -> p k n", p=128)
            for j in range(8):
                cv(scrD[l, j].rearrange("p (k n) -> p k n", k=32), ("scrD", l, j), wdv[:, :, 128 * j:128 * (j + 1)], 258)
            natcv("w_pg"); natcv("w_pe")

        convq = conv[0] + conv[1]
        def drain(n):
            for _ in range(min(n, len(convq))):
                convq.pop(0)()
        drain(8)

        class WStream:
            def __init__(s): s.seq = []; s.issued = 0; s.used = 0
            def plan(s, items): s.seq.extend(items)
            def _issue(s):
                l, k = s.seq[s.issued]
                sl = s.issued % NSLOT
                wk = [("WS", sl)]
                def nat(nm): return nW[nm][l].rearrange("(k p) n -> p k n", p=128), [("nw", l, nm)]
                if k == 1:
                    for (np_, c0, c1) in [(64, 0, 2048), (128, 2048, 3072), (32, 3072, 4096)]:
                        b.dma("sp", V(WSt[sl][0:np_, c0:c1], wk), V(scr[l, 1, 0:np_, c0:c1], [("scr", l, 1)]))
                elif k < 8:
                    src, sk = nat("w_in")
                    dst = WSt[sl][:, 0:4096].rearrange("p (k n) -> p k n", k=8)
                    for q in range(4):
                        fb = 4 * (k - 2) + q; c, j = fb // 3, fb % 3
                        b.dma("sp", V(dst[:, :, q * 128:(q + 1) * 128], wk), V(src[:, :, j * 1024 + c * 128:j * 1024 + (c + 1) * 128], sk), cost=64)
                elif k < 26 or k in (34, 35):
                    nm, j, off = ("w_in", k - 8, 3072) if k < 16 else (("w_out", k - 16, 0) if k < 18 else (("w_up", k - 18, 0) if k < 26 else ("w_pg", k - 34, 0)))
                    src, sk = nat(nm)
                    dst = WSt[sl][:, 0:4096].rearrange("p (k n) -> p k n", k=8)
                    b.dma("sp", V(dst, wk), V(src[:, :, off + 512 * j:off + 512 * (j + 1)], sk), cost=64)
                elif k < 34:
                    j = k - 26
                    b.dma("sp", V(WSt[sl][:, 0:4096], wk), V(scrD[l, j], [("scrD", l, j)]))
                else:
                    src, sk = nat("w_pe")
                    j = k - 36
                    dst = WSt[sl][:, 0:1024].rearrange("p (k n) -> p k n", k=2)
                    b.dma("sp", V(dst, wk), V(src[:, :, 512 * j:512 * (j + 1)], sk))
                s.issued += 1
            def get(s, ahead=NSLOT - 1):
                while s.issued < min(len(s.seq), s.used + 1 + ahead): s._issue()
                v = WS(s.used % NSLOT); s.used += 1
                return v
        ws = WStream()
        ORDER = [1, 2, 3, 4, 5, 6, 7, 1, 10, 11, 8, 9, 12, 13, 14, 15, 16, 17] + list(range(18, 34)) + [34, 36, 35, 37]
        tiles = [(False, i) for i in range(4)] + [(True, 0)]
        for _t in tiles:
            for l in range(2):
                ws.plan([(l, k) for k in ORDER])

        def rms_stats(g, arr):
            NT = g.NT
            ps = b.bank()
            for c in range(8):
                sq = TB(c % 2)
                b.act(sq[:, 0:NT], arr.c(c)[:, 0:NT], AF.Square)
                b.mm(ps[:, 0:NT], ones, sq[:, 0:NT], c == 0, c == 7)
            b.act(TF(0)[:, 0:NT], ps[:, 0:NT], AF.Ln, bias=V(EPSt[:, 0:1], [("EPSC",)]), scale=1.0 / D)
            b.act(TF(1)[:, 0:NT], TF(0)[:, 0:NT], AF.Exp, scale=-0.5)
            return TF(1)

        EPSt = sb("EPSC", [128, 2], F32)
        b.memset("pool", V(EPSt[:, 0:1], [("EPSC",)]), EPS)
        b.memset("pool", V(EPSt[:, 1:2], [("EPSC",)]), GN_EPS)
        epsv = V(EPSt[:, 0:1], [("EPSC",)]); gnepsv = V(EPSt[:, 1:2], [("EPSC",)])

        def load_l1(ln):
            fence_m4(2)
            b.dma("sp", V(M4flat[:, 0:4608], [("TD", i) for i in range(9)]), V(scr[ln, 0, :, 0:4608], [("scr", ln, 0)]))
        b.s.marks.append(("init_end", b.s.total, len(b.s.ops["pe"])))
        for (sample, ti) in tiles:
            g = Geo(sample); NT, T, NCH = g.NT, g.T, g.NCH
            col0 = SEQ if sample else ti * 512
            last_prompt = (not sample) and ti == 3
            if sample or ti > 0:
                b.dma("sp", V(Ht[:, :, 0:NT], H.all().keys), V(xT[:, :, col0:col0 + NT], []), cost=64)
            for l in range(2):
                b.s.marks.append(("Phase A t%d l%d" % (ti + 4 * sample, l), b.s.total, len(b.s.ops["pe"])))
                if ti == 0 and l == 0 and not sample: drain(0)
                rstd = rms_stats(g, H)
                for c in range(8):
                    b.stt(g.cur(XN.c(c)), g.cmp(H.c(c)), PRM(l, G_PRE + c), g.cmp(rstd), MUL, MUL)
                if sample:
                    b.dma("sp", V(SHIt[:], [("TF", 2)]), V(shT[l], []))
                    b.cp("pool", V(XNt[:, :, 0:80].rearrange("p c (g t) -> p c g t", t=5)[:, :, :, 0], XN.all().keys), V(SHIt[:], [("TF", 2)]))
                    hv = V(Ht[:, :, 0:64].rearrange("p c (g t) -> p c g t", t=4)[:, :, :, 3], H.all().keys)
                    b.tt("pool", V(SHt[:], [("TF", 3)]), hv, V(PRt[:, l, G_PRE:G_PRE + 8].unsqueeze(2).to_broadcast([128, 8, NSQ]), [("PRM", l)]), MUL)
                    rv = V(TFt[:, 1, 0:64].rearrange("p (g t) -> p g t", t=4)[:, :, 3].unsqueeze(1).to_broadcast([128, 8, NSQ]), [("TF", 1)])
                    b.tt("pool", V(SH2t[:], [("TF", 4)]), V(SHt[:], [("TF", 3)]), rv, MUL)
                    b.dma("sp", V(shs_o[l], []), V(SH2t[:], [("TF", 4)]))
                else:
                    b.cp("pool", V(XNt[:, :, 0:1], XN.all().keys), V(XCt[:, l, :].unsqueeze(2), [("XC", l)]))
                    b.cp("pool", V(XCt[:, l, :].unsqueeze(2), [("XC", l)]), V(XNt[:, :, 512:513], XN.all().keys))
                    if last_prompt:
                        b.tt("pool", V(SHt[:, :, 0:1], [("TF", 3)]), V(Ht[:, :, 511:512], H.all().keys), V(PRt[:, l, G_PRE:G_PRE + 8].unsqueeze(2), [("PRM", l)]), MUL)
                        b.ts("pool", V(SHPt[:], [("SHP",)]), V(SHt[:, :, 0], [("TF", 3)]), rstd[:, 511:512], None, MUL)
                        b.dma("sp", V(shp_o[l], []), V(SHPt[:], [("SHP",)]))
                for c in range(8):
                    b.tt("pool", g.cmp(P4.c(c)), g.prev(XN.c(c)), g.cur(XN.c(c)), SUB)
                b.memset("pool", V(F1t[:, :, 0:g.CE].rearrange("p c (n t) -> p c n t", t=T + 1)[:, :, :, 0], F1.all().keys), 0.0)

                b.s.marks.append(("Phase B t%d l%d" % (ti + 4 * sample, l), b.s.total, len(b.s.ops["pe"])))
                if ti == 0 and not sample: drain(4)
                L1f = V(M4flat[:, 0:4608], [("TD", i) for i in range(9)])
                L1 = L1f.re("p (k n) -> p k n", k=8)
                def lora1(o, n, m0, m1, ps):
                    for kc in range(8):
                        b.mm(g.cmp(ps[0:m1 - m0, :]), L1[:, kc, o + m0:o + m1], g.cur(XN.c(kc)), kc == 0, False)
                        b.mm(g.cmp(ps[0:m1 - m0, :]), L1[:, kc, o + n + m0:o + n + m1], g.cmp(P4.c(kc)), False, kc == 7)
                ps = b.bank(); lora1(0, 64, 0, 64, ps)
                b.act(TB(2)[0:64, 0:NT], ps[0:64, 0:NT], AF.Tanh)
                ps = b.bank(); lora1(128, 64, 0, 64, ps)
                b.cp("dve", TB(3)[0:64, 0:NT], ps[0:64, 0:NT])
                ps = b.bank(); lora1(256, 160, 0, 128, ps)
                b.act(TB(4)[:, 0:NT], ps[:, 0:NT], AF.Sigmoid)
                ps = b.bank(); lora1(256, 160, 128, 160, ps)
                b.act(TB(5)[0:32, 0:NT], ps[0:32, 0:NT], AF.Sigmoid)
                L2 = ws.get()
                rsm = CB(C_RSS, 64) if sample else CB(C_RSP, 512)
                for c in range(8):
                    ps = b.bank()
                    b.mm(ps[:, 0:NT], L2[0:64, c * 128:(c + 1) * 128], TB(2)[0:64, 0:NT])
                    b.act(TF(2 + c % 2)[:, 0:NT], ps[:, 0:NT], AF.Sigmoid, bias=PRM(l, W0 + c))
                    for n in range(NCH):
                        b.scan(g.ccur(F1.c(c))[:, n, :], ones[:, 0:T], TF(2 + c % 2)[:, n * T:(n + 1) * T])
                    ps = b.bank()
                    b.mm(ps[:, 0:NT], L2[0:64, 1024 + c * 128:1024 + (c + 1) * 128], TB(3)[0:64, 0:NT])
                    b.act(P1.c(c)[:, 0:NT], ps[:, 0:NT], AF.Sigmoid, bias=PRM(l, A0 + c))

                b.s.marks.append(("Phase C/D t%d l%d" % (ti + 4 * sample, l), b.s.total, len(b.s.ops["pe"])))
                if ti == 0 and l == 0 and not sample: drain(0)
                fence_m4(2)
                wcur = None
                def rkv_block(fb):
                    nonlocal wcur
                    if fb % 4 == 0: wcur = ws.get()
                    return wcur[:, 0:4096].re("p (k n) -> p k n", k=8)[:, :, (fb % 4) * 128:(fb % 4 + 1) * 128]
                def proj_rkv(c, j, outv, eb):
                    wv = rkv_block(3 * c + j)
                    ps = b.bank()
                    NE = g.NE if sample else 512
                    rhs = (lambda kc: XN.c(kc)[:, 0:80]) if sample else (lambda kc: XN.c(kc)[:, 1:513])
                    for kc in range(8):
                        b.mm(ps[:, 0:NE], wv[:, kc, :], rhs(kc), kc == 0, kc == 7)
                    mu = PRM(l, MU_R + 8 * j + c); om = PRM(l, OMM_R + 8 * j + c)
                    if sample:
                        b.act(eb[:, 0:80], ps[:, 0:80], AF.Copy, scale=mu)
                        raw = ps[:, 0:80].re("p (g t) -> p g t", t=5)[:, :, 1:5]
                    else:
                        b.act(eb[:, 1:513], ps[:, 0:512], AF.Copy, scale=mu)
                        rc = V(RCt[:, l, 8 * j + c:8 * j + c + 1], [("RC", l)])
                        b.cp("pool", eb[:, 0:1], rc)
                        b.cp("pool", rc, eb[:, 512:513])
                        raw = ps[:, 0:512]
                    b.stt(g.cmp(outv), raw, om, g.prev(eb), MUL, ADD)
                def d_temps(c):
                    s3 = c % 3
                    if s3 == 0: Pe, Pinv, rr, kraw, ebR, ebK = TB(6), TB(7), TB(8), TB(9), TB(0), TB(1)
                    elif s3 == 1: Pe, Pinv, rr, kraw, ebR, ebK = TBa(0), TBa(1), TBa(2), TBa(3), TBa(14), V(TFt[:, 2, :].bitcast(BF16)[:, 0:W], [("TF", 2)])
                    else: Pe, Pinv, rr, kraw, ebR, ebK = XBa(0), XBa(1), XBa(2), XBa(3), XBa(4), XBa(5)
                    if c % 2 == 0: sqk, kkn, f, kp, rk, bb, sdT, rsT = TB(10), TB(11), TB(12), TB(13), TB(2), TB(3), TF(0), TF(1)
                    else: sqk, kkn, f, kp, rk, bb, sdT, rsT = TBa(4), TBa(5), TBa(6), TBa(7), TBa(8), TBa(9), TFa(10), TFa(12)
                    return Pe, Pinv, rr, kraw, sqk, kkn, f, kp, rk, bb, sdT, rsT, ebR, ebK
                def d_stage1(c):
                    Pe, Pinv, rr, kraw, sqk, kkn, f, kp, rk, bb, sdT, rsT, ebR, ebK = d_temps(c)
                    b.act(Pe[:, 0:g.CE], F1.c(c)[:, 0:g.CE], AF.Exp, scale=-LAM)
                    b.act(g.c3(Pinv), g.ccur(F1.c(c)), AF.Exp, scale=LAM)
                    endv = V(F1t[:, c, 0:g.CE].rearrange("p (n t) -> p n t", t=T + 1)[:, :, T], [("F1", c)])
                    b.act(V(PTt[:, c, 0:NCH], [("PT",)]), endv, AF.Exp, scale=-LAM)
                    proj_rkv(c, 0, rr, ebR)
                    proj_rkv(c, 1, kraw, ebK)
                    proj_rkv(c, 2, P6.c(c), ebR)
                def d_stage2a(c):
                    Pe, Pinv, rr, kraw, sqk, kkn, f, kp, rk, bb, sdT, rsT, ebR, ebK = d_temps(c)
                    b.tt("dve", g.c3(AR(c, 1)), g.c3(rr), g.ccur(Pe), MUL)
                    b.act(sqk[:, 0:NT], kraw[:, 0:NT], AF.Square, scale=PRM(l, KK + c))
                    ps = b.bank()
                    b.mm(ps[:, 0:NT], bones, sqk[:, 0:NT])
                    b.ts("dve", sdT[:, 0:NT], ps[:, 0:NT], 1e-24, None, MAX)
                    b.act(sdT[:, 0:NT], sdT[:, 0:NT], AF.Ln)
                    b.act(rsT[:, 0:NT], sdT[:, 0:NT], AF.Exp, scale=-0.5)
                    b.act(f[:, 0:NT], P1.c(c)[:, 0:NT], AF.Identity, bias=PRM(l, OMKA + c), scale=PRM(l, KA + c))
                    b.stt(kkn[:, 0:NT], kraw[:, 0:NT], PRM(l, KK + c), rsT[:, 0:NT], MUL, MUL)
                def d_stage2b(c):
                    Pe, Pinv, rr, kraw, sqk, kkn, f, kp, rk, bb, sdT, rsT, ebR, ebK = d_temps(c)
                    b.tt("pool", kp[:, 0:NT], kraw[:, 0:NT], f[:, 0:NT], MUL)
                    b.tt("pool", P4.c(c)[:, 0:NT], kp[:, 0:NT], Pinv[:, 0:NT], MUL)
                    b.tt("dve", bb[:, 0:NT], kkn[:, 0:NT], P1.c(c)[:, 0:NT], MUL)
                    b.tt("pool", P5.c(c)[:, 0:NT], bb[:, 0:NT], Pinv[:, 0:NT], MUL)
                    b.stt(g.c3(AR(c, 0)), g.c3(kkn), -1.0, g.cprev(Pe), MUL, MUL)
                    b.tt("pool", rk[:, 0:NT], rr[:, 0:NT], kp[:, 0:NT], MUL)
                def d_stage2c(c):
                    Pe, Pinv, rr, kraw, sqk, kkn, f, kp, rk, bb, sdT, rsT, ebR, ebK = d_temps(c)
                    ps2 = b.bank()
                    b.mm(ps2[:, 0:NT], V(RKt[:, l, c, :], [("RKO", l)]), rk[:, 0:NT])
                    b.tt("dve", P7.c(c)[:, 0:NT], ps2[:, 0:NT], P6.c(c)[:, 0:NT], MUL)

                d_stage1(0); d_stage1(1); d_stage2a(0)
                for c in range(8):
                    if ti == 0 and not sample: drain(1)
                    if c + 2 < 8: d_stage1(c + 2)
                    if c + 1 < 8: d_stage2a(c + 1)
                    d_stage2b(c)
                    if c >= 1: d_stage2c(c - 1)
                d_stage2c(7)
                b.s.marks.append(("Phase E t%d l%d" % (ti + 4 * sample, l), b.s.total, len(b.s.ops["pe"])))
                if ti == 0 and l == 0 and not sample: drain(0)
                fence_m4(3)
                vt, kt, bt, Wb, Ub = TB2(0), TB2(2), TB2(6), TB2(8), TB2(10)
                Ystg = V(TFt[:, 3:5, :].rearrange("p a w -> p (a w)"), [("TF", 3), ("TF", 4)])
                def chunk_gen(n):
                    cs = slice(n * T, (n + 1) * T)
                    si = (n % 2) if sample else l
                    Sv = V(St[:, si], [("S", si)])
                    def sbd_update():
                        b.cp("act", V(Sbt[0:64, si, :, 0:64], [("Sb", si)]), V(St[0:64, si], [("S", si)]))
                        b.cp("act", V(Sbt[64:128, si, :, 64:128], [("Sb", si)]), V(St[64:128, si], [("S", si)]))
                    def Sbd(c): return V(Sbt[:, si, c, :], [("Sb", si)])
                    if sample:
                        b.dma("sp", Sv, V(wkvT[l, :, n], []))
                        sbd_update()
                    def XBs(sidx, k, j=None):
                        if sidx < 2:
                            t_ = XBt[:, sidx, k] if j is None else XBt[:, sidx, k, j]
                        else:
                            base = P1flat[:, (sidx - 2) * 2048:(sidx - 1) * 2048].rearrange("p (k j t) -> p k j t", k=4, j=4)
                            t_ = base[:, k] if j is None else base[:, k, j]
                        return V(t_, [("XB", sidx, k)])
                    if n == 0:
                        b.memset("pool", V(JNKt[:, 0:1], list(P1.all().keys) + [("XB", s_, k_) for s_ in (2, 3) for k_ in range(4)]), 0.0)
                    b.cp("act", V(BMt[64:128, :, 0:T], [("BM",)]), V(P5t[64:128, :, cs], P5.all().keys))
                    b.cp("dve", V(KMt[64:128, :, 0:T], [("KM",)]), V(P4t[64:128, :, cs], P4.all().keys))
                    b.cp("act", V(AMt[64:128, :, 0:T], [("AM",)]), V(ARt[64:128, :, 0, cs], [("AR", c_) for c_ in range(8)]))
                    for hg in range(4):
                        psLr = b.bank(True)
                        psL = psLr.re("p (m t) -> p m t", m=4)
                        for j in range(4):
                            h = 4 * hg + j; c = h // 2
                            psG = b.bank().re("p (m t) -> p m t", m=4)
                            if h % 2 == 0:
                                rhs = AR(c)[0:64, :, cs]
                                b.mm(psG[0:T, 0:2, 0:T], P5.c(c)[0:64, cs], rhs)
                                b.mm(psG[0:T, 2:4, 0:T], P4.c(c)[0:64, cs], rhs)
                                b.tt("dve", V(M4t[0:T, h, :, 0:T], [("M4", h)]), psG[0:T, :, 0:T], mask4[0:T, :, 0:T], MUL)
                                b.mm(psL[0:T, j, 0:T], AR(c, 0)[0:64, cs], P5.c(c)[0:64, cs])
                            else:
                                rhs = AR(c)[:, :, cs]
                                b.mm(psG[0:T, 0:2, 0:T], V(BMt[:, c, 0:T], [("BM",)]), rhs)
                                b.mm(psG[0:T, 2:4, 0:T], V(KMt[:, c, 0:T], [("KM",)]), rhs)
                                tmpm = TB(12 + c % 2)[0:T, 0:512].re("p (m t) -> p m t", m=4)[:, :, 0:T]
                                b.cp("act", tmpm, psG[0:T, :, 0:T])
                                b.tt("pool", V(M4t[0:T, h, :, 0:T], [("M4", h)]), tmpm, mask4[0:T, :, 0:T], MUL)
                                b.mm(psL[0:T, j, 0:T], V(AMt[:, c, 0:T], [("AM",)]), P5.c(c)[:, cs])
                        b.tt("dve", XBs(hg, 0)[0:T, :, 0:T], psL[0:T, :, 0:T], maskL4[0:T, :, 0:T], MUL)
                        b.release(psLr)
                        hs = slice(4 * hg, 4 * hg + 4)
                        hk = [("M4", h) for h in range(4 * hg, 4 * hg + 4)]
                        b.tt("pool", V(TTt[0:T, hs, 0:T], [("TTm", hg)]), V(M4t[0:T, hs, 0, 0:T], hk), ident4[0:T, :, 0:T], ADD)
                    for (src, dst, eng) in ((P6, vt, "act"), (P4, kt, "dve"), (P5, bt, "act")):
                        psb = b.bank().cast(BF16)
                        for c in range(8):
                            b.tr(psb[0:T, c * 128:(c + 1) * 128], src.c(c)[:, cs], ident)
                        b.cp(eng, dst[0:T, 0:1024], psb[0:T, 0:1024])
                    yield 1
                    Xc = {hg: (lambda hg: (lambda j: XBs(hg, 0, j)[0:T, 0:T]))(hg) for hg in range(4)}
                    XTc = {hg: (lambda hg: (lambda j: V(M4t[0:T, 4 * hg + j, 0, 0:T], [("M4", 4 * hg + j)])))(hg) for hg in range(4)}
                    for k in range(1, g.M):
                        kx = k % 2
                        for hg in range(4):
                            psX = b.bank().re("p (m t) -> p m t", m=4)
                            for j in range(4):
                                b.mm(psX[0:T, j, 0:T], XTc[hg](j), Xc[hg](j))
                            b.cp("act", XBs(hg, kx)[0:T, :, 0:T], psX[0:T, :, 0:T])
                        if k < g.M - 1:
                            for hg in range(4):
                                psXT = b.bank().re("p (m t) -> p m t", m=4)
                                for j in range(4):
                                    b.mm(psXT[0:T, j, 0:T], Xc[hg](j), XTc[hg](j))
                                b.cp("dve" if hg % 2 == 0 else "act", XBs(hg, 2 + kx)[0:T, :, 0:T], psXT[0:T, :, 0:T])
                        for hg in range(4):
                            Xc[hg] = (lambda hg, kx: (lambda j: XBs(hg, kx, j)[0:T, 0:T]))(hg, kx)
                            XTc[hg] = (lambda hg, kx: (lambda j: XBs(hg, 2 + kx, j)[0:T, 0:T]))(hg, kx)
                        for hg in range(4):
                            psT = b.bank().re("p (m t) -> p m t", m=4)
                            TTv = V(TTt[0:T, 4 * hg:4 * hg + 4, 0:T], [("TTm", hg)])
                            for j in range(4):
                                b.mm(psT[0:T, j, 0:T], Xc[hg](j), V(TTt[0:T, 4 * hg + j, 0:T], [("TTm", hg)]))
                            b.tt("dve", TTv, psT[0:T, :, 0:T], TTv, ADD)
                    if n == NCH - 1:
                        b.memset("pool", V(JNKt[:, 1:2], list(P1.all().keys) + [("XB", s_, k_) for s_ in (2, 3) for k_ in range(4)]), 0.0)
                    yield 2
                    def hsl(h): return slice(h * 64, (h + 1) * 64)
                    psW = [b.bank(), b.bank()]
                    for c in range(8):
                        o = psW[c // 4][0:T, (c % 4) * 128:(c % 4 + 1) * 128]
                        b.mm(o, AR(c, 0)[:, cs], Sbd(c), True, False)
                        for h2 in range(2):
                            h = 2 * c + h2
                            oo = psW[c // 4][0:T, (c % 4) * 128 + h2 * 64:(c % 4) * 128 + h2 * 64 + 64]
                            b.mm(oo, V(M4t[0:T, h, 2, 0:T], [("M4", h)]), vt[0:T, hsl(h)], False, h2 == 1)
                    b.cp("act", Wb[0:T, 0:512], psW[0][0:T, :])
                    b.cp("dve", Wb[0:T, 512:1024], psW[1][0:T, :])
                    psU = [b.bank(), b.bank()]
                    for h in range(16):
                        b.mm(psU[h // 8][0:T, (h % 8) * 64:(h % 8 + 1) * 64], V(TTt[0:T, h, 0:T], [("TTm", h // 4)]), Wb[0:T, hsl(h)])
                    b.cp("act", Ub[0:T, 0:512], psU[0][0:T, :])
                    b.cp("dve", Ub[0:T, 512:1024], psU[1][0:T, :])
                    psY = [b.bank(True), b.bank(True)]
                    for c in range(8):
                        o = psY[c // 4][0:T, (c % 4) * 128:(c % 4 + 1) * 128]
                        b.mm(o, AR(c, 1)[:, cs], Sbd(c), True, False)
                        for h2 in range(2):
                            h = 2 * c + h2
                            oo = psY[c // 4][0:T, (c % 4) * 128 + h2 * 64:(c % 4) * 128 + h2 * 64 + 64]
                            b.mm(oo, V(M4t[0:T, h, 1, 0:T], [("M4", h)]), Ub[0:T, hsl(h)], False, False)
                            b.mm(oo, V(M4t[0:T, h, 3, 0:T], [("M4", h)]), vt[0:T, hsl(h)], False, h2 == 1)
                    psDr = [b.bank(True), b.bank(True)]
                    psD = [psDr[0].re("p (m t) -> p m t", m=4), psDr[1].re("p (m t) -> p m t", m=4)]
                    for c in range(8):
                        o = psD[c // 4][:, c % 4, :]
                        csl = slice(c * 128, (c + 1) * 128)
                        b.mm(o, bt[0:T, csl], Ub[0:T, csl], True, False)
                        b.mm(o, kt[0:T, csl], vt[0:T, csl], False, True)
                    yield 3
                    b.cp("act", Ystg[0:T, 0:512], psY[0][0:T, :])
                    b.cp("dve", Ystg[0:T, 512:1024], psY[1][0:T, :])
                    b.release(psY[0], psY[1])
                    psYT = [b.bank().re("p (m t) -> p m t", m=4), b.bank().re("p (m t) -> p m t", m=4)]
                    for c in range(8):
                        b.tr(psYT[c // 4][:, c % 4, 0:T], Ystg[0:T, c * 128:(c + 1) * 128], V(IDFt[0:T, 0:T], [("IDF",)]))
                    b.cp("act", V(F1t[:, 0:4, cs], F1.rng(0, 4).keys), psYT[0][:, :, 0:T])
                    b.cp("dve", V(F1t[:, 4:8, cs], F1.rng(4, 8).keys), psYT[1][:, :, 0:T])
                    for bk in range(2):
                        for hh in range(2):
                            pp = slice(64 * hh, 64 * hh + 64)
                            sv = V(St[pp, si, 4 * bk:4 * bk + 4, :], [("S", si)])
                            b.tt("dve", sv, psD[bk][pp, :, 64 * hh:64 * hh + 64], sv, ADD)
                    b.release(psDr[0], psDr[1])
                    ptv = V(PTt[:, :, n:n + 1].to_broadcast([128, 8, 64]), [("PT",)])
                    b.tt("pool", Sv, Sv, ptv, MUL)
                    if sample:
                        b.dma("sp", V(wkvs_o[l, :, n], []), Sv)
                    else:
                        sbd_update()
                        if last_prompt and n == NCH - 1:
                            b.dma("sp", V(wkvp_o[l], []), Sv)

                gens = [chunk_gen(n) for n in range(NCH)]
                next(gens[0])
                for n in range(NCH):
                    if ti == 0 and not sample: drain(2)
                    next(gens[n]); next(gens[n])
                    if n + 1 < NCH: next(gens[n + 1])
                    for _ in gens[n]: pass
                b.s.marks.append(("Phase F t%d l%d" % (ti + 4 * sample, l), b.s.total, len(b.s.ops["pe"])))
                if ti == 0 and not sample: drain(5 if l == 0 else 1000)
                fence_m4(2)
                L2 = ws.get()
                def f_temps(c):
                    return (TB(6), TB(7), TF(0), TF(1), TF(2), TF(3), TF(4)) if c % 2 == 0 else (TBa(0), TBa(1), TFa(2), TFa(4), TFa(6), TFa(8), TFa(10))
                fps = {}
                def f_a(c):
                    ybf, y2, f0, f1, f2, f3, f4 = f_temps(c)
                    b.cp("pool", ybf[:, 0:NT], F1.c(c)[:, 0:NT])
                    b.act(y2[:, 0:NT], F1.c(c)[:, 0:NT], AF.Square)
                    psM = b.bank(True); psE = b.bank()
                    fps[c] = psM
                    b.mm(psM[:, 0:NT], bmean, ybf[:, 0:NT])
                    b.mm(psE[:, 0:NT], bmean, y2[:, 0:NT])
                    b.act(f0[:, 0:NT], psM[:, 0:NT], AF.Square)
                    b.tt("dve", f1[:, 0:NT], psE[:, 0:NT], f0[:, 0:NT], SUB)
                    b.ts("dve", f1[:, 0:NT], f1[:, 0:NT], 0.0, None, MAX)
                    b.act(f1[:, 0:NT], f1[:, 0:NT], AF.Ln, bias=gnepsv)
                    b.act(f2[:, 0:NT], f1[:, 0:NT], AF.Exp, scale=-0.5)
                def f_b(c):
                    ybf, y2, f0, f1, f2, f3, f4 = f_temps(c)
                    psM = fps.pop(c)
                    b.tt("dve", f3[:, 0:NT], F1.c(c)[:, 0:NT], psM[:, 0:NT], SUB)
                    b.release(psM)
                    b.tt("pool", f4[:, 0:NT], f3[:, 0:NT], f2[:, 0:NT], MUL)
                    b.stt(f3[:, 0:NT], f4[:, 0:NT], PRM(l, LNG + c), P7.c(c)[:, 0:NT], MUL, ADD)
                    psg = b.bank()
                    b.mm(psg[:, 0:NT], L2[:, 2048 + c * 128:2048 + (c + 1) * 128], TB(4)[:, 0:NT], True, False)
                    b.mm(psg[:, 0:NT], L2[0:32, 3072 + c * 128:3072 + (c + 1) * 128], TB(5)[0:32, 0:NT], False, True)
                    b.stt(P1.c(c)[:, 0:NT], f3[:, 0:NT], PRM(l, LNB + c), psg[:, 0:NT], ADD, MUL)
                def proj512(wv, c4, ps):
                    for kc in range(8):
                        b.mm(g.cmp(ps), wv[:, kc, c4 * 128:(c4 + 1) * 128], g.cur(XN.c(kc)), kc == 0, kc == 7)
                gM = b.bank(True); gE = b.bank(True)
                gvw = [None]
                def gv_step(c):
                    if c % 4 == 0: gvw[0] = ws.get(1 if c == 0 else 0)[:, 0:4096].re("p (k n) -> p k n", k=8)
                    ps = b.bank(); proj512(gvw[0], c % 4, ps)
                    b.act(AR(c, 1)[:, 0:NT], ps[:, 0:NT], AF.Gelu)
                    gg2 = TB(c % 2)
                    b.tt("pool", gg2[:, 0:NT], AR(c, 1)[:, 0:NT], AR(c, 1)[:, 0:NT], MUL)
                    b.mm(gM[:, 0:NT], ones, AR(c, 1)[:, 0:NT], c == 0, c == 7)
                    b.mm(gE[:, 0:NT], ones, gg2[:, 0:NT], c == 0, c == 7)
                f_a(0)
                for c in range(8):
                    if c % 4 == 0:
                        for c4 in range(c, c + 4): gv_step(c4)
                    if c + 1 < 8: f_a(c + 1)
                    f_b(c)
                if not (sample and l == 1): load_l1(1 - l)
                b.s.marks.append(("Phase G t%d l%d" % (ti + 4 * sample, l), b.s.total, len(b.s.ops["pe"])))
                if ti == 0 and l == 0 and not sample: drain(4)
                b.act(TF(0)[:, 0:NT], gM[:, 0:NT], AF.Square, scale=1.0 / D)
                b.stt(TF(1)[:, 0:NT], gE[:, 0:NT], 1.0 / D, TF(0)[:, 0:NT], MUL, SUB)
                b.ts("dve", TF(1)[:, 0:NT], TF(1)[:, 0:NT], 0.0, None, MAX)
                b.act(TF(1)[:, 0:NT], TF(1)[:, 0:NT], AF.Ln, bias=epsv)
                b.act(TF(0)[:, 0:NT], TF(1)[:, 0:NT], AF.Exp, scale=-0.5)
                b.stt(TF(2)[:, 0:NT], gM[:, 0:NT], 1.0 / D, TF(0)[:, 0:NT], MUL, MUL)
                b.release(gM, gE)
                for c in range(8):
                    if c % 4 == 0: wv = ws.get()[:, 0:4096].re("p (k n) -> p k n", k=8)
                    ps = b.bank(); proj512(wv, c % 4, ps)
                    b.act(AR(c, 0)[:, 0:NT], ps[:, 0:NT], AF.Gelu)
                for (arr_, nm) in ((P5, "ga"), (P6, "gb")):
                    for c in range(8):
                        if c % 4 == 0: wv = ws.get()[:, 0:4096].re("p (k n) -> p k n", k=8)
                        ps = b.bank(); proj512(wv, c % 4, ps)
                        b.act(arr_.c(c)[:, 0:NT], ps[:, 0:NT], AF.Sigmoid)
                for c in range(8):
                    t = TF(3 + c % 2)
                    b.tt("dve", t[:, 0:NT], AR(c, 1)[:, 0:NT], TF(0)[:, 0:NT], MUL)
                    b.tt("pool", t[:, 0:NT], t[:, 0:NT], TF(2)[:, 0:NT], SUB)
                    if sample:
                        b.act(t[:, 0:NT], t[:, 0:NT], AF.Identity, bias=PRM(l, VNB + c), scale=PRM(l, VNG + c))
                        b.cp("dve", P4.c(c)[:, 0:NT], t[:, 0:NT])
                        b.dma("sp", V(cvs_o[l, :, c, :], []), t[:, 0:NT])
                    else:
                        b.act(P4.c(c)[:, 0:NT], t[:, 0:NT], AF.Identity, bias=PRM(l, VNB + c), scale=PRM(l, VNG + c))
                for c in range(8):
                    vnt = TB2(0) if c % 2 == 0 else TB2(2)
                    psS = b.bank()
                    for n0 in range(0, NCH, 8):
                        nn = min(8, NCH - n0)
                        psb = b.bank().cast(BF16)
                        for n in range(n0, n0 + nn):
                            b.tr(psb[0:T, (n - n0) * 128:(n - n0 + 1) * 128], P4.c(c)[:, n * T:(n + 1) * T], ident)
                        b.cp("act", vnt[0:T, 0:nn * 128], psb[0:T, 0:nn * 128])
                        for n in range(n0, n0 + nn):
                            o = psS[:, n * T:(n + 1) * T]
                            b.mm(o, vnt[0:T, (n - n0) * 128:(n - n0 + 1) * 128], V(WSMt[0:T, l, c, 0:T], [("WSM", l)]), True, False)
                            b.mm(o, ones[0:1, :], V(BSt[0:1, l, c, 0:T], [("BSR", l)]), False, True)
                    b.tt("dve", P7.c(c)[:, 0:NT], psS[:, 0:NT], AR(c, 0)[:, 0:NT], MUL)
                    b.tt("pool", P7.c(c)[:, 0:NT], P7.c(c)[:, 0:NT], P6.c(c)[:, 0:NT], MUL)
                    b.tt("dve", P1.c(c)[:, 0:NT], P1.c(c)[:, 0:NT], P5.c(c)[:, 0:NT], MUL)
                    b.tt("pool", P1.c(c)[:, 0:NT], P1.c(c)[:, 0:NT], P7.c(c)[:, 0:NT], ADD)

                def dense_out(nblk_w, src_fn, nk, kdiv):
                    for c2 in range(8):
                        if kdiv == 8:
                            if c2 % 4 == 0: wvv = ws.get()[:, 0:4096].re("p (k n) -> p k n", k=8)
                            lw = lambda kc: wvv[:, kc, (c2 % 4) * 128:(c2 % 4 + 1) * 128]
                        else:
                            wvv = ws.get()[:, 0:4096].re("p (k n) -> p k n", k=32)
                            lw = lambda kc: wvv[:, kc, :]
                        ps = b.bank()
                        for kc in range(nk):
                            b.mm(ps[:, 0:NT], lw(kc), src_fn(kc), kc == 0, kc == nk - 1)
                        b.cp("act", F1.c(c2)[:, 0:NT], ps[:, 0:NT])

                def add_normed(gcol):
                    rs_ = rms_stats(g, F1)
                    for c in range(8):
                        t = TF(3 + c % 2)
                        b.stt(t[:, 0:NT], F1.c(c)[:, 0:NT], PRM(l, gcol + c), rs_[:, 0:NT], MUL, MUL)
                        b.tt("pool", H.c(c)[:, 0:NT], H.c(c)[:, 0:NT], t[:, 0:NT], ADD)

                dense_out(2, lambda kc: P1.c(kc)[:, 0:NT], 8, 8)
                add_normed(G_POST)

                b.s.marks.append(("Phase H t%d l%d" % (ti + 4 * sample, l), b.s.total, len(b.s.ops["pe"])))
                if ti == 0 and l == 0 and not sample: drain(4)
                rstd = rms_stats(g, H)
                for c in range(8):
                    b.stt(XN.c(c)[:, 0:NT], H.c(c)[:, 0:NT], PRM(l, F_PRE + c), rstd[:, 0:NT], MUL, MUL)
                ZA = (P4, P5, P6, P7)
                for kb in range(32):
                    if kb % 4 == 0: wv = ws.get()[:, 0:4096].re("p (k n) -> p k n", k=8)
                    ps = b.bank()
                    for kc in range(8):
                        b.mm(ps[:, 0:NT], wv[:, kc, (kb % 4) * 128:(kb % 4 + 1) * 128], XN.c(kc)[:, 0:NT], kc == 0, kc == 7)
                    rl = TB(kb % 4)
                    b.act(rl[:, 0:NT], ps[:, 0:NT], AF.Relu)
                    b.tt("pool", ZA[kb // 8].c(kb % 8)[:, 0:NT], rl[:, 0:NT], rl[:, 0:NT], MUL)
                dense_out(8, lambda kb: ZA[kb // 8].c(kb % 8)[:, 0:NT], 32, 32)
                add_normed(F_POST)

                b.s.marks.append(("Phase I t%d l%d" % (ti + 4 * sample, l), b.s.total, len(b.s.ops["pe"])))
                if ti == 0 and l == 0 and not sample: drain(0)
                for c in range(8):
                    b.cp("pool", XN.c(c)[:, 0:NT], H.c(c)[:, 0:NT])
                for kc in range(2):
                    b.dma("sp", TF(3 + kc)[:, 0:NT], V(pT[l, :, kc, col0:col0 + NT], []))
                    b.cp("pool", TB(8 + kc)[:, 0:NT], TF(3 + kc)[:, 0:NT])
                for c2 in range(8):
                    if c2 % 4 == 0:
                        wpg = ws.get(1)[:, 0:4096].re("p (k n) -> p k n", k=8)
                        wpe = ws.get(1)[:, 0:1024].re("p (k n) -> p k n", k=2)
                    ps1 = b.bank(); ps2 = b.bank()
                    for kc in range(8):
                        b.mm(ps1[:, 0:NT], wpg[:, kc, (c2 % 4) * 128:(c2 % 4 + 1) * 128], XN.c(kc)[:, 0:NT], kc == 0, kc == 7)
                    for kc in range(2):
                        b.mm(ps2[:, 0:NT], wpe[:, kc, (c2 % 4) * 128:(c2 % 4 + 1) * 128], TB(8 + kc)[:, 0:NT], kc == 0, kc == 1)
                    sg = TF(c2 % 2)
                    b.act(sg[:, 0:NT], ps1[:, 0:NT], AF.Sigmoid)
                    b.tt("dve", sg[:, 0:NT], ps2[:, 0:NT], sg[:, 0:NT], MUL)
                    b.tt("pool", H.c(c2)[:, 0:NT], H.c(c2)[:, 0:NT], sg[:, 0:NT], ADD)
                if not (ti == 0 and l == 0 and not sample): drain(1000)
            b.dma("sp", V(yT[:, :, col0:col0 + NT], []), V(Ht[:, :, 0:NT], H.all().keys))

        b.s.marks.append(("end", b.s.total, len(b.s.ops["pe"])))
        import os
        if os.environ.get("KMARKS"):
            for m in b.s.marks: print("MARK", m)
            lo, hi = [int(x) for x in os.environ.get("KDUMP", "0,0").split(",")]
            for e_ in b.s.log:
                if lo <= e_[0] <= hi: print("OP", e_)
            print("OPS", {e: len(b.s.ops[e]) for e in ENGS}, "waits", {e: sum(len(o.waits) for o in b.s.ops[e]) for e in ENGS})
        sch = b.s
        esem = {e: es.enter_context(nc.semaphore(f"sem_{e}")) for e in ENGS}
        dsem = [es.enter_context(nc.semaphore(f"dsem{i}")) for i in range(sch.ndma)]
        for e in ENGS:
            cnt = 0
            for op in sch.ops[e]:
                if op.sig and not op.dma:
                    cnt += 1
                op.cnt = cnt
        final = {}
        for e in ENGS:
            for op in sch.ops[e]:
                if op.dma: final[op.dsem] = max(final.get(op.dsem, 0), op.dval)

        def run(ename, e):
            for op in sch.ops[ename]:
                for d in op.waits:
                    if d.dma: e.wait_ge(dsem[d.dsem], d.dval)
                    else: e.wait_ge(esem[d.eng], d.cnt)
                ins = op.fn(e)
                if op.dma: ins.then_inc(dsem[op.dsem], 16)
                elif op.sig: ins.then_inc(esem[ename], 1)
            if ename == "sp":
                for i, v in final.items():
                    e.wait_ge(dsem[i], v)

        with nc.Block() as block:
            @block.tensor
            def _(e): run("pe", e)
            @block.scalar
            def _(e): run("act", e)
            @block.vector
            def _(e): run("dve", e)
            @block.gpsimd
            def _(e): run("pool", e)
            @block.sync
            def _(e): run("sp", e)
    return nc


def _fm(a):
    n, d = a.shape
    return np.ascontiguousarray(a.T.reshape(d // 128, 128, n).transpose(1, 0, 2))


def _consts():
    c = np.zeros((128, CW), np.float32)
    i = np.arange(128)
    c[:, C_ID:C_ID + 128] = np.eye(128)
    c[:, C_ONES:C_ONES + 128] = 1.0
    blk = (i[:, None] // 64 == i[None, :] // 64).astype(np.float32)
    c[:, C_BONES:C_BONES + 128] = blk
    c[:, C_BMEAN:C_BMEAN + 128] = blk / 64.0
    su = (i[None, :] > i[:, None]).astype(np.float32)
    iu = (i[None, :] >= i[:, None]).astype(np.float32)
    c[:, C_M4:C_M4 + 512] = np.concatenate([su, iu, su, iu], 1)
    sl = (i[None, :] < i[:, None]).astype(np.float32)
    c[:, C_ML4:C_ML4 + 512] = np.concatenate([sl] * 4, 1)
    c[:, C_ID4:C_ID4 + 512] = np.concatenate([np.eye(128, dtype=np.float32)] * 4, 1)
    rp = np.ones(512, np.float32); rp[::128] = 0.0
    rs = np.ones(64, np.float32); rs[::4] = 0.0
    c[:, C_RSP:C_RSP + 512] = rp[None]; c[:, C_RSS:C_RSS + 64] = rs[None]
    c[:, C_SH:C_SH + 64] = (i[:, None] == (np.arange(64)[None, :] + 64)).astype(np.float32)
    return c


_NC_CACHE = {}


def kernel(x_prompt, x_sample, state_wkv, state_shift, p_prompt, p_sample,
           mix_pre_g, mix_post_g, ffn_pre_g, ffn_post_g, w_in, mu_rkv, mu_wag,
           w0, w1, w2, a0, a1, a2, g1, g2, k_k, k_a, r_k, lnx_g, lnx_b,
           vn_g, vn_b, w_s, b_s, w_out, w_up, w_down, w_pe, w_pg):
    f = lambda a: np.ascontiguousarray(np.asarray(a, dtype=np.float32))
    x_prompt, x_sample, state_wkv, state_shift, p_prompt, p_sample = map(f, (x_prompt, x_sample, state_wkv, state_shift, p_prompt, p_sample))
    prm = np.zeros((128, 2, NPC), np.float32)
    def col(v): return np.asarray(v, np.float32).reshape(-1, 128).T
    for l in range(2):
        mr = np.asarray(mu_rkv[l], np.float32)
        items = [(G_PRE, mix_pre_g[l]), (G_POST, mix_post_g[l]), (F_PRE, ffn_pre_g[l]), (F_POST, ffn_post_g[l]),
                 (MU_R, mr[0:1024]), (MU_K, mr[1024:2048]), (MU_V, mr[2048:3072]),
                 (MU_W, mu_wag[l][0]), (MU_A, mu_wag[l][1]), (MU_G, mu_wag[l][2]),
                 (W0, w0[l]), (A0, a0[l]), (KK, k_k[l]), (KA, k_a[l]), (RK, np.asarray(r_k[l]).reshape(-1)),
                 (LNG, lnx_g[l]), (LNB, lnx_b[l]), (VNG, vn_g[l]), (VNB, vn_b[l])]
        for o, v in items:
            prm[:, l, o:o + 8] = col(v)
    cst = _consts()
    wsT = np.ascontiguousarray(np.asarray(w_s, np.float32).transpose(0, 3, 1, 2))
    bs = np.ascontiguousarray(np.asarray(b_s, np.float32).reshape(2, 1, 8, 128))
    shared = dict(prm=prm, cst=cst, wsT=wsT, bs=bs, w_in=f(w_in), w_out=f(w_out), w_up=f(w_up), w_down=f(w_down),
                  w_pe=f(w_pe), w_pg=f(w_pg), w1=f(w1), w2=f(w2), a1=f(a1), a2=f(a2), g1=f(g1), g2=f(g2))
    in_maps = []
    for i in range(NCORE):
        xs = x_sample[NSQ * i:NSQ * (i + 1)].reshape(NSQ * DEC, D)
        xT = np.concatenate([_fm(x_prompt[i]), _fm(xs)], axis=2)
        pTl = []
        for l in range(2):
            ps_ = p_sample[l, NSQ * i:NSQ * (i + 1)].reshape(NSQ * DEC, 256)
            pTl.append(np.concatenate([_fm(p_prompt[l, i]), _fm(ps_)], axis=2))
        pT = np.stack(pTl)
        shT = np.stack([_fm(state_shift[l, NSQ * i:NSQ * (i + 1)]) for l in range(2)])
        sw = state_wkv[:, NSQ * i:NSQ * (i + 1)].reshape(2, NSQ, 8, 2, 64, 64)
        wkvT = np.ascontiguousarray(sw.transpose(0, 3, 5, 1, 2, 4)).reshape(2, 128, NSQ, 8, 64)
        m = dict(shared); m.update(xT=np.ascontiguousarray(xT), pT=np.ascontiguousarray(pT), shT=np.ascontiguousarray(shT), wkvT=wkvT)
        in_maps.append(m)
    if "nc" not in _NC_CACHE:
        _NC_CACHE["nc"] = build_nc()
    res = run_bass_kernel_spmd(_NC_CACHE["nc"], in_maps, core_ids=list(range(NCORE)))
    R = list(res.results)
    def unfm(a):
        return np.ascontiguousarray(a.transpose(2, 1, 0).reshape(a.shape[2], -1))
    y_prompt = np.stack([unfm(R[i]["yT"][:, :, 0:SEQ]) for i in range(NCORE)])
    y_sample = np.concatenate([unfm(R[i]["yT"][:, :, SEQ:]).reshape(NSQ, DEC, D) for i in range(NCORE)])
    wkv_p = np.stack([np.stack([R[i]["wkvp"][l].reshape(2, 64, 8, 64).transpose(2, 0, 3, 1).reshape(16, 64, 64) for i in range(NCORE)]) for l in range(2)])
    sh_p = np.stack([np.stack([R[i]["shp"][l].T.reshape(D) for i in range(NCORE)]) for l in range(2)])
    wkv_s = np.stack([np.concatenate([R[i]["wkvs"][l].reshape(2, 64, NSQ, 8, 64).transpose(2, 3, 0, 4, 1).reshape(NSQ, 16, 64, 64) for i in range(NCORE)]) for l in range(2)])
    sh_s = np.stack([np.concatenate([R[i]["shs"][l].transpose(2, 1, 0).reshape(NSQ, D) for i in range(NCORE)]) for l in range(2)])
    cv_s = np.stack([np.concatenate([R[i]["cvs"][l].reshape(128, 8, NSQ, DEC).transpose(2, 3, 1, 0).reshape(NSQ, DEC, D) for i in range(NCORE)]) for l in range(2)])
    o = lambda a: np.ascontiguousarray(a, dtype=np.float32)
    return (o(y_prompt), o(y_sample), o(wkv_p), o(sh_p), o(wkv_s), o(sh_s), o(cv_s))
```
